# Optimizing a Trainium2 kernel written in Bass

```python
import math
import jax, jax.numpy as jnp
from jax import lax
import numpy as np


D_MODEL = 1024
BATCH = 16
SEQ = 4096
DEPTH = 4

CHUNK = 64
Q_BLOCK = 128
ATT_HEADS = 4
ATT_HEAD_DIM = 64
ATT_V_DIM = 2 * ATT_HEAD_DIM
ATT_WIDTH = ATT_HEADS * ATT_V_DIM
QK_COLS = ATT_HEADS * 2 * ATT_HEAD_DIM
ROPE_THETA = 10000.0
CONV_CH = 512
CONV_WIDTH = 31
GMLP_CH = 512
GMLP_GROUPS = 4
GMLP_GROUP_CH = GMLP_CH // GMLP_GROUPS
GMLP_BLOCK = 128
N_BRANCH = 3
FFN_DIM = 2816
FFN_CONV_WIDTH = 3
EPS = 1e-6
SPLIT_POINTS = (QK_COLS, 2 * QK_COLS, 2 * QK_COLS + ATT_WIDTH,
                2 * QK_COLS + ATT_WIDTH + 2 * CONV_CH,
                2 * QK_COLS + ATT_WIDTH + 2 * CONV_CH + 2 * GMLP_CH)
IN_COLS = 2 * QK_COLS + ATT_WIDTH + 2 * CONV_CH + 2 * GMLP_CH + N_BRANCH * D_MODEL

kernel_name = 'hybrid_diffattn_conformer_gmlp_convffn_adaln'


def rms_norm(x, g):
    x32 = x.astype(jnp.float32)
    y = x32 * lax.rsqrt(jnp.mean(x32 * x32, axis=-1, keepdims=True) + EPS)
    return (y * g.astype(jnp.float32)).astype(x.dtype)


def layer_norm(x, g, b):
    x32 = x.astype(jnp.float32)
    mu = jnp.mean(x32, axis=-1, keepdims=True)
    var = jnp.mean(jnp.square(x32 - mu), axis=-1, keepdims=True)
    y = (x32 - mu) * lax.rsqrt(var + EPS) * g.astype(jnp.float32) + b.astype(jnp.float32)
    return y.astype(x.dtype)


def causal_depthwise_conv(x, w, b):
    k = w.shape[0]
    xp = jnp.pad(x, ((0, 0), (k - 1, 0), (0, 0)))
    y = lax.conv_general_dilated(xp, w[:, None, :].astype(x.dtype), window_strides=(1,), padding='VALID',
                                 dimension_numbers=('NWC', 'WIO', 'NWC'),
                                 feature_group_count=x.shape[-1])
    return y + b


def apply_rope(x, cos, sin):
    half = x.shape[-1] // 2
    x1, x2 = x[..., :half], x[..., half:]
    return jnp.concatenate([x1 * cos - x2 * sin, x2 * cos + x1 * sin], axis=-1)


def diff_attention(q, k, v, lam):
    B, S = q.shape[0], q.shape[1]
    nb = S // Q_BLOCK
    qb = q.reshape(B, nb, Q_BLOCK, ATT_HEADS, 2, ATT_HEAD_DIM).transpose(1, 0, 2, 3, 4, 5)
    k_chunk = jnp.arange(S) // CHUNK

    def one_block(args):
        qi, bi = args
        q_chunk = (bi * Q_BLOCK + jnp.arange(Q_BLOCK)) // CHUNK
        mask = q_chunk[:, None] >= k_chunk[None, :]
        s = jnp.einsum('bqhmd,bkhmd->bhmqk', qi, k).astype(jnp.float32)
        s = jnp.where(mask, s, -1e30)
        p = jax.nn.softmax(s, axis=-1)
        a = (p[:, :, 0] - lam * p[:, :, 1]).astype(v.dtype)
        return jnp.einsum('bhqk,bkhe->bqhe', a, v)

    out = lax.map(one_block, (qb, jnp.arange(nb)))
    return out.transpose(1, 0, 2, 3, 4).reshape(B, S, ATT_HEADS, ATT_V_DIM)


def setup_inputs(seed: int = 0) -> dict:
    key = jax.random.key(seed)
    ks = jax.random.split(key, 32)
    L, D, F = DEPTH, D_MODEL, FFN_DIM

    def nrm(k, shape, scale):
        return jax.random.normal(k, shape, dtype=jnp.float32) * scale

    offset = jax.random.randint(ks[2], (BATCH, 1), 0, SEQ, dtype=jnp.int32)
    positions = offset + jnp.arange(SEQ, dtype=jnp.int32)[None, :]
    return {
        'x': nrm(ks[0], (BATCH, SEQ, D), 1.0),
        'c': nrm(ks[1], (BATCH, D), 1.0),
        'positions': positions,
        'ln1_g': 1.0 + nrm(ks[3], (L, D), 0.02),
        'ln2_g': 1.0 + nrm(ks[4], (L, D), 0.02),
        'w_ada': nrm(ks[5], (L, D, 6 * D), 0.5 * D ** -0.5),
        'b_ada': nrm(ks[6], (L, 6 * D), 0.02),
        'w_in': nrm(ks[7], (L, D, IN_COLS), D ** -0.5),
        'b_in': nrm(ks[8], (L, IN_COLS), 0.02),
        'lambda_q1': nrm(ks[9], (L, ATT_HEAD_DIM), 0.1),
        'lambda_k1': nrm(ks[10], (L, ATT_HEAD_DIM), 0.1),
        'lambda_q2': nrm(ks[11], (L, ATT_HEAD_DIM), 0.1),
        'lambda_k2': nrm(ks[12], (L, ATT_HEAD_DIM), 0.1),
        'attn_subln_g': 1.0 + nrm(ks[13], (L, ATT_V_DIM), 0.02),
        'w_attn_out': nrm(ks[14], (L, ATT_WIDTH, D), ATT_WIDTH ** -0.5),
        'conv_dw_w': nrm(ks[15], (L, CONV_WIDTH, CONV_CH), CONV_WIDTH ** -0.5),
        'conv_dw_b': nrm(ks[16], (L, CONV_CH), 0.02),
        'conv_ln_g': 1.0 + nrm(ks[17], (L, CONV_CH), 0.02),
        'conv_ln_b': nrm(ks[18], (L, CONV_CH), 0.02),
        'w_conv_out': nrm(ks[19], (L, CONV_CH, D), CONV_CH ** -0.5),
        'gmlp_ln_g': 1.0 + nrm(ks[20], (L, GMLP_CH), 0.02),
        'gmlp_ln_b': nrm(ks[21], (L, GMLP_CH), 0.02),
        'w_spatial': nrm(ks[22], (L, GMLP_GROUPS, GMLP_BLOCK, GMLP_BLOCK), GMLP_BLOCK ** -0.5),
        'b_spatial': 1.0 + nrm(ks[23], (L, GMLP_GROUPS, GMLP_BLOCK), 0.02),
        'w_gmlp_out': nrm(ks[24], (L, GMLP_CH, D), GMLP_CH ** -0.5),
        'w_o': nrm(ks[25], (L, D, D), D ** -0.5),
        'w_up': nrm(ks[26], (L, D, 2 * F), D ** -0.5),
        'ffn_dw_w': nrm(ks[27], (L, FFN_CONV_WIDTH, 2 * F), FFN_CONV_WIDTH ** -0.5),
        'ffn_dw_b': nrm(ks[28], (L, 2 * F), 0.02),
        'w_down': nrm(ks[29], (L, F, D), F ** -0.5),
        'final_g': 1.0 + nrm(ks[30], (D,), 0.02),
    }


def reference(x, c, positions, ln1_g, ln2_g, w_ada, b_ada, w_in, b_in,
              lambda_q1, lambda_k1, lambda_q2, lambda_k2, attn_subln_g, w_attn_out,
              conv_dw_w, conv_dw_b, conv_ln_g, conv_ln_b, w_conv_out,
              gmlp_ln_g, gmlp_ln_b, w_spatial, b_spatial, w_gmlp_out, w_o,
              w_up, ffn_dw_w, ffn_dw_b, w_down, final_g):
    B, S, D = x.shape
    inv_freq = 1.0 / (ROPE_THETA ** (jnp.arange(0, ATT_HEAD_DIM, 2, dtype=jnp.float32) / ATT_HEAD_DIM))
    ang = positions.astype(jnp.float32)[..., None] * inv_freq
    cos = jnp.cos(ang)[:, :, None, None, :].astype(x.dtype)
    sin = jnp.sin(ang)[:, :, None, None, :].astype(x.dtype)
    tri = jnp.tril(jnp.ones((GMLP_BLOCK, GMLP_BLOCK), dtype=bool))
    c_act = jax.nn.silu(c)

    for l in range(DEPTH):
        mod = c_act @ w_ada[l] + b_ada[l]
        sh1, sc1, g1, sh2, sc2, g2 = [m[:, None, :] for m in jnp.split(mod, 6, axis=-1)]

        h = rms_norm(x, ln1_g[l]) * (1.0 + sc1) + sh1
        cols = h @ w_in[l] + b_in[l]
        q, k, v, conv_in, gmlp_in, gate_in = jnp.split(cols, SPLIT_POINTS, axis=-1)

        q = apply_rope(q.reshape(B, S, ATT_HEADS, 2, ATT_HEAD_DIM), cos, sin) * (ATT_HEAD_DIM ** -0.5)
        k = apply_rope(k.reshape(B, S, ATT_HEADS, 2, ATT_HEAD_DIM), cos, sin)
        v = v.reshape(B, S, ATT_HEADS, ATT_V_DIM)
        lam_init = 0.8 - 0.6 * math.exp(-0.3 * l)
        lam = (jnp.exp(jnp.sum(lambda_q1[l].astype(jnp.float32) * lambda_k1[l].astype(jnp.float32)))
               - jnp.exp(jnp.sum(lambda_q2[l].astype(jnp.float32) * lambda_k2[l].astype(jnp.float32)))
               + lam_init)
        o = diff_attention(q, k, v, lam)
        o = rms_norm(o, attn_subln_g[l]) * (1.0 - lam_init)
        y_att = o.reshape(B, S, ATT_WIDTH) @ w_attn_out[l]

        a, gt = jnp.split(conv_in, 2, axis=-1)
        hc = a * jax.nn.sigmoid(gt)
        hc = causal_depthwise_conv(hc, conv_dw_w[l], conv_dw_b[l])
        hc = jax.nn.silu(layer_norm(hc, conv_ln_g[l], conv_ln_b[l]))
        y_conv = hc @ w_conv_out[l]

        z = jax.nn.gelu(gmlp_in)
        u, vv = jnp.split(z, 2, axis=-1)
        vv = layer_norm(vv, gmlp_ln_g[l], gmlp_ln_b[l])
        vv = vv.reshape(B, S // GMLP_BLOCK, GMLP_BLOCK, GMLP_GROUPS, GMLP_GROUP_CH)
        ws = jnp.where(tri, w_spatial[l], 0.0)
        sg = jnp.einsum('gqp,bnpgc->bnqgc', ws, vv) + b_spatial[l].T[:, :, None]
        y_gmlp = (u * sg.reshape(B, S, GMLP_CH)) @ w_gmlp_out[l]

        ga, gc, gm = jnp.split(jax.nn.sigmoid(gate_in), N_BRANCH, axis=-1)
        mixed = ga * y_att + gc * y_conv + gm * y_gmlp
        x = x + g1 * (mixed @ w_o[l])

        h2 = rms_norm(x, ln2_g[l]) * (1.0 + sc2) + sh2
        up = causal_depthwise_conv(h2 @ w_up[l], ffn_dw_w[l], ffn_dw_b[l])
        ff_gate, ff_val = jnp.split(up, 2, axis=-1)
        x = x + g2 * ((jax.nn.silu(ff_gate) * ff_val) @ w_down[l])

    return rms_norm(x, final_g)
```

```python
import math
import contextlib
import numpy as np
import concourse.bass as bass
import concourse.mybir as mybir
from concourse.bass_utils import run_bass_kernel_spmd

F32 = mybir.dt.float32
BF16 = mybir.dt.bfloat16
I32 = mybir.dt.int32
AF = mybir.ActivationFunctionType
ALU = mybir.AluOpType

D = 1024
S = 4096
L = 4
T = 512
FF = 2816
NEXT = 7680
EPS = 1e-6
TWO_PI = 2.0 * math.pi

P_BIN = 0
P_LN1G = 60
P_LN2G = 68
P_CW = 76
P_CB = 200
P_CLG = 204
P_CLB = 208
P_SUBG = 212
P_FW = 213
P_FB = 345
P_BADA = 389
NV = 437
B_BV = 0
B_BVV = 512
B_GG = 1024
B_GB = 1536
B_BSP = 2048
B_BG1 = 2560
B_BG2 = 3584
B_LAM = 4608
NB = 4864


class _ES:
    def __init__(self, name, eng, sem, inc):
        self.name, self.eng, self.sem, self.inc = name, eng, sem, inc
        self.count = 0
        self.seen = {}


class _Reg:
    __slots__ = ("w", "r")

    def __init__(self):
        self.w = None
        self.r = {}


def build_nc(NL=L, NSEQ=2, NT=8, DEBUG=False, LW=L):
    nc = bass.Bass("TRN2", target_bir_lowering=False)

    def din(name, shape, dt=F32):
        return nc.dram_tensor(name, list(shape), dt, kind="ExternalInput").ap()

    def dscr(name, shape, dt=BF16):
        return nc.dram_tensor(name, list(shape), dt, kind="Internal").ap()

    x_in = din("x", [2 * S, D])
    cT_in = din("cT", [128, 8, 2])
    pos_in = din("pos", [2, S], I32)
    cst_in = din("cst", [128, 4])
    pvec_in = din("pvec", [LW, 128, NV])
    bvec_in = din("bvec", [LW, NB])
    fing_in = din("fing", [1, D])
    wsp_in = din("wspT", [LW, 128, 4, 128])
    wnames = [("w_in", D, NEXT), ("w_ada", D, 6144), ("w_att", 512, D), ("w_conv", 512, D),
              ("w_gmlp", 512, D), ("w_o", D, D), ("w_up", D, 2 * FF), ("w_down", FF, D)]
    wf = {}
    wb = {}
    for nm, r, c in wnames:
        wf[nm] = din(nm, [LW, r, c])
        wb[nm] = dscr(nm + "_b", [LW, r, c])
    out = nc.dram_tensor("out", [2 * S, D], F32, kind="ExternalOutput").ap()
    dbg_outs = {}

    es = contextlib.ExitStack()
    with es:
        def sem(n):
            return es.enter_context(nc.semaphore(n))

        PE = _ES("pe", nc.tensor, sem("s_pe"), 1)
        ACT = _ES("act", nc.scalar, sem("s_act"), 1)
        DVE = _ES("dve", nc.vector, sem("s_dve"), 1)
        POOL = _ES("pool", nc.gpsimd, sem("s_pool"), 1)
        SP = _ES("sp", nc.sync, sem("s_sp"), 1)
        COMPUTE = [PE, ACT, DVE, POOL]

        def chan(n):
            return _ES(n, None, sem("c_" + n), 16)

        def _deps(reads, writes):
            d = {}
            for r in reads:
                if r.w is not None:
                    st, c = r.w
                    if d.get(st, 0) < c:
                        d[st] = c
            for w in writes:
                if w.w is not None:
                    st, c = w.w
                    if d.get(st, 0) < c:
                        d[st] = c
                for st, c in w.r.items():
                    if d.get(st, 0) < c:
                        d[st] = c
            return d

        def _wait(E, d):
            for st, c in d.items():
                if st is E and (E is PE or c < E.count):
                    continue
                if E.seen.get(st, 0) < c:
                    E.eng.wait_ge(st.sem, c)
                    E.seen[st] = c

        def op(E, fn, reads=(), writes=()):
            _wait(E, _deps(reads, writes))
            ins = fn()
            E.count += 1
            ins.then_inc(E.sem, 1)
            for w in writes:
                w.w = (E, E.count)
                w.r = {}
            for r in reads:
                r.r[E] = E.count

        bar_counts = {}

        def dma(Q, ch, fn, reads=(), writes=(), local=False):
            if local:
                _wait(Q, dict(bar_counts))
            _wait(Q, _deps(reads, writes))
            ins = fn()
            ch.count += 16
            ins.then_inc(ch.sem, 16)
            for w in writes:
                w.w = (ch, ch.count)
                w.r = {}
            for r in reads:
                r.r[ch] = ch.count
            if ch.name == "misc":
                Q.eng.wait_ge(ch.sem, ch.count)

        def barrier():
            for E in COMPUTE:
                for Fx in COMPUTE:
                    if Fx is E:
                        continue
                    if E.seen.get(Fx, 0) < Fx.count:
                        E.eng.wait_ge(Fx.sem, Fx.count)
                        E.seen[Fx] = Fx.count
            for E in COMPUTE:
                bar_counts[E] = E.count

        uniq = [0]

        def sb(name, shape, dt=F32, stack=es):
            uniq[0] += 1
            return stack.enter_context(nc.sbuf_tensor(f"{name}_{uniq[0]}", list(shape), dt))

        cv = [sem(f"cv{l}") for l in range(L)]
        cvtot = [0] * L
        for l in range(NL):
            for nm, r, c in wnames:
                for r0 in range(0, r, 128):
                    nc.gpsimd.dma_start(out=wb[nm][l, r0:r0 + 128, :], in_=wf[nm][l, r0:r0 + 128, :]).then_inc(cv[l], 16)
                    cvtot[l] += 16
        cv_waited = [False] * L

        xt = sb("xt", [128, 4, D])
        KT = sb("KT", [128, 4, S], BF16)
        VC = sb("VC", [128, 32, 512], BF16)
        NSLOT = 3
        wpan = [sb(f"wpan{i}", [128, 4096], BF16) for i in range(NSLOT)]
        cin = sb("cin", [128, 4, 542])
        hal = sb("hal", [128, 44, 2])
        hT = sb("hT", [128, 8, T], BF16)
        qT = sb("qT", [128, 4, T], BF16)
        oT = sb("oT", [128, 4, T], BF16)
        hc2 = sb("hc2", [128, 4, T], BF16)
        gmT = sb("gmT", [128, 4, T], BF16)
        gbc = sb("gbc", [128, 2, D])
        pv = sb("pv", [128, NV])
        bvt = sb("bvt", [128, 2560])
        fgt = sb("fgt", [128, D])
        cstt = sb("cstt", [128, 4])
        ident = sb("ident", [128, 128], BF16)
        onesb = sb("onesb", [128, 128], BF16)
        wsT = sb("wsT", [128, 4, 128], BF16)
        cact = sb("cact", [128, 8, 2], BF16)
        cactf = sb("cactf", [128, 8, 2])
        crep = sb("crep", [128, 8, 128], BF16)
        modsb = sb("modsb", [128, 48])
        drv = sb("drv", [128, 40])
        nlam_t = sb("nlam_t", [128, 16])
        gsub_t = sb("gsub_t", [128, 16])
        eps_t = sb("eps_t", [128, 16])
        psall = es.enter_context(nc.psum_tensor("psall", [128, 8, 512], F32))
        ps = [psall[:, i, :] for i in range(8)]

        R = {}

        def reg(name):
            if name not in R:
                R[name] = _Reg()
            return R[name]

        def regs(name, n):
            return [reg(f"{name}{i}") for i in range(n)]

        r_xt = regs("xt", 4)
        r_KT = regs("KT", 4)
        r_VC = reg("VC")
        r_wpan = regs("wpan", NSLOT)
        r_cin = regs("cin", 4)
        r_hal = reg("hal")
        r_hT = regs("hT", 8)
        r_qT = regs("qT", 4)
        r_oT = regs("oT", 4)
        r_hc2 = regs("hc2", 4)
        r_gm = regs("gm", 4)
        r_gbc = regs("gbc", 4)
        r_pv = reg("pv")
        r_bvt = reg("bvt")
        r_fgt = reg("fgt")
        r_cst = reg("cst")
        r_const = reg("const")
        r_wsT = reg("wsT")
        r_cact = reg("cact")
        r_crep = reg("crep")
        r_mod = reg("modsb")
        r_drv = reg("drv")
        r_ps = regs("ps", 8)
        r_xdram = {}

        ch_w = [chan(f"w{i}") for i in range(NSLOT)]
        ch_x = chan("xld")
        ch_xs = chan("xst")
        ch_m = chan("misc")
        ch_p = chan("pos")
        ch_d = chan("dbg")

        def dbg(name, ap, shape, dt, rlist):
            if not DEBUG:
                return
            t = nc.dram_tensor("dbg_" + name, list(shape), dt, kind="ExternalOutput").ap()
            dbg_outs[name] = (list(shape), dt)
            dma(SP, ch_d, lambda: nc.sync.dma_start(out=t, in_=ap), reads=rlist)
            SP.eng.wait_ge(ch_d.sem, ch_d.count)

        dma(SP, ch_m, lambda: nc.sync.dma_start(out=cstt[:], in_=cst_in[:, :]), writes=[r_cst])
        dma(SP, ch_m, lambda: nc.sync.dma_start(out=cactf[:], in_=cT_in[:, :, :]), writes=[r_cact])
        dma(SP, ch_m, lambda: nc.sync.dma_start(
            out=fgt[:], in_=bass.AP(tensor=fing_in.tensor, offset=0, ap=[[0, 128], [1, D]])), writes=[r_fgt])
        with contextlib.ExitStack() as cs:
            onesf = sb("onesf", [128, 128], F32, cs)
            idf = sb("idf", [128, 128], F32, cs)
            r_t = reg("ctmp")
            op(POOL, lambda: nc.gpsimd.memset(onesf[:], 1.0), writes=[r_t])
            op(POOL, lambda: nc.gpsimd.affine_select(out=idf[:], in_=onesf[:], pattern=[[1, 128]],
                                                     compare_op=ALU.is_equal, fill=0.0, base=0,
                                                     channel_multiplier=-1), reads=[r_t], writes=[r_const])
            op(DVE, lambda: nc.vector.tensor_copy(ident[:], idf[:]), reads=[r_const], writes=[r_const])
            op(DVE, lambda: nc.vector.tensor_copy(onesb[:], onesf[:]), reads=[r_t], writes=[r_const])
            op(ACT, lambda: nc.scalar.activation(out=cact[:], in_=cactf[:], func=AF.Silu), reads=[r_cact], writes=[r_cact])
            op(DVE, lambda: nc.vector.memset(drv[:], 0.0), writes=[r_drv])
            op(POOL, lambda: nc.gpsimd.memset(eps_t[:], EPS), reads=[r_drv], writes=[r_drv])
            op(POOL, lambda: nc.gpsimd.memset(nlam_t[:], 0.0), reads=[r_drv], writes=[r_drv])
            op(POOL, lambda: nc.gpsimd.memset(gsub_t[:], 0.0), reads=[r_drv], writes=[r_drv])
            barrier()

        bank_rr = [0]

        def next_bank(lo=0, hi=8):
            b = lo + (bank_rr[0] % (hi - lo))
            bank_rr[0] += 1
            return b

        slot_rr = [0]

        def load_panel(nm, l, kc0, kcn, c0, ncols):
            if not cv_waited[l]:
                SP.eng.wait_ge(cv[l], cvtot[l])
                cv_waited[l] = True
            s = slot_rr[0] % NSLOT
            slot_rr[0] += 1
            view = wpan[s][:, 0:kcn * ncols].rearrange("p (k n) -> p k n", n=ncols)
            src = wb[nm][l].rearrange("(kc p) n -> p kc n", p=128)[:, kc0:kc0 + kcn, c0:c0 + ncols]
            dma(SP, ch_w[s], lambda: nc.sync.dma_start(out=view, in_=src), writes=[r_wpan[s]])
            return view, r_wpan[s]

        def rsqrt_act(out_ap, in_ap, scale, rin, rout, tmp_ap, rtmp):
            op(ACT, lambda: nc.scalar.activation(out=tmp_ap, in_=in_ap, func=AF.Ln, scale=scale, bias=eps_t[:, 0:1]),
               reads=rin + [r_drv], writes=[rtmp])
            op(ACT, lambda: nc.scalar.activation(out=out_ap, in_=tmp_ap, func=AF.Exp, scale=-0.5),
               reads=[rtmp], writes=rout)

        def norm_to_hT(gs_col, sh_col, stack_name):
            with contextlib.ExitStack() as ls:
                xn = sb("xn" + stack_name, [128, 4, D], BF16, ls)
                junk = sb("junk" + stack_name, [128, D], BF16, ls)
                ss = sb("ss" + stack_name, [128, 4], F32, ls)
                rs = sb("rs" + stack_name, [128, 4], F32, ls)
                lt = sb("lt" + stack_name, [128, 4], F32, ls)
                r_xn = regs("xn_" + stack_name, 4)
                r_junk, r_ss, r_rs, r_lt = reg("junk"), reg("ss"), reg("rs"), reg("lt")
                for ts in range(4):
                    op(ACT, lambda ts=ts: nc.scalar.activation(out=junk[:], in_=xt[:, ts, :], func=AF.Square,
                                                                accum_out=ss[:, ts:ts + 1]),
                       reads=[r_xt[ts]], writes=[r_junk, r_ss])
                rsqrt_act(rs[:], ss[:], 1.0 / D, [r_ss], [r_rs], lt[:], r_lt)
                for ts in range(4):
                    op(DVE, lambda ts=ts: nc.vector.tensor_scalar(xn[:, ts, :], xt[:, ts, :], rs[:, ts:ts + 1], None,
                                                                  op0=ALU.mult),
                       reads=[r_xt[ts], r_rs], writes=[r_xn[ts]])
                for kc in range(8):
                    b = next_bank()
                    pb = ps[b][:].bitcast(BF16)
                    for ts in range(4):
                        op(PE, lambda ts=ts, kc=kc, pb=pb: nc.tensor.transpose(
                            pb[:, ts * 128:(ts + 1) * 128], xn[:, ts, kc * 128:(kc + 1) * 128], ident[:]),
                           reads=[r_xn[ts], r_const], writes=[r_ps[b]])
                    op(DVE, lambda kc=kc, pb=pb: nc.vector.tensor_scalar(
                        hT[:, kc, :], pb[:, 0:T], drv[:, gs_col + kc:gs_col + kc + 1],
                        drv[:, sh_col + kc:sh_col + kc + 1], op0=ALU.mult, op1=ALU.add),
                       reads=[r_ps[b], r_drv], writes=[r_hT[kc]])
                barrier()

        for l in range(NL):
            lam_init = 0.8 - 0.6 * math.exp(-0.3 * l)
            dma(SP, ch_m, lambda: nc.sync.dma_start(out=pv[:], in_=pvec_in[l, :, :]), writes=[r_pv])
            dma(SP, ch_m, lambda: nc.sync.dma_start(
                out=bvt[:], in_=bass.AP(tensor=bvec_in.tensor, offset=l * NB, ap=[[0, 128], [1, 2560]])),
                writes=[r_bvt])
            with contextlib.ExitStack() as ls:
                wsf = sb("wsf", [128, 4, 128], F32, ls)
                wsm = sb("wsm", [128, 4, 128], F32, ls)
                lmt = sb("lmt", [128, 256], F32, ls)
                lpr = sb("lpr", [128, 128], F32, ls)
                lsm = sb("lsm", [128, 2], F32, ls)
                lex = sb("lex", [128, 2], F32, ls)
                r_wsf, r_lmt, r_l2, r_l3, r_l4 = reg("wsf"), reg("lmt"), reg("lpr"), reg("lsm"), reg("lex")
                dma(SP, ch_m, lambda: nc.sync.dma_start(out=wsf[:], in_=wsp_in[l, :, :, :]), writes=[r_wsf], local=True)
                dma(SP, ch_m, lambda: nc.sync.dma_start(
                    out=lmt[:], in_=bass.AP(tensor=bvec_in.tensor, offset=l * NB + B_LAM, ap=[[0, 128], [1, 256]])),
                    writes=[r_lmt], local=True)
                op(POOL, lambda: nc.gpsimd.affine_select(out=wsm[:], in_=wsf[:], pattern=[[0, 4], [1, 128]],
                                                         compare_op=ALU.is_ge, fill=0.0, base=0,
                                                         channel_multiplier=-1), reads=[r_wsf], writes=[r_l2])
                op(DVE, lambda: nc.vector.tensor_copy(wsT[:], wsm[:]), reads=[r_l2], writes=[r_wsT])
                lp3 = sb("lp3", [128, 2, 64], F32, ls)
                lyy = sb("lyy", [128, 2], F32, ls)
                r_l6 = reg("lyy")
                op(DVE, lambda: nc.vector.tensor_tensor(out=lp3[:, 0, :], in0=lmt[:, 0:64], in1=lmt[:, 64:128], op=ALU.mult),
                   reads=[r_lmt], writes=[r_l3])
                op(DVE, lambda: nc.vector.tensor_tensor(out=lp3[:, 1, :], in0=lmt[:, 128:192], in1=lmt[:, 192:256], op=ALU.mult),
                   reads=[r_lmt], writes=[r_l3])
                for w_ in (32, 16, 8, 4, 2, 1):
                    op(DVE, lambda w_=w_: nc.vector.tensor_tensor(out=lp3[:, :, 0:w_], in0=lp3[:, :, 0:w_], in1=lp3[:, :, w_:2 * w_], op=ALU.add),
                       reads=[], writes=[r_l3])
                lxa = sb("lxa", [128, 16], F32, ls)
                lxb = sb("lxb", [128, 16], F32, ls)
                r_lx = [reg("lxa"), reg("lxb")]
                lx = [lxa, lxb]
                step = [0]

                def chain(fn_dve, fn_pool, extra_reads):
                    i_ = step[0] % 2
                    src, dst = lx[i_], lx[1 - i_]
                    if step[0] % 2 == 0:
                        op(DVE, lambda: fn_dve(dst, src), reads=[r_lx[i_]] + extra_reads, writes=[r_lx[1 - i_]])
                    else:
                        op(POOL, lambda: fn_pool(dst, src), reads=[r_lx[i_]] + extra_reads, writes=[r_lx[1 - i_]])
                    step[0] += 1

                op(POOL, lambda: nc.gpsimd.tensor_scalar(lyy[:], lp3[:, :, 0], 1.0 / 64, None, op0=ALU.mult), reads=[r_l3], writes=[r_l6])
                op(DVE, lambda: nc.vector.tensor_scalar(lxa[:, 0:2], lyy[:], 0.2, 1.0, op0=ALU.mult, op1=ALU.add), reads=[r_l6], writes=[r_lx[0]])
                for cf in (0.25, 1.0 / 3.0, 0.5, 1.0):
                    chain(lambda d_, s_: nc.vector.tensor_tensor(out=d_[:, 0:2], in0=s_[:, 0:2], in1=lyy[:], op=ALU.mult),
                          lambda d_, s_: nc.gpsimd.tensor_tensor(out=d_[:, 0:2], in0=s_[:, 0:2], in1=lyy[:], op=ALU.mult), [r_l6])
                    chain(lambda d_, s_, cf=cf: nc.vector.tensor_scalar(d_[:, 0:2], s_[:, 0:2], cf, 1.0, op0=ALU.mult, op1=ALU.add),
                          lambda d_, s_, cf=cf: nc.gpsimd.tensor_scalar(d_[:, 0:2], s_[:, 0:2], cf, 1.0, op0=ALU.mult, op1=ALU.add), [])
                for _ in range(6):
                    chain(lambda d_, s_: nc.vector.tensor_tensor(out=d_[:, 0:2], in0=s_[:, 0:2], in1=s_[:, 0:2], op=ALU.mult),
                          lambda d_, s_: nc.gpsimd.tensor_tensor(out=d_[:, 0:2], in0=s_[:, 0:2], in1=s_[:, 0:2], op=ALU.mult), [])
                lexf = lx[step[0] % 2]
                r_lexf = r_lx[step[0] % 2]
                op(DVE, lambda: nc.vector.scalar_tensor_tensor(out=nlam_t[:, 0:1], in0=lexf[:, 1:2], scalar=-lam_init,
                                                               in1=lexf[:, 0:1], op0=ALU.add, op1=ALU.subtract),
                   reads=[r_lexf, r_drv], writes=[r_drv])
                op(POOL, lambda: nc.gpsimd.tensor_scalar(gsub_t[:, 0:1], pv[:, P_SUBG:P_SUBG + 1], 1.0 - lam_init, None,
                                                        op0=ALU.mult), reads=[r_pv, r_drv], writes=[r_drv])
                barrier()

            for s in range(NSEQ):
                with contextlib.ExitStack() as ls:
                    bgt = sb("bgt", [128, 2048], F32, ls)
                    onesl = sb("onesl", [128, 128], BF16, ls)
                    r_bgt = reg("bgt")
                    dma(SP, ch_m, lambda: nc.sync.dma_start(
                        out=bgt[:], in_=bass.AP(tensor=bvec_in.tensor, offset=l * NB + B_BG1, ap=[[0, 128], [1, 2048]])),
                        writes=[r_bgt], local=True)
                    for kc in range(8):
                        op(DVE, lambda kc=kc: nc.vector.tensor_scalar(crep[:, kc, :], onesb[:], cact[:, kc, s:s + 1], None,
                                                                      op0=ALU.mult),
                           reads=[r_const, r_cact], writes=[r_crep])
                    mb = next_bank()
                    for pi in range(12):
                        pan, rp = load_panel("w_ada", l, 0, 8, pi * 512, 512)
                        which = pi // 2
                        if which in (2, 5):
                            gb = next_bank()
                            while gb == mb:
                                gb = next_bank()
                            for kc in range(8):
                                op(PE, lambda kc=kc, gb=gb, pan=pan: nc.tensor.matmul(
                                    ps[gb][:], lhsT=crep[:, kc, :], rhs=pan[:, kc, :], start=(kc == 0), stop=(kc == 7)),
                                   reads=[r_crep, rp], writes=[r_ps[gb]])
                            gi = 0 if which == 2 else 1
                            half = pi % 2
                            op(DVE, lambda gb=gb, gi=gi, half=half: nc.vector.tensor_tensor(
                                out=gbc[:, gi, half * 512:(half + 1) * 512], in0=ps[gb][:],
                                in1=bgt[:, gi * 1024 + half * 512: gi * 1024 + (half + 1) * 512], op=ALU.add),
                               reads=[r_ps[gb], r_bgt], writes=[r_gbc[gi * 2 + half]])
                        else:
                            for cc in range(4):
                                j = pi * 4 + cc
                                for kc in range(8):
                                    op(PE, lambda kc=kc, j=j, cc=cc, pan=pan: nc.tensor.matmul(
                                        ps[mb][:, j:j + 1], lhsT=pan[:, kc, cc * 128:(cc + 1) * 128],
                                        rhs=cact[:, kc, s:s + 1], start=(kc == 0), stop=(kc == 7)),
                                       reads=[r_cact, rp], writes=[r_ps[mb]])
                    op(DVE, lambda: nc.vector.tensor_tensor(out=modsb[:, 0:16], in0=ps[mb][:, 0:16],
                                                            in1=pv[:, P_BADA:P_BADA + 16], op=ALU.add),
                       reads=[r_ps[mb], r_pv], writes=[r_mod])
                    op(DVE, lambda: nc.vector.tensor_tensor(out=modsb[:, 24:40], in0=ps[mb][:, 24:40],
                                                            in1=pv[:, P_BADA + 24:P_BADA + 40], op=ALU.add),
                       reads=[r_ps[mb], r_pv, r_mod], writes=[r_mod])
                    op(DVE, lambda: nc.vector.scalar_tensor_tensor(out=drv[:, 0:8], in0=modsb[:, 8:16], scalar=1.0,
                                                                   in1=pv[:, P_LN1G:P_LN1G + 8], op0=ALU.add, op1=ALU.mult),
                       reads=[r_mod, r_pv, r_drv], writes=[r_drv])
                    op(POOL, lambda: nc.gpsimd.tensor_copy(drv[:, 8:16], modsb[:, 0:8]), reads=[r_mod, r_drv], writes=[r_drv])
                    op(DVE, lambda: nc.vector.scalar_tensor_tensor(out=drv[:, 16:24], in0=modsb[:, 32:40], scalar=1.0,
                                                                   in1=pv[:, P_LN2G:P_LN2G + 8], op0=ALU.add, op1=ALU.mult),
                       reads=[r_mod, r_pv, r_drv], writes=[r_drv])
                    op(POOL, lambda: nc.gpsimd.tensor_copy(drv[:, 24:32], modsb[:, 24:32]), reads=[r_mod, r_drv], writes=[r_drv])
                    barrier()

                op(POOL, lambda: nc.gpsimd.memset(hal[:], 0.0), writes=[r_hal])
                op(POOL, lambda: nc.gpsimd.memset(cin[:, :, 0:30], 0.0), writes=r_cin)

                for t in range(NT):
                    row0 = s * S + t * T
                    key = (s, t)
                    if key not in r_xdram:
                        r_xdram[key] = _Reg()
                    xsrc = (x_in if l == 0 else out)[row0:row0 + T, :].rearrange("(ts p) d -> p ts d", p=128)
                    dma(SP, ch_x, lambda: nc.sync.dma_start(out=xt[:], in_=xsrc), reads=[r_xdram[key]], writes=r_xt)

                    norm_to_hT(0, 8, "a")
                    if DEBUG and l == 0 and s == 0 and t == 0:
                        dbg("hT", hT[:], [128, 8, T], BF16, r_hT)
                    with contextlib.ExitStack() as ls:
                        cosT = sb("cosT", [128, T], F32, ls)
                        sinT = sb("sinT", [128, T], F32, ls)
                        pti = sb("pti", [128, T], I32, ls)
                        ang = sb("ang", [128, T], F32, ls)
                        tk = sb("tk", [128, T], I32, ls)
                        tr = sb("tr", [128, T], F32, ls)
                        tm = sb("tm", [128, T], F32, ls)
                        tA = sb("tA", [128, T], F32, ls)
                        tB = sb("tB", [128, T], F32, ls)
                        uT = sb("uT", [128, 4, T], BF16, ls)
                        g1 = sb("g1", [128, T], F32, ls)
                        g2 = sb("g2", [128, T], F32, ls)
                        g3 = sb("g3", [128, T], F32, ls)
                        g4 = sb("g4", [128, T], F32, ls)
                        vvn = sb("vvn", [128, T], BF16, ls)
                        st6 = sb("st6", [128, 6], F32, ls)
                        mv = sb("mv", [128, 2], F32, ls)
                        mv2 = sb("mv2", [128, 2], F32, ls)
                        r_cos, r_sin, r_pti, r_ang, r_tk, r_tr, r_tm = (reg("cosT"), reg("sinT"), reg("pti"), reg("ang"),
                                                                         reg("tk"), reg("tr"), reg("tm"))
                        r_tA, r_tB = reg("tA"), reg("tB")
                        r_uT = regs("uT", 4)
                        r_g = regs("gtmp", 5)
                        r_vvn, r_st6, r_mv, r_mv2 = reg("vvn"), reg("st6"), reg("mv"), reg("mv2")

                        psrc = bass.AP(tensor=pos_in.tensor, offset=s * S + t * T, ap=[[0, 128], [1, T]])
                        dma(SP, ch_p, lambda: nc.sync.dma_start(out=pti[:], in_=psrc), writes=[r_pti], local=True)
                        op(DVE, lambda: nc.vector.tensor_scalar(ang[:], pti[:], cstt[:, 0:1], None, op0=ALU.mult),
                           reads=[r_pti, r_cst], writes=[r_ang])

                        def sin_table(dst, r_dst, shift, scale_ap):
                            src = ang
                            if shift != 0.0:
                                op(DVE, lambda: nc.vector.tensor_scalar(tm[:], ang[:], shift, None, op0=ALU.add),
                                   reads=[r_ang], writes=[r_tm])
                                src = tm
                            op(DVE, lambda: nc.vector.tensor_scalar(tk[:], src[:], 1.0 / TWO_PI, None, op0=ALU.mult),
                               reads=[r_ang, r_tm], writes=[r_tk])
                            op(DVE, lambda: nc.vector.scalar_tensor_tensor(out=tr[:], in0=tk[:], scalar=-TWO_PI, in1=src[:],
                                                                           op0=ALU.mult, op1=ALU.add),
                               reads=[r_tk, r_ang, r_tm], writes=[r_tr])
                            op(DVE, lambda: nc.vector.tensor_scalar(tm[:], tr[:], math.pi, TWO_PI, op0=ALU.is_gt, op1=ALU.mult),
                               reads=[r_tr], writes=[r_tm])
                            op(DVE, lambda: nc.vector.tensor_tensor(out=tr[:], in0=tr[:], in1=tm[:], op=ALU.subtract),
                               reads=[r_tm, r_tr], writes=[r_tr])
                            op(DVE, lambda: nc.vector.tensor_scalar(tm[:], tr[:], -math.pi, TWO_PI, op0=ALU.is_lt, op1=ALU.mult),
                               reads=[r_tr], writes=[r_tm])
                            op(DVE, lambda: nc.vector.tensor_tensor(out=tr[:], in0=tr[:], in1=tm[:], op=ALU.add),
                               reads=[r_tm, r_tr], writes=[r_tr])
                            op(DVE, lambda: nc.vector.tensor_scalar(tr[:], tr[:], 3.1415925, -3.1415925, op0=ALU.min, op1=ALU.max),
                               reads=[r_tr], writes=[r_tr])
                            if scale_ap is None:
                                op(ACT, lambda: nc.scalar.activation(out=dst[:], in_=tr[:], func=AF.Sin),
                                   reads=[r_tr], writes=[r_dst])
                            else:
                                op(ACT, lambda: nc.scalar.activation(out=dst[:], in_=tr[:], func=AF.Sin, scale=scale_ap),
                                   reads=[r_tr, r_cst], writes=[r_dst])

                        sin_table(sinT, r_sin, 0.0, cstt[:, 1:2])
                        sin_table(cosT, r_cos, math.pi / 2.0, None)

                        def fm_chunk(pan, rp, cc):
                            b = next_bank()
                            for kc in range(8):
                                op(PE, lambda kc=kc, b=b: nc.tensor.matmul(
                                    ps[b][:], lhsT=pan[:, kc, cc * 128:(cc + 1) * 128], rhs=hT[:, kc, :],
                                    start=(kc == 0), stop=(kc == 7)),
                                   reads=[rp, r_hT[kc]], writes=[r_ps[b]])
                            return b

                        def tm_sub(pan, rp, ts):
                            b = next_bank()
                            for kc in range(8):
                                op(PE, lambda kc=kc, b=b: nc.tensor.matmul(
                                    ps[b][:], lhsT=hT[:, kc, ts * 128:(ts + 1) * 128], rhs=pan[:, kc, :],
                                    start=(kc == 0), stop=(kc == 7)),
                                   reads=[rp, r_hT[kc]], writes=[r_ps[b]])
                            return b

                        for which in range(2):
                            panA, rpA = load_panel("w_in", l, 0, 8, (2 * which) * 512, 512)
                            panB, rpB = load_panel("w_in", l, 0, 8, (2 * which + 1) * 512, 512)
                            for hd in range(4):
                                bA = fm_chunk(panA, rpA, hd)
                                bB = fm_chunk(panB, rpB, hd)
                                colA = P_BIN + (2 * which) * 4 + hd
                                colB = P_BIN + (2 * which + 1) * 4 + hd
                                op(DVE, lambda bA=bA, colA=colA: nc.vector.scalar_tensor_tensor(
                                    out=tA[:], in0=ps[bA][:], scalar=pv[:, colA:colA + 1], in1=cosT[:],
                                    op0=ALU.add, op1=ALU.mult), reads=[r_ps[bA], r_pv, r_cos], writes=[r_tA])
                                op(DVE, lambda bB=bB, colB=colB: nc.vector.scalar_tensor_tensor(
                                    out=tB[:], in0=ps[bB][:], scalar=pv[:, colB:colB + 1], in1=sinT[:],
                                    op0=ALU.add, op1=ALU.mult), reads=[r_ps[bB], r_pv, r_sin], writes=[r_tB])
                                if which == 0:
                                    op(POOL, lambda hd=hd: nc.gpsimd.tensor_tensor(out=qT[:, hd, :], in0=tA[:], in1=tB[:], op=ALU.add),
                                       reads=[r_tA, r_tB], writes=[r_qT[hd]])
                                else:
                                    op(POOL, lambda hd=hd: nc.gpsimd.tensor_tensor(out=KT[:, hd, t * T:(t + 1) * T], in0=tA[:],
                                                                                   in1=tB[:], op=ALU.add),
                                       reads=[r_tA, r_tB], writes=[r_KT[hd]])
                        pan, rp = load_panel("w_in", l, 0, 8, 4 * 512, 512)
                        for ts in range(4):
                            b = tm_sub(pan, rp, ts)
                            op(DVE, lambda b=b, ts=ts: nc.vector.tensor_tensor(
                                out=VC[:, t * 4 + ts, :], in0=ps[b][:], in1=bvt[:, B_BV:B_BV + 512], op=ALU.add),
                               reads=[r_ps[b], r_bvt], writes=[r_VC])
                        if DEBUG and l == 0 and s == 0 and t == 0:
                            dbg("qT", qT[:], [128, 4, T], BF16, r_qT)
                            dbg("KT", KT[:, :, 0:T], [128, 4, T], BF16, r_KT)
                            dbg("VC", VC[:, 0:4, :], [128, 4, 512], BF16, [r_VC])

                        panA, rpA = load_panel("w_in", l, 0, 8, 5 * 512, 512)
                        panG, rpG = load_panel("w_in", l, 0, 8, 6 * 512, 512)
                        for c in range(4):
                            bG = fm_chunk(panG, rpG, c)
                            op(ACT, lambda bG=bG, c=c: nc.scalar.activation(out=g1[:], in_=ps[bG][:], func=AF.Sigmoid,
                                                                            bias=pv[:, P_BIN + 24 + c:P_BIN + 25 + c]),
                               reads=[r_ps[bG], r_pv], writes=[r_g[0]])
                            bA = fm_chunk(panA, rpA, c)
                            op(DVE, lambda bA=bA, c=c: nc.vector.scalar_tensor_tensor(
                                out=cin[:, c, 30:542], in0=ps[bA][:], scalar=pv[:, P_BIN + 20 + c:P_BIN + 21 + c], in1=g1[:],
                                op0=ALU.add, op1=ALU.mult), reads=[r_ps[bA], r_pv, r_g[0]], writes=[r_cin[c]])
                        def gelu_tanh(dst_ap, r_dst, xin, r_x):
                            op(POOL, lambda: nc.gpsimd.tensor_tensor(out=g2[:], in0=xin[:], in1=xin[:], op=ALU.mult),
                               reads=[r_x], writes=[r_g[1]])
                            op(DVE, lambda: nc.vector.tensor_scalar(g2[:], g2[:], 0.044715, 1.0, op0=ALU.mult, op1=ALU.add),
                               reads=[r_g[1]], writes=[r_g[1]])
                            op(POOL, lambda: nc.gpsimd.tensor_tensor(out=g2[:], in0=g2[:], in1=xin[:], op=ALU.mult),
                               reads=[r_x, r_g[1]], writes=[r_g[1]])
                            op(ACT, lambda: nc.scalar.activation(out=g3[:], in_=g2[:], func=AF.Sigmoid, scale=1.5957691216057308),
                               reads=[r_g[1]], writes=[r_g[2]])
                            op(DVE, lambda: nc.vector.tensor_tensor(out=dst_ap, in0=xin[:], in1=g3[:], op=ALU.mult),
                               reads=[r_x, r_g[2]], writes=r_dst)

                        pan, rp = load_panel("w_in", l, 0, 8, 7 * 512, 512)
                        for c in range(4):
                            b = fm_chunk(pan, rp, c)
                            op(ACT, lambda b=b, c=c: nc.scalar.activation(out=g1[:], in_=ps[b][:], func=AF.Identity,
                                                                          bias=pv[:, P_BIN + 28 + c:P_BIN + 29 + c]),
                               reads=[r_ps[b], r_pv], writes=[r_g[0]])
                            gelu_tanh(uT[:, c, :], [r_uT[c]], g1, r_g[0])
                        pan, rp = load_panel("w_in", l, 0, 8, 8 * 512, 512)
                        bsg = [next_bank() for _ in range(4)]
                        for ts in range(4):
                            b = next_bank()
                            while b in bsg:
                                b = next_bank()
                            for kc in range(8):
                                op(PE, lambda kc=kc, b=b, ts=ts: nc.tensor.matmul(
                                    ps[b][:], lhsT=hT[:, kc, ts * 128:(ts + 1) * 128], rhs=pan[:, kc, :],
                                    start=(kc == 0), stop=(kc == 7)), reads=[rp, r_hT[kc]], writes=[r_ps[b]])
                            op(DVE, lambda b=b: nc.vector.tensor_tensor(out=g1[:], in0=ps[b][:], in1=bvt[:, B_BVV:B_BVV + 512], op=ALU.add),
                               reads=[r_ps[b], r_bvt], writes=[r_g[0]])
                            gelu_tanh(g4[:], [r_g[3]], g1, r_g[0])
                            op(DVE, lambda: nc.vector.bn_stats(st6[:], g4[:]), reads=[r_g[3]], writes=[r_st6])
                            op(DVE, lambda: nc.vector.bn_aggr(mv[:], st6[:]), reads=[r_st6], writes=[r_mv])
                            rsqrt_act(mv2[:, 0:1], mv[:, 1:2], 1.0, [r_mv], [r_mv2], mv2[:, 1:2], r_mv2)
                            op(DVE, lambda: nc.vector.tensor_scalar(g4[:], g4[:], mv[:, 0:1], mv2[:, 0:1], op0=ALU.subtract, op1=ALU.mult),
                               reads=[r_mv, r_mv2], writes=[r_g[3]])
                            op(POOL, lambda: nc.gpsimd.tensor_tensor(out=g4[:], in0=g4[:], in1=bvt[:, B_GG:B_GG + 512], op=ALU.mult),
                               reads=[r_bvt], writes=[r_g[3]])
                            op(DVE, lambda: nc.vector.tensor_tensor(out=vvn[:], in0=g4[:], in1=bvt[:, B_GB:B_GB + 512], op=ALU.add),
                               reads=[r_g[3], r_bvt], writes=[r_vvn])
                            for g in range(4):
                                op(PE, lambda g=g, ts=ts: nc.tensor.matmul(
                                    ps[bsg[g]][:, ts * 128:(ts + 1) * 128], lhsT=vvn[:, g * 128:(g + 1) * 128], rhs=wsT[:, g, :],
                                    start=True, stop=True), reads=[r_vvn, r_wsT], writes=[r_ps[bsg[g]]])
                        for g in range(4):
                            bview = bvt[:, B_BSP + g * 128:B_BSP + (g + 1) * 128]
                            for ts in range(4):
                                op(DVE, lambda g=g, ts=ts, bview=bview: nc.vector.tensor_tensor(
                                    out=g1[:, ts * 128:(ts + 1) * 128], in0=ps[bsg[g]][:, ts * 128:(ts + 1) * 128], in1=bview, op=ALU.add),
                                   reads=[r_ps[bsg[g]], r_bvt], writes=[r_g[0]])
                            op(POOL, lambda g=g: nc.gpsimd.tensor_tensor(out=gmT[:, g, :], in0=g1[:], in1=uT[:, g, :], op=ALU.mult),
                               reads=[r_g[0], r_uT[g]], writes=[r_gm[g]])
                        if DEBUG and l == 0 and s == 0 and t == 0:
                            dbg("gmT", gmT[:], [128, 4, T], BF16, r_gm)
                        barrier()

                    with contextlib.ExitStack() as ls:
                        ptg = [sb(f"ptg{i}", [128, 4, T], BF16, ls) for i in range(2)]
                        r_ptg = regs("ptg", 2)
                        a1 = sb("a1", [128, T], F32, ls)
                        a2 = sb("a2", [128, T], F32, ls)
                        a3 = sb("a3", [128, T], F32, ls)
                        a4 = sb("a4", [128, T], F32, ls)
                        a5 = sb("a5", [128, T], BF16, ls)
                        a3s = sb("a3s", [128, 4, T], F32, ls)
                        r_a = regs("atmp", 5)
                        r_a3s = regs("a3s", 4)
                        cout = sb("cout", [128, 4, T], F32, ls)
                        g2 = sb("g2b", [128, T], F32, ls)
                        g3 = sb("g3b", [128, T], F32, ls)
                        g4 = sb("g4b", [128, T], F32, ls)
                        g1 = sb("g1b", [128, T], F32, ls)
                        hcb = sb("hcb", [128, T], BF16, ls)
                        sqb = sb("sqb", [128, T], BF16, ls)
                        r_cout = regs("cout", 4)
                        r_g = regs("gtmpb", 5)
                        r_hcb, r_sqb = reg("hcb"), reg("sqb")
                        nkt = 4 * (t + 1)
                        ngrp = nkt // 2
                        for hd in range(4):
                            bo1, bo2, bs1, bs2 = 4, 5, 6, 7

                            def scores(g, hd=hd):
                                for i_ in range(2):
                                    kt = 2 * g + i_
                                    for m in range(2):
                                        bk = i_ * 2 + m
                                        op(PE, lambda m=m, bk=bk, kt=kt: nc.tensor.matmul(
                                            ps[bk][:, 0:T], lhsT=KT[m * 64:(m + 1) * 64, hd, kt * 128:(kt + 1) * 128],
                                            rhs=qT[m * 64:(m + 1) * 64, hd, 0:T], start=True, stop=True),
                                           reads=[r_KT[hd], r_qT[hd]], writes=[r_ps[bk]])

                            def exps(g):
                                op(ACT, lambda g=g: nc.scalar.activation(out=ptg[g % 2][:, :, :], in_=psall[:, 0:4, :],
                                                                         func=AF.Exp, scale=0.125),
                                   reads=[r_ps[0], r_ps[1], r_ps[2], r_ps[3]], writes=[r_ptg[g % 2]])

                            def pvs(g, hd=hd):
                                for i_ in range(2):
                                    kt = 2 * g + i_
                                    j = kt - 4 * t
                                    c0 = 128 * j if j > 0 else 0
                                    c1 = c0 + 64 if j >= 0 else c0
                                    first = (kt == 0)
                                    last = (kt == nkt - 1)
                                    for m in range(2):
                                        p_ = ptg[g % 2][:, i_ * 2 + m, :]
                                        rp_ = r_ptg[g % 2]
                                        bo = bo1 if m == 0 else bo2
                                        bs_ = bs1 if m == 0 else bs2
                                        for (dst_b, lw_full, lw_half) in (
                                                (bo, VC[:, kt, hd * 128:(hd + 1) * 128], VC[0:64, kt, hd * 128:(hd + 1) * 128]),
                                                (bs_, onesb[:], onesb[0:64, :])):
                                            op(PE, lambda dst_b=dst_b, lw_full=lw_full, p_=p_, c1=c1, first=first, last=last, j=j: nc.tensor.matmul(
                                                ps[dst_b][:, c1:T], lhsT=lw_full, rhs=p_[:, c1:T],
                                                start=first, stop=(last and j < 0), skip_group_check=True),
                                               reads=[r_VC, r_const, rp_], writes=[r_ps[dst_b]])
                                            if j >= 0:
                                                op(PE, lambda dst_b=dst_b, lw_half=lw_half, p_=p_, c0=c0, last=last: nc.tensor.matmul(
                                                    ps[dst_b][:, c0:c0 + 64], lhsT=lw_half, rhs=p_[0:64, c0:c0 + 64],
                                                    start=False, stop=last, skip_group_check=True),
                                                   reads=[r_VC, r_const, rp_], writes=[r_ps[dst_b]])

                            scores(0)
                            exps(0)
                            for g in range(ngrp):
                                if g + 1 < ngrp:
                                    scores(g + 1)
                                    exps(g + 1)
                                pvs(g)
                            c = hd
                            op(DVE, lambda c=c: nc.vector.tensor_scalar(
                                cout[:, c, :], cin[:, c, 0:512], pv[:, P_CW + c:P_CW + c + 1], pv[:, P_CB + c:P_CB + c + 1],
                                op0=ALU.mult, op1=ALU.add), reads=[r_cin[c], r_pv], writes=[r_cout[c]])
                            op(DVE, lambda c=c: nc.vector.tensor_scalar(
                                g2[:], cin[:, c, 1:513], pv[:, P_CW + 4 + c:P_CW + 4 + c + 1], None, op0=ALU.mult),
                               reads=[r_cin[c], r_pv], writes=[r_g[1]])
                            for k in range(2, 31):
                                if k % 2 == 0:
                                    op(DVE, lambda c=c, k=k: nc.vector.scalar_tensor_tensor(
                                        out=cout[:, c, :], in0=cin[:, c, k:k + 512], scalar=pv[:, P_CW + k * 4 + c:P_CW + k * 4 + c + 1],
                                        in1=cout[:, c, :], op0=ALU.mult, op1=ALU.add),
                                       reads=[r_cin[c], r_pv], writes=[r_cout[c]])
                                else:
                                    op(DVE, lambda c=c, k=k: nc.vector.scalar_tensor_tensor(
                                        out=g2[:], in0=cin[:, c, k:k + 512], scalar=pv[:, P_CW + k * 4 + c:P_CW + k * 4 + c + 1],
                                        in1=g2[:], op0=ALU.mult, op1=ALU.add),
                                       reads=[r_cin[c], r_pv], writes=[r_g[1]])
                            op(DVE, lambda c=c: nc.vector.tensor_tensor(out=cout[:, c, :], in0=cout[:, c, :], in1=g2[:], op=ALU.add),
                               reads=[r_g[1]], writes=[r_cout[c]])
                            op(POOL, lambda c=c: nc.gpsimd.tensor_copy(cin[:, c, 0:30], cin[:, c, 512:542]),
                               reads=[], writes=[r_cin[c]])
                            op(ACT, lambda: nc.scalar.activation(out=a1[:], in_=ps[bs1][:], func=AF.Ln), reads=[r_ps[bs1]], writes=[r_a[0]])
                            op(ACT, lambda: nc.scalar.activation(out=a2[:], in_=ps[bs2][:], func=AF.Ln), reads=[r_ps[bs2]], writes=[r_a[1]])
                            op(ACT, lambda: nc.scalar.activation(out=a1[:], in_=a1[:], func=AF.Exp, scale=-1.0), reads=[], writes=[r_a[0]])
                            op(ACT, lambda: nc.scalar.activation(out=a2[:], in_=a2[:], func=AF.Exp, scale=-1.0), reads=[], writes=[r_a[1]])
                            op(DVE, lambda: nc.vector.tensor_tensor(out=a1[:], in0=ps[bo1][:], in1=a1[:], op=ALU.mult),
                               reads=[r_ps[bo1]], writes=[r_a[0]])
                            op(DVE, lambda: nc.vector.tensor_tensor(out=a2[:], in0=ps[bo2][:], in1=a2[:], op=ALU.mult),
                               reads=[r_ps[bo2]], writes=[r_a[1]])
                            op(DVE, lambda hd=hd: nc.vector.scalar_tensor_tensor(out=a3s[:, hd, :], in0=a2[:], scalar=nlam_t[:, 0:1], in1=a1[:],
                                                                                 op0=ALU.mult, op1=ALU.add),
                               reads=[r_a[0], r_a[1], r_drv], writes=[r_a3s[hd]])
                            if DEBUG and l == 0 and s == 0 and t == 0 and hd == 0:
                                dbg("a1", a1[:], [128, T], F32, [r_a[0]])
                                dbg("a2", a2[:], [128, T], F32, [r_a[1]])
                        for hd in range(4):
                            op(POOL, lambda hd=hd: nc.gpsimd.tensor_tensor(out=a5[:], in0=a3s[:, hd, :], in1=a3s[:, hd, :], op=ALU.mult),
                               reads=[r_a3s[hd]], writes=[r_a[4]])
                            bq = hd % 4
                            op(PE, lambda bq=bq: nc.tensor.matmul(ps[bq][:], lhsT=onesb[:], rhs=a5[:], start=True, stop=True),
                               reads=[r_a[4], r_const], writes=[r_ps[bq]])
                            rsqrt_act(a4[:], ps[bq][:], 1.0 / 128, [r_ps[bq]], [r_a[3]], a1[:], r_a[0])
                            op(DVE, lambda hd=hd: nc.vector.scalar_tensor_tensor(out=oT[:, hd, :], in0=a3s[:, hd, :], scalar=gsub_t[:, 0:1],
                                                                                 in1=a4[:], op0=ALU.mult, op1=ALU.mult),
                               reads=[r_a3s[hd], r_a[3], r_drv], writes=[r_oT[hd]])
                        bS, bQ = 4, 5
                        for c in range(4):
                            op(ACT, lambda c=c: nc.scalar.activation(out=hcb[:], in_=cout[:, c, :], func=AF.Copy),
                               reads=[r_cout[c]], writes=[r_hcb])
                            op(ACT, lambda c=c: nc.scalar.activation(out=sqb[:], in_=cout[:, c, :], func=AF.Square),
                               reads=[r_cout[c]], writes=[r_sqb])
                            op(PE, lambda c=c: nc.tensor.matmul(ps[bS][:], lhsT=onesb[:], rhs=hcb[:], start=(c == 0), stop=(c == 3)),
                               reads=[r_hcb, r_const], writes=[r_ps[bS]])
                            op(PE, lambda c=c: nc.tensor.matmul(ps[bQ][:], lhsT=onesb[:], rhs=sqb[:], start=(c == 0), stop=(c == 3)),
                               reads=[r_sqb, r_const], writes=[r_ps[bQ]])
                        op(DVE, lambda: nc.vector.tensor_scalar(g2[:], ps[bS][:], 1.0 / 512, None, op0=ALU.mult),
                           reads=[r_ps[bS]], writes=[r_g[1]])
                        op(DVE, lambda: nc.vector.tensor_tensor(out=g3[:], in0=g2[:], in1=g2[:], op=ALU.mult),
                           reads=[r_g[1]], writes=[r_g[2]])
                        op(DVE, lambda: nc.vector.scalar_tensor_tensor(out=g3[:], in0=ps[bQ][:], scalar=1.0 / 512, in1=g3[:],
                                                                       op0=ALU.mult, op1=ALU.subtract),
                           reads=[r_ps[bQ], r_g[2]], writes=[r_g[2]])
                        op(DVE, lambda: nc.vector.tensor_scalar(g3[:], g3[:], 0.0, None, op0=ALU.max),
                           reads=[r_g[2]], writes=[r_g[2]])
                        rsqrt_act(g4[:], g3[:], 1.0, [r_g[2]], [r_g[3]], g1[:], r_g[0])
                        for c in range(4):
                            op(DVE, lambda c=c: nc.vector.tensor_tensor(out=cout[:, c, :], in0=cout[:, c, :], in1=g2[:], op=ALU.subtract),
                               reads=[r_g[1]], writes=[r_cout[c]])
                            op(POOL, lambda c=c: nc.gpsimd.tensor_tensor(out=cout[:, c, :], in0=cout[:, c, :], in1=g4[:], op=ALU.mult),
                               reads=[r_g[3]], writes=[r_cout[c]])
                            op(ACT, lambda c=c: nc.scalar.activation(out=hc2[:, c, :], in_=cout[:, c, :], func=AF.Silu,
                                                                     scale=pv[:, P_CLG + c:P_CLG + c + 1],
                                                                     bias=pv[:, P_CLB + c:P_CLB + c + 1]),
                               reads=[r_cout[c], r_pv], writes=[r_hc2[c]])
                        if DEBUG and l == 0 and s == 0 and t == 0:
                            dbg("oT", oT[:], [128, 4, T], BF16, r_oT)
                            dbg("hc2", hc2[:], [128, 4, T], BF16, r_hc2)
                        barrier()

                    with contextlib.ExitStack() as ls:
                        mixf = sb("mixf", [128, 8, T], F32, ls)
                        mixb = sb("mixb", [128, 8, T], BF16, ls)
                        gt4 = [sb(f"gt4_{i}", [128, 4, T], BF16, ls) for i in range(2)]
                        tmpc = sb("tmpc", [128, T], F32, ls)
                        r_mixf = regs("mixf", 8)
                        r_mixb = regs("mixb", 8)
                        r_gt4 = regs("gt4", 2)
                        r_tmpc = reg("tmpc")
                        srcs = [(oT, r_oT, "w_att"), (hc2, r_hc2, "w_conv"), (gmT, r_gm, "w_gmlp")]
                        gi = 0
                        for bi, (srcT, r_src, wn) in enumerate(srcs):
                            wo_, rwo = load_panel(wn, l, 0, 4, 0, D)
                            for half in range(2):
                                gp, rgp = load_panel("w_in", l, 0, 8, (9 + bi * 2 + half) * 512, 512)
                                gtile = gt4[gi % 2]
                                rg = r_gt4[gi % 2]
                                gi += 1
                                for cc in range(4):
                                    b = next_bank()
                                    for kc in range(8):
                                        op(PE, lambda kc=kc, b=b, cc=cc, gp=gp: nc.tensor.matmul(
                                            ps[b][:], lhsT=gp[:, kc, cc * 128:(cc + 1) * 128], rhs=hT[:, kc, :],
                                            start=(kc == 0), stop=(kc == 7)), reads=[rgp, r_hT[kc]], writes=[r_ps[b]])
                                    col = P_BIN + (9 + bi * 2 + half) * 4 + cc
                                    op(ACT, lambda b=b, cc=cc, col=col, gtile=gtile: nc.scalar.activation(
                                        out=gtile[:, cc, :], in_=ps[b][:], func=AF.Sigmoid, bias=pv[:, col:col + 1]),
                                       reads=[r_ps[b], r_pv], writes=[rg])
                                for cc in range(4):
                                    dc = half * 4 + cc
                                    b = next_bank()
                                    for kc in range(4):
                                        op(PE, lambda kc=kc, b=b, dc=dc, wo_=wo_, srcT=srcT: nc.tensor.matmul(
                                            ps[b][:], lhsT=wo_[:, kc, dc * 128:(dc + 1) * 128], rhs=srcT[:, kc, :],
                                            start=(kc == 0), stop=(kc == 3)), reads=[rwo, r_src[kc]], writes=[r_ps[b]])
                                    if bi == 0:
                                        op(DVE, lambda b=b, dc=dc, cc=cc, gtile=gtile: nc.vector.tensor_tensor(
                                            out=mixf[:, dc, :], in0=ps[b][:], in1=gtile[:, cc, :], op=ALU.mult),
                                           reads=[r_ps[b], rg], writes=[r_mixf[dc]])
                                    else:
                                        op(DVE, lambda b=b, cc=cc, gtile=gtile: nc.vector.tensor_tensor(
                                            out=tmpc[:], in0=ps[b][:], in1=gtile[:, cc, :], op=ALU.mult),
                                           reads=[r_ps[b], rg], writes=[r_tmpc])
                                        if bi == 1:
                                            op(POOL, lambda dc=dc: nc.gpsimd.tensor_tensor(
                                                out=mixf[:, dc, :], in0=mixf[:, dc, :], in1=tmpc[:], op=ALU.add),
                                               reads=[r_tmpc], writes=[r_mixf[dc]])
                                        else:
                                            op(POOL, lambda dc=dc: nc.gpsimd.tensor_tensor(
                                                out=mixb[:, dc, :], in0=mixf[:, dc, :], in1=tmpc[:], op=ALU.add),
                                               reads=[r_tmpc, r_mixf[dc]], writes=[r_mixb[dc]])
                        if DEBUG and l == 0 and s == 0 and t == 0:
                            dbg("mixb", mixb[:], [128, 8, T], BF16, r_mixb)
                        for nh in range(2):
                            wo_, rwo = load_panel("w_o", l, 0, 8, nh * 512, 512)
                            for ts in range(4):
                                b = next_bank()
                                for kc in range(8):
                                    op(PE, lambda kc=kc, b=b, ts=ts, wo_=wo_: nc.tensor.matmul(
                                        ps[b][:], lhsT=mixb[:, kc, ts * 128:(ts + 1) * 128], rhs=wo_[:, kc, :],
                                        start=(kc == 0), stop=(kc == 7)), reads=[rwo, r_mixb[kc]], writes=[r_ps[b]])
                                op(DVE, lambda b=b, nh=nh: nc.vector.tensor_tensor(
                                    out=tmpc[:], in0=ps[b][:], in1=gbc[:, 0, nh * 512:(nh + 1) * 512], op=ALU.mult),
                                   reads=[r_ps[b], r_gbc[nh]], writes=[r_tmpc])
                                op(POOL, lambda ts=ts, nh=nh: nc.gpsimd.tensor_tensor(
                                    out=xt[:, ts, nh * 512:(nh + 1) * 512], in0=xt[:, ts, nh * 512:(nh + 1) * 512], in1=tmpc[:], op=ALU.add),
                                   reads=[r_tmpc], writes=[r_xt[ts]])
                        if DEBUG and l == 0 and s == 0 and t == 0:
                            dbg("xmid", xt[:], [128, 4, D], F32, r_xt)
                        barrier()

                    norm_to_hT(16, 24, "d")
                    with contextlib.ExitStack() as ls:
                        actT = sb("actT", [128, 22, T], BF16, ls)
                        stg = [sb(f"stg{i}", [128, 514], F32, ls) for i in range(4)]
                        ft = [sb(f"ft{i}", [128, T], F32, ls) for i in range(4)]
                        r_act = regs("actT", 22)
                        r_stg = regs("stg", 4)
                        r_ft = regs("ft", 4)
                        pending = []
                        pair_i = [0]
                        for pi in range(11):
                            pan, rp = load_panel("w_up", l, 0, 8, pi * 512, 512)
                            for pr in range(2):
                                info = []
                                for q_ in range(2):
                                    cc = pr * 2 + q_
                                    j = pi * 4 + cc
                                    b = next_bank()
                                    for kc in range(8):
                                        op(PE, lambda kc=kc, b=b, cc=cc, pan=pan: nc.tensor.matmul(
                                            ps[b][:], lhsT=pan[:, kc, cc * 128:(cc + 1) * 128], rhs=hT[:, kc, :],
                                            start=(kc == 0), stop=(kc == 7)), reads=[rp, r_hT[kc]], writes=[r_ps[b]])
                                    bi_ = (pair_i[0] % 2) * 2 + q_
                                    sg_, rs_, f_, rf_ = stg[bi_], r_stg[bi_], ft[bi_], r_ft[bi_]
                                    op(POOL, lambda sg_=sg_, j=j: nc.gpsimd.tensor_copy(sg_[:, 0:2], hal[:, j, :]),
                                       reads=[r_hal], writes=[rs_])
                                    op(ACT, lambda sg_=sg_, b=b: nc.scalar.activation(out=sg_[:, 2:514], in_=ps[b][:], func=AF.Copy),
                                       reads=[r_ps[b]], writes=[rs_])
                                    op(POOL, lambda sg_=sg_, j=j: nc.gpsimd.tensor_copy(hal[:, j, :], sg_[:, 512:514]),
                                       reads=[rs_], writes=[r_hal])
                                    info.append((j, sg_, rs_, f_, rf_))
                                for (j, sg_, rs_, f_, rf_) in info:
                                    op(DVE, lambda sg_=sg_, f_=f_, j=j: nc.vector.tensor_scalar(
                                        f_[:], sg_[:, 2:514], pv[:, P_FW + 2 * 44 + j:P_FW + 2 * 44 + j + 1], pv[:, P_FB + j:P_FB + j + 1],
                                        op0=ALU.mult, op1=ALU.add), reads=[rs_, r_pv], writes=[rf_])
                                for (j, sg_, rs_, f_, rf_) in info:
                                    op(DVE, lambda sg_=sg_, f_=f_, j=j: nc.vector.scalar_tensor_tensor(
                                        out=f_[:], in0=sg_[:, 1:513], scalar=pv[:, P_FW + 44 + j:P_FW + 44 + j + 1], in1=f_[:],
                                        op0=ALU.mult, op1=ALU.add), reads=[rs_, r_pv], writes=[rf_])
                                for (j, sg_, rs_, f_, rf_) in info:
                                    op(DVE, lambda sg_=sg_, f_=f_, j=j: nc.vector.scalar_tensor_tensor(
                                        out=f_[:], in0=sg_[:, 0:512], scalar=pv[:, P_FW + j:P_FW + j + 1], in1=f_[:],
                                        op0=ALU.mult, op1=ALU.add), reads=[rs_, r_pv], writes=[rf_])
                                pair_i[0] += 1

                                def finals(items):
                                    for (j, sg_, rs_, f_, rf_) in items:
                                        if j < 22:
                                            op(ACT, lambda f_=f_, j=j: nc.scalar.activation(out=actT[:, j, :], in_=f_[:], func=AF.Silu),
                                               reads=[rf_], writes=[r_act[j]])
                                        else:
                                            op(POOL, lambda f_=f_, j=j: nc.gpsimd.tensor_tensor(
                                                out=actT[:, j - 22, :], in0=actT[:, j - 22, :], in1=f_[:], op=ALU.mult),
                                               reads=[rf_], writes=[r_act[j - 22]])

                                if pending:
                                    finals(pending.pop())
                                pending.append(info)
                        if pending:
                            finals(pending.pop())
                        if DEBUG and l == 0 and s == 0 and t == 0:
                            dbg("actT", actT[:], [128, 22, T], BF16, r_act)
                        for pi in range(11):
                            pan, rp = load_panel("w_down", l, 2 * pi, 2, 0, D)
                            for kk in range(2):
                                kc = 2 * pi + kk
                                for ts in range(4):
                                    for nh in range(2):
                                        b = ts * 2 + nh
                                        op(PE, lambda kc=kc, kk=kk, b=b, ts=ts, nh=nh, pan=pan: nc.tensor.matmul(
                                            ps[b][:], lhsT=actT[:, kc, ts * 128:(ts + 1) * 128], rhs=pan[:, kk, nh * 512:(nh + 1) * 512],
                                            start=(kc == 0), stop=(kc == 21), skip_group_check=True),
                                           reads=[rp, r_act[kc]], writes=[r_ps[b]])
                        for ts in range(4):
                            for nh in range(2):
                                b = ts * 2 + nh
                                f_ = ft[b % 2]
                                rf_ = r_ft[b % 2]
                                op(DVE, lambda b=b, nh=nh, f_=f_: nc.vector.tensor_tensor(
                                    out=f_[:], in0=ps[b][:], in1=gbc[:, 1, nh * 512:(nh + 1) * 512], op=ALU.mult),
                                   reads=[r_ps[b], r_gbc[2 + nh]], writes=[rf_])
                                op(POOL, lambda ts=ts, nh=nh, f_=f_: nc.gpsimd.tensor_tensor(
                                    out=xt[:, ts, nh * 512:(nh + 1) * 512], in0=xt[:, ts, nh * 512:(nh + 1) * 512], in1=f_[:], op=ALU.add),
                                   reads=[rf_], writes=[r_xt[ts]])
                        barrier()

                    if l == NL - 1:
                        with contextlib.ExitStack() as ls:
                            junk = sb("fjunk", [128, D], BF16, ls)
                            ss = sb("fss", [128, 4], F32, ls)
                            rs = sb("frs", [128, 4], F32, ls)
                            lt = sb("flt", [128, 4], F32, ls)
                            r_j, r_s1, r_s2, r_s3 = reg("fjunk"), reg("fss"), reg("frs"), reg("flt")
                            for ts in range(4):
                                op(ACT, lambda ts=ts: nc.scalar.activation(out=junk[:], in_=xt[:, ts, :], func=AF.Square,
                                                                            accum_out=ss[:, ts:ts + 1]),
                                   reads=[r_xt[ts]], writes=[r_j, r_s1])
                            rsqrt_act(rs[:], ss[:], 1.0 / D, [r_s1], [r_s2], lt[:], r_s3)
                            for ts in range(4):
                                op(DVE, lambda ts=ts: nc.vector.scalar_tensor_tensor(
                                    out=xt[:, ts, :], in0=xt[:, ts, :], scalar=rs[:, ts:ts + 1], in1=fgt[:],
                                    op0=ALU.mult, op1=ALU.mult), reads=[r_s2, r_fgt], writes=[r_xt[ts]])
                            barrier()
                    xdst = out[row0:row0 + T, :].rearrange("(ts p) d -> p ts d", p=128)
                    dma(POOL, ch_xs, lambda: nc.gpsimd.dma_start(out=xdst, in_=xt[:]), reads=r_xt, writes=[r_xdram[key]])

        nc.gpsimd.wait_ge(ch_xs.sem, ch_xs.count)
        nc.sync.wait_ge(ch_xs.sem, ch_xs.count)
    return nc, dbg_outs


def _prep_shared(inp):
    f = lambda a: np.ascontiguousarray(np.asarray(a, dtype=np.float32))
    w_in = f(inp["w_in"])
    b_in = f(inp["b_in"])
    perm = np.arange(512).reshape(4, 2, 64)
    perm = np.concatenate([perm[:, :, 32:], perm[:, :, :32]], axis=2).reshape(512)
    segs = [np.arange(0, 512), perm, 512 + np.arange(512), 512 + perm, 1024 + np.arange(512),
            1536 + np.arange(1024), 2560 + np.arange(1024), 3584 + np.arange(3072)]
    cols = np.concatenate(segs)
    assert cols.shape[0] == NEXT
    w_in_e = np.ascontiguousarray(w_in[:, :, cols])
    b_in_e = b_in[:, cols]

    def fm(v):
        Ln, n = v.shape
        return v.reshape(Ln, n // 128, 128).transpose(0, 2, 1)

    pvec = np.zeros((L, 128, NV), np.float32)
    pvec[:, :, P_BIN:P_BIN + 60] = fm(b_in_e)
    pvec[:, :, P_LN1G:P_LN1G + 8] = fm(f(inp["ln1_g"]))
    pvec[:, :, P_LN2G:P_LN2G + 8] = fm(f(inp["ln2_g"]))
    cw = f(inp["conv_dw_w"])
    pvec[:, :, P_CW:P_CW + 124] = cw.reshape(L, 31, 4, 128).transpose(0, 3, 1, 2).reshape(L, 128, 124)
    pvec[:, :, P_CB:P_CB + 4] = fm(f(inp["conv_dw_b"]))
    pvec[:, :, P_CLG:P_CLG + 4] = fm(f(inp["conv_ln_g"]))
    pvec[:, :, P_CLB:P_CLB + 4] = fm(f(inp["conv_ln_b"]))
    pvec[:, :, P_SUBG:P_SUBG + 1] = fm(f(inp["attn_subln_g"]))
    fw = f(inp["ffn_dw_w"])
    pvec[:, :, P_FW:P_FW + 132] = fw.reshape(L, 3, 44, 128).transpose(0, 3, 1, 2).reshape(L, 128, 132)
    pvec[:, :, P_FB:P_FB + 44] = fm(f(inp["ffn_dw_b"]))
    b_ada = f(inp["b_ada"])
    pvec[:, :, P_BADA:P_BADA + 48] = fm(b_ada)

    bvec = np.zeros((L, NB), np.float32)
    bvec[:, B_BV:B_BV + 512] = b_in[:, 1024:1536]
    bvec[:, B_BVV:B_BVV + 512] = b_in[:, 3072:3584]
    bvec[:, B_GG:B_GG + 512] = f(inp["gmlp_ln_g"])
    bvec[:, B_GB:B_GB + 512] = f(inp["gmlp_ln_b"])
    bvec[:, B_BSP:B_BSP + 512] = f(inp["b_spatial"]).reshape(L, 512)
    bvec[:, B_BG1:B_BG1 + 1024] = b_ada[:, 2048:3072]
    bvec[:, B_BG2:B_BG2 + 1024] = b_ada[:, 5120:6144]
    bvec[:, B_LAM:B_LAM + 64] = f(inp["lambda_q1"])
    bvec[:, B_LAM + 64:B_LAM + 128] = f(inp["lambda_k1"])
    bvec[:, B_LAM + 128:B_LAM + 192] = f(inp["lambda_q2"])
    bvec[:, B_LAM + 192:B_LAM + 256] = f(inp["lambda_k2"])

    inv_freq = (1.0 / (10000.0 ** (np.arange(0, 64, 2, dtype=np.float32) / 64.0))).astype(np.float32)
    cst = np.zeros((128, 4), np.float32)
    d = np.arange(128) % 64
    cst[:, 0] = inv_freq[d % 32]
    cst[:, 1] = np.where(d < 32, -1.0, 1.0)

    shared = {
        "cst": cst, "pvec": pvec, "bvec": bvec, "fing": f(inp["final_g"]).reshape(1, D),
        "wspT": np.ascontiguousarray(f(inp["w_spatial"]).transpose(0, 3, 1, 2)),
        "w_in": w_in_e, "w_ada": f(inp["w_ada"]), "w_att": f(inp["w_attn_out"]), "w_conv": f(inp["w_conv_out"]),
        "w_gmlp": f(inp["w_gmlp_out"]), "w_o": f(inp["w_o"]), "w_up": f(inp["w_up"]), "w_down": f(inp["w_down"]),
    }
    return shared


def _in_maps(inp, shared):
    x = np.asarray(inp["x"], dtype=np.float32)
    c = np.asarray(inp["c"], dtype=np.float32)
    pos = np.asarray(inp["positions"], dtype=np.int32)
    maps = []
    for core in range(8):
        b0 = 2 * core
        m = dict(shared)
        m["x"] = np.ascontiguousarray(x[b0:b0 + 2].reshape(2 * S, D))
        m["cT"] = np.ascontiguousarray(c[b0:b0 + 2].reshape(2, 8, 128).transpose(2, 1, 0))
        m["pos"] = np.ascontiguousarray(pos[b0:b0 + 2])
        maps.append(m)
    return maps


def kernel(**inputs):
    shared = _prep_shared(inputs)
    maps = _in_maps(inputs, shared)
    nc, _ = build_nc()
    res = run_bass_kernel_spmd(nc, maps, core_ids=list(range(8)))
    outs = [np.asarray(r["out"], dtype=np.float32).reshape(2, S, D) for r in res.results]
    return np.concatenate(outs, axis=0)
```

```python
import math
import contextlib
import numpy as np
import concourse.bass as bass
import concourse.mybir as mybir
from concourse.bass_utils import run_bass_kernel_spmd

F32 = mybir.dt.float32
BF16 = mybir.dt.bfloat16
I32 = mybir.dt.int32
AF = mybir.ActivationFunctionType
ALU = mybir.AluOpType

D = 1024
S = 4096
L = 4
T = 512
FF = 2816
NEXT = 7680
EPS = 1e-6
TWO_PI = 2.0 * math.pi

P_BIN = 0
P_LN1G = 60
P_LN2G = 68
P_CW = 76
P_CB = 200
P_CLG = 204
P_CLB = 208
P_SUBG = 212
P_FW = 213
P_FB = 345
P_BADA = 389
NV = 437
B_BV = 0
B_BVV = 512
B_GG = 1024
B_GB = 1536
B_BSP = 2048
B_BG1 = 2560
B_BG2 = 3584
B_LAM = 4608
NB = 4864


class _ES:
    def __init__(self, name, eng, sem, inc):
        self.name, self.eng, self.sem, self.inc = name, eng, sem, inc
        self.count = 0
        self.seen = {}


class _Reg:
    __slots__ = ("w", "r")

    def __init__(self):
        self.w = None
        self.r = {}


def build_nc(NL=L, NSEQ=2, NT=8, DEBUG=False, LW=L):
    nc = bass.Bass("TRN2", target_bir_lowering=False)

    def din(name, shape, dt=F32):
        return nc.dram_tensor(name, list(shape), dt, kind="ExternalInput").ap()

    def dscr(name, shape, dt=BF16):
        return nc.dram_tensor(name, list(shape), dt, kind="Internal").ap()

    x_in = din("x", [2 * S, D])
    cT_in = din("cT", [128, 8, 2])
    pos_in = din("pos", [2, S], I32)
    cst_in = din("cst", [128, 4])
    pvec_in = din("pvec", [LW, 128, NV])
    bvec_in = din("bvec", [LW, NB])
    fing_in = din("fing", [1, D])
    wsp_in = din("wspT", [LW, 128, 4, 128])
    wnames = [("w_in", D, NEXT), ("w_ada", D, 6144), ("w_att", 512, D), ("w_conv", 512, D),
              ("w_gmlp", 512, D), ("w_o", D, D), ("w_up", D, 2 * FF), ("w_down", FF, D)]
    wf = {}
    wb = {}
    for nm, r, c in wnames:
        wf[nm] = din(nm, [LW, r, c])
        wb[nm] = dscr(nm + "_b", [LW, r, c])
    out = nc.dram_tensor("out", [2 * S, D], F32, kind="ExternalOutput").ap()
    dbg_outs = {}

    es = contextlib.ExitStack()
    with es:
        def sem(n):
            return es.enter_context(nc.semaphore(n))

        PE = _ES("pe", nc.tensor, sem("s_pe"), 1)
        ACT = _ES("act", nc.scalar, sem("s_act"), 1)
        DVE = _ES("dve", nc.vector, sem("s_dve"), 1)
        POOL = _ES("pool", nc.gpsimd, sem("s_pool"), 1)
        SP = _ES("sp", nc.sync, sem("s_sp"), 1)
        COMPUTE = [PE, ACT, DVE, POOL]

        def chan(n):
            return _ES(n, None, sem("c_" + n), 16)

        def _deps(reads, writes):
            d = {}
            for r in reads:
                if r.w is not None:
                    st, c = r.w
                    if d.get(st, 0) < c:
                        d[st] = c
            for w in writes:
                if w.w is not None:
                    st, c = w.w
                    if d.get(st, 0) < c:
                        d[st] = c
                for st, c in w.r.items():
                    if d.get(st, 0) < c:
                        d[st] = c
            return d

        def _wait(E, d):
            for st, c in d.items():
                if st is E and (E is PE or c < E.count):
                    continue
                if E.seen.get(st, 0) < c:
                    E.eng.wait_ge(st.sem, c)
                    E.seen[st] = c

        def op(E, fn, reads=(), writes=()):
            _wait(E, _deps(reads, writes))
            ins = fn()
            E.count += 1
            ins.then_inc(E.sem, 1)
            for w in writes:
                w.w = (E, E.count)
                w.r = {}
            for r in reads:
                r.r[E] = E.count

        bar_counts = {}

        def dma(Q, ch, fn, reads=(), writes=(), local=False):
            if local:
                _wait(Q, dict(bar_counts))
            _wait(Q, _deps(reads, writes))
            ins = fn()
            ch.count += 16
            ins.then_inc(ch.sem, 16)
            for w in writes:
                w.w = (ch, ch.count)
                w.r = {}
            for r in reads:
                r.r[ch] = ch.count
            if ch.name == "misc":
                Q.eng.wait_ge(ch.sem, ch.count)

        def barrier():
            for E in COMPUTE:
                for Fx in COMPUTE:
                    if Fx is E:
                        continue
                    if E.seen.get(Fx, 0) < Fx.count:
                        E.eng.wait_ge(Fx.sem, Fx.count)
                        E.seen[Fx] = Fx.count
            for E in COMPUTE:
                bar_counts[E] = E.count

        uniq = [0]

        def sb(name, shape, dt=F32, stack=es):
            uniq[0] += 1
            return stack.enter_context(nc.sbuf_tensor(f"{name}_{uniq[0]}", list(shape), dt))

        cv = [sem(f"cv{l}") for l in range(L)]
        cvtot = [0] * L
        for l in range(NL):
            for nm, r, c in wnames:
                for r0 in range(0, r, 128):
                    nc.gpsimd.dma_start(out=wb[nm][l, r0:r0 + 128, :], in_=wf[nm][l, r0:r0 + 128, :]).then_inc(cv[l], 16)
                    cvtot[l] += 16
        cv_waited = [False] * L

        xt = sb("xt", [128, 4, D])
        KT = sb("KT", [128, 4, S], BF16)
        VC = sb("VC", [128, 32, 512], BF16)
        NSLOT = 3
        wpan = [sb(f"wpan{i}", [128, 4096], BF16) for i in range(NSLOT)]
        cin = sb("cin", [128, 4, 542])
        hal = sb("hal", [128, 44, 2])
        hT = sb("hT", [128, 8, T], BF16)
        qT = sb("qT", [128, 4, T], BF16)
        oT = sb("oT", [128, 4, T], BF16)
        hc2 = sb("hc2", [128, 4, T], BF16)
        gmT = sb("gmT", [128, 4, T], BF16)
        gbc = sb("gbc", [128, 2, D])
        pv = sb("pv", [128, NV])
        bvt = sb("bvt", [128, 2560])
        fgt = sb("fgt", [128, D])
        cstt = sb("cstt", [128, 4])
        ident = sb("ident", [128, 128], BF16)
        onesb = sb("onesb", [128, 128], BF16)
        wsT = sb("wsT", [128, 4, 128], BF16)
        cact = sb("cact", [128, 8, 2], BF16)
        cactf = sb("cactf", [128, 8, 2])
        crep = sb("crep", [128, 8, 128], BF16)
        modsb = sb("modsb", [128, 48])
        drv = sb("drv", [128, 40])
        nlam_t = sb("nlam_t", [128, 16])
        gsub_t = sb("gsub_t", [128, 16])
        eps_t = sb("eps_t", [128, 16])
        psall = es.enter_context(nc.psum_tensor("psall", [128, 8, 512], F32))
        ps = [psall[:, i, :] for i in range(8)]

        R = {}

        def reg(name):
            if name not in R:
                R[name] = _Reg()
            return R[name]

        def regs(name, n):
            return [reg(f"{name}{i}") for i in range(n)]

        r_xt = regs("xt", 4)
        r_KT = regs("KT", 4)
        r_VC = reg("VC")
        r_wpan = regs("wpan", NSLOT)
        r_cin = regs("cin", 4)
        r_hal = reg("hal")
        r_hT = regs("hT", 8)
        r_qT = regs("qT", 4)
        r_oT = regs("oT", 4)
        r_hc2 = regs("hc2", 4)
        r_gm = regs("gm", 4)
        r_gbc = regs("gbc", 4)
        r_pv = reg("pv")
        r_bvt = reg("bvt")
        r_fgt = reg("fgt")
        r_cst = reg("cst")
        r_const = reg("const")
        r_wsT = reg("wsT")
        r_cact = reg("cact")
        r_crep = reg("crep")
        r_mod = reg("modsb")
        r_drv = reg("drv")
        r_ps = regs("ps", 8)
        r_xdram = {}

        ch_w = [chan(f"w{i}") for i in range(NSLOT)]
        ch_x = chan("xld")
        ch_xs = chan("xst")
        ch_m = chan("misc")
        ch_p = chan("pos")
        ch_d = chan("dbg")

        def dbg(name, ap, shape, dt, rlist):
            if not DEBUG:
                return
            t = nc.dram_tensor("dbg_" + name, list(shape), dt, kind="ExternalOutput").ap()
            dbg_outs[name] = (list(shape), dt)
            dma(SP, ch_d, lambda: nc.sync.dma_start(out=t, in_=ap), reads=rlist)
            SP.eng.wait_ge(ch_d.sem, ch_d.count)

        dma(SP, ch_m, lambda: nc.sync.dma_start(out=cstt[:], in_=cst_in[:, :]), writes=[r_cst])
        dma(SP, ch_m, lambda: nc.sync.dma_start(out=cactf[:], in_=cT_in[:, :, :]), writes=[r_cact])
        dma(SP, ch_m, lambda: nc.sync.dma_start(
            out=fgt[:], in_=bass.AP(tensor=fing_in.tensor, offset=0, ap=[[0, 128], [1, D]])), writes=[r_fgt])
        with contextlib.ExitStack() as cs:
            onesf = sb("onesf", [128, 128], F32, cs)
            idf = sb("idf", [128, 128], F32, cs)
            r_t = reg("ctmp")
            op(POOL, lambda: nc.gpsimd.memset(onesf[:], 1.0), writes=[r_t])
            op(POOL, lambda: nc.gpsimd.affine_select(out=idf[:], in_=onesf[:], pattern=[[1, 128]],
                                                     compare_op=ALU.is_equal, fill=0.0, base=0,
                                                     channel_multiplier=-1), reads=[r_t], writes=[r_const])
            op(DVE, lambda: nc.vector.tensor_copy(ident[:], idf[:]), reads=[r_const], writes=[r_const])
            op(DVE, lambda: nc.vector.tensor_copy(onesb[:], onesf[:]), reads=[r_t], writes=[r_const])
            op(ACT, lambda: nc.scalar.activation(out=cact[:], in_=cactf[:], func=AF.Silu), reads=[r_cact], writes=[r_cact])
            op(DVE, lambda: nc.vector.memset(drv[:], 0.0), writes=[r_drv])
            op(POOL, lambda: nc.gpsimd.memset(eps_t[:], EPS), reads=[r_drv], writes=[r_drv])
            op(POOL, lambda: nc.gpsimd.memset(nlam_t[:], 0.0), reads=[r_drv], writes=[r_drv])
            op(POOL, lambda: nc.gpsimd.memset(gsub_t[:], 0.0), reads=[r_drv], writes=[r_drv])
            barrier()

        bank_rr = [0]

        def next_bank(lo=0, hi=8):
            b = lo + (bank_rr[0] % (hi - lo))
            bank_rr[0] += 1
            return b

        slot_rr = [0]

        def load_panel(nm, l, kc0, kcn, c0, ncols):
            if not cv_waited[l]:
                SP.eng.wait_ge(cv[l], cvtot[l])
                cv_waited[l] = True
            s = slot_rr[0] % NSLOT
            slot_rr[0] += 1
            view = wpan[s][:, 0:kcn * ncols].rearrange("p (k n) -> p k n", n=ncols)
            src = wb[nm][l].rearrange("(kc p) n -> p kc n", p=128)[:, kc0:kc0 + kcn, c0:c0 + ncols]
            dma(SP, ch_w[s], lambda: nc.sync.dma_start(out=view, in_=src), writes=[r_wpan[s]])
            return view, r_wpan[s]

        def rsqrt_act(out_ap, in_ap, scale, rin, rout, tmp_ap, rtmp):
            op(ACT, lambda: nc.scalar.activation(out=tmp_ap, in_=in_ap, func=AF.Ln, scale=scale, bias=eps_t[:, 0:1]),
               reads=rin + [r_drv], writes=[rtmp])
            op(ACT, lambda: nc.scalar.activation(out=out_ap, in_=tmp_ap, func=AF.Exp, scale=-0.5),
               reads=[rtmp], writes=rout)

        def norm_to_hT(gs_col, sh_col, stack_name):
            with contextlib.ExitStack() as ls:
                xn = sb("xn" + stack_name, [128, 4, D], BF16, ls)
                junk = sb("junk" + stack_name, [128, D], BF16, ls)
                ss = sb("ss" + stack_name, [128, 4], F32, ls)
                rs = sb("rs" + stack_name, [128, 4], F32, ls)
                lt = sb("lt" + stack_name, [128, 4], F32, ls)
                r_xn = regs("xn_" + stack_name, 4)
                r_junk, r_ss, r_rs, r_lt = reg("junk"), reg("ss"), reg("rs"), reg("lt")
                for ts in range(4):
                    op(ACT, lambda ts=ts: nc.scalar.activation(out=junk[:], in_=xt[:, ts, :], func=AF.Square,
                                                                accum_out=ss[:, ts:ts + 1]),
                       reads=[r_xt[ts]], writes=[r_junk, r_ss])
                rsqrt_act(rs[:], ss[:], 1.0 / D, [r_ss], [r_rs], lt[:], r_lt)
                for ts in range(4):
                    op(DVE, lambda ts=ts: nc.vector.tensor_scalar(xn[:, ts, :], xt[:, ts, :], rs[:, ts:ts + 1], None,
                                                                  op0=ALU.mult),
                       reads=[r_xt[ts], r_rs], writes=[r_xn[ts]])
                for kc in range(8):
                    b = next_bank()
                    pb = ps[b][:].bitcast(BF16)
                    for ts in range(4):
                        op(PE, lambda ts=ts, kc=kc, pb=pb: nc.tensor.transpose(
                            pb[:, ts * 128:(ts + 1) * 128], xn[:, ts, kc * 128:(kc + 1) * 128], ident[:]),
                           reads=[r_xn[ts], r_const], writes=[r_ps[b]])
                    op(DVE, lambda kc=kc, pb=pb: nc.vector.tensor_scalar(
                        hT[:, kc, :], pb[:, 0:T], drv[:, gs_col + kc:gs_col + kc + 1],
                        drv[:, sh_col + kc:sh_col + kc + 1], op0=ALU.mult, op1=ALU.add),
                       reads=[r_ps[b], r_drv], writes=[r_hT[kc]])
                barrier()

        for l in range(NL):
            lam_init = 0.8 - 0.6 * math.exp(-0.3 * l)
            dma(SP, ch_m, lambda: nc.sync.dma_start(out=pv[:], in_=pvec_in[l, :, :]), writes=[r_pv])
            dma(SP, ch_m, lambda: nc.sync.dma_start(
                out=bvt[:], in_=bass.AP(tensor=bvec_in.tensor, offset=l * NB, ap=[[0, 128], [1, 2560]])),
                writes=[r_bvt])
            with contextlib.ExitStack() as ls:
                wsf = sb("wsf", [128, 4, 128], F32, ls)
                wsm = sb("wsm", [128, 4, 128], F32, ls)
                lmt = sb("lmt", [128, 256], F32, ls)
                lpr = sb("lpr", [128, 128], F32, ls)
                lsm = sb("lsm", [128, 2], F32, ls)
                lex = sb("lex", [128, 2], F32, ls)
                r_wsf, r_lmt, r_l2, r_l3, r_l4 = reg("wsf"), reg("lmt"), reg("lpr"), reg("lsm"), reg("lex")
                dma(SP, ch_m, lambda: nc.sync.dma_start(out=wsf[:], in_=wsp_in[l, :, :, :]), writes=[r_wsf], local=True)
                dma(SP, ch_m, lambda: nc.sync.dma_start(
                    out=lmt[:], in_=bass.AP(tensor=bvec_in.tensor, offset=l * NB + B_LAM, ap=[[0, 128], [1, 256]])),
                    writes=[r_lmt], local=True)
                op(POOL, lambda: nc.gpsimd.affine_select(out=wsm[:], in_=wsf[:], pattern=[[0, 4], [1, 128]],
                                                         compare_op=ALU.is_ge, fill=0.0, base=0,
                                                         channel_multiplier=-1), reads=[r_wsf], writes=[r_l2])
                op(DVE, lambda: nc.vector.tensor_copy(wsT[:], wsm[:]), reads=[r_l2], writes=[r_wsT])
                lp3 = sb("lp3", [128, 2, 64], F32, ls)
                lyy = sb("lyy", [128, 2], F32, ls)
                r_l6 = reg("lyy")
                op(DVE, lambda: nc.vector.tensor_tensor(out=lp3[:, 0, :], in0=lmt[:, 0:64], in1=lmt[:, 64:128], op=ALU.mult),
                   reads=[r_lmt], writes=[r_l3])
                op(DVE, lambda: nc.vector.tensor_tensor(out=lp3[:, 1, :], in0=lmt[:, 128:192], in1=lmt[:, 192:256], op=ALU.mult),
                   reads=[r_lmt], writes=[r_l3])
                for w_ in (32, 16, 8, 4, 2, 1):
                    op(DVE, lambda w_=w_: nc.vector.tensor_tensor(out=lp3[:, :, 0:w_], in0=lp3[:, :, 0:w_], in1=lp3[:, :, w_:2 * w_], op=ALU.add),
                       reads=[], writes=[r_l3])
                lxa = sb("lxa", [128, 16], F32, ls)
                lxb = sb("lxb", [128, 16], F32, ls)
                r_lx = [reg("lxa"), reg("lxb")]
                lx = [lxa, lxb]
                step = [0]

                def chain(fn_dve, fn_pool, extra_reads):
                    i_ = step[0] % 2
                    src, dst = lx[i_], lx[1 - i_]
                    if step[0] % 2 == 0:
                        op(DVE, lambda: fn_dve(dst, src), reads=[r_lx[i_]] + extra_reads, writes=[r_lx[1 - i_]])
                    else:
                        op(POOL, lambda: fn_pool(dst, src), reads=[r_lx[i_]] + extra_reads, writes=[r_lx[1 - i_]])
                    step[0] += 1

                op(POOL, lambda: nc.gpsimd.tensor_scalar(lyy[:], lp3[:, :, 0], 1.0 / 64, None, op0=ALU.mult), reads=[r_l3], writes=[r_l6])
                op(DVE, lambda: nc.vector.tensor_scalar(lxa[:, 0:2], lyy[:], 0.2, 1.0, op0=ALU.mult, op1=ALU.add), reads=[r_l6], writes=[r_lx[0]])
                for cf in (0.25, 1.0 / 3.0, 0.5, 1.0):
                    chain(lambda d_, s_: nc.vector.tensor_tensor(out=d_[:, 0:2], in0=s_[:, 0:2], in1=lyy[:], op=ALU.mult),
                          lambda d_, s_: nc.gpsimd.tensor_tensor(out=d_[:, 0:2], in0=s_[:, 0:2], in1=lyy[:], op=ALU.mult), [r_l6])
                    chain(lambda d_, s_, cf=cf: nc.vector.tensor_scalar(d_[:, 0:2], s_[:, 0:2], cf, 1.0, op0=ALU.mult, op1=ALU.add),
                          lambda d_, s_, cf=cf: nc.gpsimd.tensor_scalar(d_[:, 0:2], s_[:, 0:2], cf, 1.0, op0=ALU.mult, op1=ALU.add), [])
                for _ in range(6):
                    chain(lambda d_, s_: nc.vector.tensor_tensor(out=d_[:, 0:2], in0=s_[:, 0:2], in1=s_[:, 0:2], op=ALU.mult),
                          lambda d_, s_: nc.gpsimd.tensor_tensor(out=d_[:, 0:2], in0=s_[:, 0:2], in1=s_[:, 0:2], op=ALU.mult), [])
                lexf = lx[step[0] % 2]
                r_lexf = r_lx[step[0] % 2]
                op(DVE, lambda: nc.vector.scalar_tensor_tensor(out=nlam_t[:, 0:1], in0=lexf[:, 1:2], scalar=-lam_init,
                                                               in1=lexf[:, 0:1], op0=ALU.add, op1=ALU.subtract),
                   reads=[r_lexf, r_drv], writes=[r_drv])
                op(POOL, lambda: nc.gpsimd.tensor_scalar(gsub_t[:, 0:1], pv[:, P_SUBG:P_SUBG + 1], 1.0 - lam_init, None,
                                                        op0=ALU.mult), reads=[r_pv, r_drv], writes=[r_drv])
                barrier()

            for s in range(NSEQ):
                with contextlib.ExitStack() as ls:
                    bgt = sb("bgt", [128, 2048], F32, ls)
                    onesl = sb("onesl", [128, 128], BF16, ls)
                    r_bgt = reg("bgt")
                    dma(SP, ch_m, lambda: nc.sync.dma_start(
                        out=bgt[:], in_=bass.AP(tensor=bvec_in.tensor, offset=l * NB + B_BG1, ap=[[0, 128], [1, 2048]])),
                        writes=[r_bgt], local=True)
                    for kc in range(8):
                        op(DVE, lambda kc=kc: nc.vector.tensor_scalar(crep[:, kc, :], onesb[:], cact[:, kc, s:s + 1], None,
                                                                      op0=ALU.mult),
                           reads=[r_const, r_cact], writes=[r_crep])
                    mb = next_bank()
                    for pi in range(12):
                        pan, rp = load_panel("w_ada", l, 0, 8, pi * 512, 512)
                        which = pi // 2
                        if which in (2, 5):
                            gb = next_bank()
                            while gb == mb:
                                gb = next_bank()
                            for kc in range(8):
                                op(PE, lambda kc=kc, gb=gb, pan=pan: nc.tensor.matmul(
                                    ps[gb][:], lhsT=crep[:, kc, :], rhs=pan[:, kc, :], start=(kc == 0), stop=(kc == 7)),
                                   reads=[r_crep, rp], writes=[r_ps[gb]])
                            gi = 0 if which == 2 else 1
                            half = pi % 2
                            op(DVE, lambda gb=gb, gi=gi, half=half: nc.vector.tensor_tensor(
                                out=gbc[:, gi, half * 512:(half + 1) * 512], in0=ps[gb][:],
                                in1=bgt[:, gi * 1024 + half * 512: gi * 1024 + (half + 1) * 512], op=ALU.add),
                               reads=[r_ps[gb], r_bgt], writes=[r_gbc[gi * 2 + half]])
                        else:
                            for cc in range(4):
                                j = pi * 4 + cc
                                for kc in range(8):
                                    op(PE, lambda kc=kc, j=j, cc=cc, pan=pan: nc.tensor.matmul(
                                        ps[mb][:, j:j + 1], lhsT=pan[:, kc, cc * 128:(cc + 1) * 128],
                                        rhs=cact[:, kc, s:s + 1], start=(kc == 0), stop=(kc == 7)),
                                       reads=[r_cact, rp], writes=[r_ps[mb]])
                    op(DVE, lambda: nc.vector.tensor_tensor(out=modsb[:, 0:16], in0=ps[mb][:, 0:16],
                                                            in1=pv[:, P_BADA:P_BADA + 16], op=ALU.add),
                       reads=[r_ps[mb], r_pv], writes=[r_mod])
                    op(DVE, lambda: nc.vector.tensor_tensor(out=modsb[:, 24:40], in0=ps[mb][:, 24:40],
                                                            in1=pv[:, P_BADA + 24:P_BADA + 40], op=ALU.add),
                       reads=[r_ps[mb], r_pv, r_mod], writes=[r_mod])
                    op(DVE, lambda: nc.vector.scalar_tensor_tensor(out=drv[:, 0:8], in0=modsb[:, 8:16], scalar=1.0,
                                                                   in1=pv[:, P_LN1G:P_LN1G + 8], op0=ALU.add, op1=ALU.mult),
                       reads=[r_mod, r_pv, r_drv], writes=[r_drv])
                    op(POOL, lambda: nc.gpsimd.tensor_copy(drv[:, 8:16], modsb[:, 0:8]), reads=[r_mod, r_drv], writes=[r_drv])
                    op(DVE, lambda: nc.vector.scalar_tensor_tensor(out=drv[:, 16:24], in0=modsb[:, 32:40], scalar=1.0,
                                                                   in1=pv[:, P_LN2G:P_LN2G + 8], op0=ALU.add, op1=ALU.mult),
                       reads=[r_mod, r_pv, r_drv], writes=[r_drv])
                    op(POOL, lambda: nc.gpsimd.tensor_copy(drv[:, 24:32], modsb[:, 24:32]), reads=[r_mod, r_drv], writes=[r_drv])
                    barrier()

                op(POOL, lambda: nc.gpsimd.memset(hal[:], 0.0), writes=[r_hal])
                op(POOL, lambda: nc.gpsimd.memset(cin[:, :, 0:30], 0.0), writes=r_cin)

                for t in range(NT):
                    row0 = s * S + t * T
                    key = (s, t)
                    if key not in r_xdram:
                        r_xdram[key] = _Reg()
                    xsrc = (x_in if l == 0 else out)[row0:row0 + T, :].rearrange("(ts p) d -> p ts d", p=128)
                    dma(SP, ch_x, lambda: nc.sync.dma_start(out=xt[:], in_=xsrc), reads=[r_xdram[key]], writes=r_xt)

                    norm_to_hT(0, 8, "a")
                    if DEBUG and l == 0 and s == 0 and t == 0:
                        dbg("hT", hT[:], [128, 8, T], BF16, r_hT)
                    with contextlib.ExitStack() as ls:
                        cosT = sb("cosT", [128, T], F32, ls)
                        sinT = sb("sinT", [128, T], F32, ls)
                        pti = sb("pti", [128, T], I32, ls)
                        ang = sb("ang", [128, T], F32, ls)
                        tk = sb("tk", [128, T], I32, ls)
                        tr = sb("tr", [128, T], F32, ls)
                        tm = sb("tm", [128, T], F32, ls)
                        tA = sb("tA", [128, T], F32, ls)
                        tB = sb("tB", [128, T], F32, ls)
                        uT = sb("uT", [128, 4, T], BF16, ls)
                        g1 = sb("g1", [128, T], F32, ls)
                        g2 = sb("g2", [128, T], F32, ls)
                        g3 = sb("g3", [128, T], F32, ls)
                        g4 = sb("g4", [128, T], F32, ls)
                        vvn = sb("vvn", [128, T], BF16, ls)
                        st6 = sb("st6", [128, 6], F32, ls)
                        mv = sb("mv", [128, 2], F32, ls)
                        mv2 = sb("mv2", [128, 2], F32, ls)
                        r_cos, r_sin, r_pti, r_ang, r_tk, r_tr, r_tm = (reg("cosT"), reg("sinT"), reg("pti"), reg("ang"),
                                                                         reg("tk"), reg("tr"), reg("tm"))
                        r_tA, r_tB = reg("tA"), reg("tB")
                        r_uT = regs("uT", 4)
                        r_g = regs("gtmp", 5)
                        r_vvn, r_st6, r_mv, r_mv2 = reg("vvn"), reg("st6"), reg("mv"), reg("mv2")

                        psrc = bass.AP(tensor=pos_in.tensor, offset=s * S + t * T, ap=[[0, 128], [1, T]])
                        dma(SP, ch_p, lambda: nc.sync.dma_start(out=pti[:], in_=psrc), writes=[r_pti], local=True)
                        op(DVE, lambda: nc.vector.tensor_scalar(ang[:], pti[:], cstt[:, 0:1], None, op0=ALU.mult),
                           reads=[r_pti, r_cst], writes=[r_ang])

                        def sin_table(dst, r_dst, shift, scale_ap):
                            src = ang
                            if shift != 0.0:
                                op(DVE, lambda: nc.vector.tensor_scalar(tm[:], ang[:], shift, None, op0=ALU.add),
                                   reads=[r_ang], writes=[r_tm])
                                src = tm
                            op(DVE, lambda: nc.vector.tensor_scalar(tk[:], src[:], 1.0 / TWO_PI, None, op0=ALU.mult),
                               reads=[r_ang, r_tm], writes=[r_tk])
                            op(DVE, lambda: nc.vector.scalar_tensor_tensor(out=tr[:], in0=tk[:], scalar=-TWO_PI, in1=src[:],
                                                                           op0=ALU.mult, op1=ALU.add),
                               reads=[r_tk, r_ang, r_tm], writes=[r_tr])
                            op(DVE, lambda: nc.vector.tensor_scalar(tm[:], tr[:], math.pi, TWO_PI, op0=ALU.is_gt, op1=ALU.mult),
                               reads=[r_tr], writes=[r_tm])
                            op(DVE, lambda: nc.vector.tensor_tensor(out=tr[:], in0=tr[:], in1=tm[:], op=ALU.subtract),
                               reads=[r_tm, r_tr], writes=[r_tr])
                            op(DVE, lambda: nc.vector.tensor_scalar(tr[:], tr[:], 3.1415925, -3.1415925, op0=ALU.min, op1=ALU.max),
                               reads=[r_tr], writes=[r_tr])
                            if scale_ap is None:
                                op(ACT, lambda: nc.scalar.activation(out=dst[:], in_=tr[:], func=AF.Sin),
                                   reads=[r_tr], writes=[r_dst])
                            else:
                                op(ACT, lambda: nc.scalar.activation(out=dst[:], in_=tr[:], func=AF.Sin, scale=scale_ap),
                                   reads=[r_tr, r_cst], writes=[r_dst])

                        sin_table(sinT, r_sin, 0.0, cstt[:, 1:2])
                        sin_table(cosT, r_cos, math.pi / 2.0, None)

                        def fm_chunk(pan, rp, cc):
                            b = next_bank()
                            for kc in range(8):
                                op(PE, lambda kc=kc, b=b: nc.tensor.matmul(
                                    ps[b][:], lhsT=pan[:, kc, cc * 128:(cc + 1) * 128], rhs=hT[:, kc, :],
                                    start=(kc == 0), stop=(kc == 7)),
                                   reads=[rp, r_hT[kc]], writes=[r_ps[b]])
                            return b

                        def tm_sub(pan, rp, ts):
                            b = next_bank()
                            for kc in range(8):
                                op(PE, lambda kc=kc, b=b: nc.tensor.matmul(
                                    ps[b][:], lhsT=hT[:, kc, ts * 128:(ts + 1) * 128], rhs=pan[:, kc, :],
                                    start=(kc == 0), stop=(kc == 7)),
                                   reads=[rp, r_hT[kc]], writes=[r_ps[b]])
                            return b

                        for which in range(2):
                            panA, rpA = load_panel("w_in", l, 0, 8, (2 * which) * 512, 512)
                            panB, rpB = load_panel("w_in", l, 0, 8, (2 * which + 1) * 512, 512)
                            for hd in range(4):
                                bA = fm_chunk(panA, rpA, hd)
                                bB = fm_chunk(panB, rpB, hd)
                                colA = P_BIN + (2 * which) * 4 + hd
                                colB = P_BIN + (2 * which + 1) * 4 + hd
                                op(DVE, lambda bA=bA, colA=colA: nc.vector.scalar_tensor_tensor(
                                    out=tA[:], in0=ps[bA][:], scalar=pv[:, colA:colA + 1], in1=cosT[:],
                                    op0=ALU.add, op1=ALU.mult), reads=[r_ps[bA], r_pv, r_cos], writes=[r_tA])
                                op(DVE, lambda bB=bB, colB=colB: nc.vector.scalar_tensor_tensor(
                                    out=tB[:], in0=ps[bB][:], scalar=pv[:, colB:colB + 1], in1=sinT[:],
                                    op0=ALU.add, op1=ALU.mult), reads=[r_ps[bB], r_pv, r_sin], writes=[r_tB])
                                if which == 0:
                                    op(POOL, lambda hd=hd: nc.gpsimd.tensor_tensor(out=qT[:, hd, :], in0=tA[:], in1=tB[:], op=ALU.add),
                                       reads=[r_tA, r_tB], writes=[r_qT[hd]])
                                else:
                                    op(POOL, lambda hd=hd: nc.gpsimd.tensor_tensor(out=KT[:, hd, t * T:(t + 1) * T], in0=tA[:],
                                                                                   in1=tB[:], op=ALU.add),
                                       reads=[r_tA, r_tB], writes=[r_KT[hd]])
                        pan, rp = load_panel("w_in", l, 0, 8, 4 * 512, 512)
                        for ts in range(4):
                            b = tm_sub(pan, rp, ts)
                            op(DVE, lambda b=b, ts=ts: nc.vector.tensor_tensor(
                                out=VC[:, t * 4 + ts, :], in0=ps[b][:], in1=bvt[:, B_BV:B_BV + 512], op=ALU.add),
                               reads=[r_ps[b], r_bvt], writes=[r_VC])
                        if DEBUG and l == 0 and s == 0 and t == 0:
                            dbg("qT", qT[:], [128, 4, T], BF16, r_qT)
                            dbg("KT", KT[:, :, 0:T], [128, 4, T], BF16, r_KT)
                            dbg("VC", VC[:, 0:4, :], [128, 4, 512], BF16, [r_VC])

                        panA, rpA = load_panel("w_in", l, 0, 8, 5 * 512, 512)
                        panG, rpG = load_panel("w_in", l, 0, 8, 6 * 512, 512)
                        for c in range(4):
                            bG = fm_chunk(panG, rpG, c)
                            op(ACT, lambda bG=bG, c=c: nc.scalar.activation(out=g1[:], in_=ps[bG][:], func=AF.Sigmoid,
                                                                            bias=pv[:, P_BIN + 24 + c:P_BIN + 25 + c]),
                               reads=[r_ps[bG], r_pv], writes=[r_g[0]])
                            bA = fm_chunk(panA, rpA, c)
                            op(DVE, lambda bA=bA, c=c: nc.vector.scalar_tensor_tensor(
                                out=cin[:, c, 30:542], in0=ps[bA][:], scalar=pv[:, P_BIN + 20 + c:P_BIN + 21 + c], in1=g1[:],
                                op0=ALU.add, op1=ALU.mult), reads=[r_ps[bA], r_pv, r_g[0]], writes=[r_cin[c]])
                        def gelu_tanh(dst_ap, r_dst, xin, r_x):
                            op(POOL, lambda: nc.gpsimd.tensor_tensor(out=g2[:], in0=xin[:], in1=xin[:], op=ALU.mult),
                               reads=[r_x], writes=[r_g[1]])
                            op(DVE, lambda: nc.vector.tensor_scalar(g2[:], g2[:], 0.044715, 1.0, op0=ALU.mult, op1=ALU.add),
                               reads=[r_g[1]], writes=[r_g[1]])
                            op(POOL, lambda: nc.gpsimd.tensor_tensor(out=g2[:], in0=g2[:], in1=xin[:], op=ALU.mult),
                               reads=[r_x, r_g[1]], writes=[r_g[1]])
                            op(ACT, lambda: nc.scalar.activation(out=g3[:], in_=g2[:], func=AF.Sigmoid, scale=1.5957691216057308),
                               reads=[r_g[1]], writes=[r_g[2]])
                            op(DVE, lambda: nc.vector.tensor_tensor(out=dst_ap, in0=xin[:], in1=g3[:], op=ALU.mult),
                               reads=[r_x, r_g[2]], writes=r_dst)

                        pan, rp = load_panel("w_in", l, 0, 8, 7 * 512, 512)
                        for c in range(4):
                            b = fm_chunk(pan, rp, c)
                            op(ACT, lambda b=b, c=c: nc.scalar.activation(out=g1[:], in_=ps[b][:], func=AF.Identity,
                                                                          bias=pv[:, P_BIN + 28 + c:P_BIN + 29 + c]),
                               reads=[r_ps[b], r_pv], writes=[r_g[0]])
                            gelu_tanh(uT[:, c, :], [r_uT[c]], g1, r_g[0])
                        pan, rp = load_panel("w_in", l, 0, 8, 8 * 512, 512)
                        bsg = [next_bank() for _ in range(4)]
                        for ts in range(4):
                            b = next_bank()
                            while b in bsg:
                                b = next_bank()
                            for kc in range(8):
                                op(PE, lambda kc=kc, b=b, ts=ts: nc.tensor.matmul(
                                    ps[b][:], lhsT=hT[:, kc, ts * 128:(ts + 1) * 128], rhs=pan[:, kc, :],
                                    start=(kc == 0), stop=(kc == 7)), reads=[rp, r_hT[kc]], writes=[r_ps[b]])
                            op(DVE, lambda b=b: nc.vector.tensor_tensor(out=g1[:], in0=ps[b][:], in1=bvt[:, B_BVV:B_BVV + 512], op=ALU.add),
                               reads=[r_ps[b], r_bvt], writes=[r_g[0]])
                            gelu_tanh(g4[:], [r_g[3]], g1, r_g[0])
                            op(DVE, lambda: nc.vector.bn_stats(st6[:], g4[:]), reads=[r_g[3]], writes=[r_st6])
                            op(DVE, lambda: nc.vector.bn_aggr(mv[:], st6[:]), reads=[r_st6], writes=[r_mv])
                            rsqrt_act(mv2[:, 0:1], mv[:, 1:2], 1.0, [r_mv], [r_mv2], mv2[:, 1:2], r_mv2)
                            op(DVE, lambda: nc.vector.tensor_scalar(g4[:], g4[:], mv[:, 0:1], mv2[:, 0:1], op0=ALU.subtract, op1=ALU.mult),
                               reads=[r_mv, r_mv2], writes=[r_g[3]])
                            op(POOL, lambda: nc.gpsimd.tensor_tensor(out=g4[:], in0=g4[:], in1=bvt[:, B_GG:B_GG + 512], op=ALU.mult),
                               reads=[r_bvt], writes=[r_g[3]])
                            op(DVE, lambda: nc.vector.tensor_tensor(out=vvn[:], in0=g4[:], in1=bvt[:, B_GB:B_GB + 512], op=ALU.add),
                               reads=[r_g[3], r_bvt], writes=[r_vvn])
                            for g in range(4):
                                op(PE, lambda g=g, ts=ts: nc.tensor.matmul(
                                    ps[bsg[g]][:, ts * 128:(ts + 1) * 128], lhsT=vvn[:, g * 128:(g + 1) * 128], rhs=wsT[:, g, :],
                                    start=True, stop=True), reads=[r_vvn, r_wsT], writes=[r_ps[bsg[g]]])
                        for g in range(4):
                            bview = bvt[:, B_BSP + g * 128:B_BSP + (g + 1) * 128]
                            for ts in range(4):
                                op(DVE, lambda g=g, ts=ts, bview=bview: nc.vector.tensor_tensor(
                                    out=g1[:, ts * 128:(ts + 1) * 128], in0=ps[bsg[g]][:, ts * 128:(ts + 1) * 128], in1=bview, op=ALU.add),
                                   reads=[r_ps[bsg[g]], r_bvt], writes=[r_g[0]])
                            op(POOL, lambda g=g: nc.gpsimd.tensor_tensor(out=gmT[:, g, :], in0=g1[:], in1=uT[:, g, :], op=ALU.mult),
                               reads=[r_g[0], r_uT[g]], writes=[r_gm[g]])
                        if DEBUG and l == 0 and s == 0 and t == 0:
                            dbg("gmT", gmT[:], [128, 4, T], BF16, r_gm)
                        barrier()

                    with contextlib.ExitStack() as ls:
                        ptg = [sb(f"ptg{i}", [128, 4, T], BF16, ls) for i in range(2)]
                        r_ptg = regs("ptg", 2)
                        a1 = sb("a1", [128, T], F32, ls)
                        a2 = sb("a2", [128, T], F32, ls)
                        a3 = sb("a3", [128, T], F32, ls)
                        a4 = sb("a4", [128, T], F32, ls)
                        a5 = sb("a5", [128, T], BF16, ls)
                        a3s = sb("a3s", [128, 4, T], F32, ls)
                        r_a = regs("atmp", 5)
                        r_a3s = regs("a3s", 4)
                        cout = sb("cout", [128, 4, T], F32, ls)
                        g2 = sb("g2b", [128, T], F32, ls)
                        g3 = sb("g3b", [128, T], F32, ls)
                        g4 = sb("g4b", [128, T], F32, ls)
                        g1 = sb("g1b", [128, T], F32, ls)
                        hcb = sb("hcb", [128, T], BF16, ls)
                        sqb = sb("sqb", [128, T], BF16, ls)
                        r_cout = regs("cout", 4)
                        r_g = regs("gtmpb", 5)
                        r_hcb, r_sqb = reg("hcb"), reg("sqb")
                        nkt = 4 * (t + 1)
                        ngrp = nkt // 2
                        for hd in range(4):
                            bo1, bo2, bs1, bs2 = 4, 5, 6, 7

                            def scores(kt, hd=hd):
                                for m in range(2):
                                    bk = (kt % 2) * 2 + m
                                    op(PE, lambda m=m, bk=bk, kt=kt: nc.tensor.matmul(
                                        ps[bk][:, 0:T], lhsT=KT[m * 64:(m + 1) * 64, hd, kt * 128:(kt + 1) * 128],
                                        rhs=qT[m * 64:(m + 1) * 64, hd, 0:T], start=True, stop=True),
                                       reads=[r_KT[hd], r_qT[hd]], writes=[r_ps[bk]])

                            def exps(kt):
                                b0 = (kt % 2) * 2
                                op(ACT, lambda kt=kt, b0=b0: nc.scalar.activation(out=ptg[kt % 2][:, 0:2, :], in_=psall[:, b0:b0 + 2, :],
                                                                               func=AF.Exp, scale=0.125),
                                   reads=[r_ps[b0], r_ps[b0 + 1]], writes=[r_ptg[kt % 2]])

                            def pvs(kt, hd=hd):
                                j = kt - 4 * t
                                c0 = 128 * j if j > 0 else 0
                                first = (kt == 0)
                                last = (kt == nkt - 1)
                                rp_ = r_ptg[kt % 2]
                                if j >= 0:
                                    op(POOL, lambda kt=kt, c0=c0: nc.gpsimd.memset(ptg[kt % 2][64:128, 0:2, c0:c0 + 64], 0.0),
                                       reads=[], writes=[rp_])
                                for m in range(2):
                                    p_ = ptg[kt % 2][:, m, :]
                                    bo = bo1 if m == 0 else bo2
                                    bs_ = bs1 if m == 0 else bs2
                                    for (dst_b, lw_full) in ((bo, VC[:, kt, hd * 128:(hd + 1) * 128]), (bs_, onesb[:])):
                                        op(PE, lambda dst_b=dst_b, lw_full=lw_full, p_=p_, c0=c0, first=first, last=last: nc.tensor.matmul(
                                            ps[dst_b][:, c0:T], lhsT=lw_full, rhs=p_[:, c0:T],
                                            start=first, stop=last, skip_group_check=True),
                                           reads=[r_VC, r_const, rp_], writes=[r_ps[dst_b]])

                            scores(0)
                            exps(0)
                            for kt in range(nkt):
                                if kt + 1 < nkt:
                                    scores(kt + 1)
                                    exps(kt + 1)
                                pvs(kt)
                            c = hd
                            op(DVE, lambda c=c: nc.vector.tensor_scalar(
                                cout[:, c, :], cin[:, c, 0:512], pv[:, P_CW + c:P_CW + c + 1], pv[:, P_CB + c:P_CB + c + 1],
                                op0=ALU.mult, op1=ALU.add), reads=[r_cin[c], r_pv], writes=[r_cout[c]])
                            op(DVE, lambda c=c: nc.vector.tensor_scalar(
                                g2[:], cin[:, c, 1:513], pv[:, P_CW + 4 + c:P_CW + 4 + c + 1], None, op0=ALU.mult),
                               reads=[r_cin[c], r_pv], writes=[r_g[1]])
                            for k in range(2, 31):
                                if k % 2 == 0:
                                    op(DVE, lambda c=c, k=k: nc.vector.scalar_tensor_tensor(
                                        out=cout[:, c, :], in0=cin[:, c, k:k + 512], scalar=pv[:, P_CW + k * 4 + c:P_CW + k * 4 + c + 1],
                                        in1=cout[:, c, :], op0=ALU.mult, op1=ALU.add),
                                       reads=[r_cin[c], r_pv], writes=[r_cout[c]])
                                else:
                                    op(DVE, lambda c=c, k=k: nc.vector.scalar_tensor_tensor(
                                        out=g2[:], in0=cin[:, c, k:k + 512], scalar=pv[:, P_CW + k * 4 + c:P_CW + k * 4 + c + 1],
                                        in1=g2[:], op0=ALU.mult, op1=ALU.add),
                                       reads=[r_cin[c], r_pv], writes=[r_g[1]])
                            op(DVE, lambda c=c: nc.vector.tensor_tensor(out=cout[:, c, :], in0=cout[:, c, :], in1=g2[:], op=ALU.add),
                               reads=[r_g[1]], writes=[r_cout[c]])
                            op(POOL, lambda c=c: nc.gpsimd.tensor_copy(cin[:, c, 0:30], cin[:, c, 512:542]),
                               reads=[], writes=[r_cin[c]])
                            op(ACT, lambda: nc.scalar.activation(out=a1[:], in_=ps[bs1][:], func=AF.Ln), reads=[r_ps[bs1]], writes=[r_a[0]])
                            op(ACT, lambda: nc.scalar.activation(out=a2[:], in_=ps[bs2][:], func=AF.Ln), reads=[r_ps[bs2]], writes=[r_a[1]])
                            op(ACT, lambda: nc.scalar.activation(out=a1[:], in_=a1[:], func=AF.Exp, scale=-1.0), reads=[], writes=[r_a[0]])
                            op(ACT, lambda: nc.scalar.activation(out=a2[:], in_=a2[:], func=AF.Exp, scale=-1.0), reads=[], writes=[r_a[1]])
                            op(DVE, lambda: nc.vector.tensor_tensor(out=a1[:], in0=ps[bo1][:], in1=a1[:], op=ALU.mult),
                               reads=[r_ps[bo1]], writes=[r_a[0]])
                            op(DVE, lambda: nc.vector.tensor_tensor(out=a2[:], in0=ps[bo2][:], in1=a2[:], op=ALU.mult),
                               reads=[r_ps[bo2]], writes=[r_a[1]])
                            op(DVE, lambda hd=hd: nc.vector.scalar_tensor_tensor(out=a3s[:, hd, :], in0=a2[:], scalar=nlam_t[:, 0:1], in1=a1[:],
                                                                                 op0=ALU.mult, op1=ALU.add),
                               reads=[r_a[0], r_a[1], r_drv], writes=[r_a3s[hd]])
                            if DEBUG and l == 0 and s == 0 and t == 0 and hd == 0:
                                dbg("a1", a1[:], [128, T], F32, [r_a[0]])
                                dbg("a2", a2[:], [128, T], F32, [r_a[1]])
                        for hd in range(4):
                            op(POOL, lambda hd=hd: nc.gpsimd.tensor_tensor(out=a5[:], in0=a3s[:, hd, :], in1=a3s[:, hd, :], op=ALU.mult),
                               reads=[r_a3s[hd]], writes=[r_a[4]])
                            bq = hd % 4
                            op(PE, lambda bq=bq: nc.tensor.matmul(ps[bq][:], lhsT=onesb[:], rhs=a5[:], start=True, stop=True),
                               reads=[r_a[4], r_const], writes=[r_ps[bq]])
                            rsqrt_act(a4[:], ps[bq][:], 1.0 / 128, [r_ps[bq]], [r_a[3]], a1[:], r_a[0])
                            op(DVE, lambda hd=hd: nc.vector.scalar_tensor_tensor(out=oT[:, hd, :], in0=a3s[:, hd, :], scalar=gsub_t[:, 0:1],
                                                                                 in1=a4[:], op0=ALU.mult, op1=ALU.mult),
                               reads=[r_a3s[hd], r_a[3], r_drv], writes=[r_oT[hd]])
                        bS, bQ = 4, 5
                        for c in range(4):
                            op(ACT, lambda c=c: nc.scalar.activation(out=hcb[:], in_=cout[:, c, :], func=AF.Copy),
                               reads=[r_cout[c]], writes=[r_hcb])
                            op(ACT, lambda c=c: nc.scalar.activation(out=sqb[:], in_=cout[:, c, :], func=AF.Square),
                               reads=[r_cout[c]], writes=[r_sqb])
                            op(PE, lambda c=c: nc.tensor.matmul(ps[bS][:], lhsT=onesb[:], rhs=hcb[:], start=(c == 0), stop=(c == 3)),
                               reads=[r_hcb, r_const], writes=[r_ps[bS]])
                            op(PE, lambda c=c: nc.tensor.matmul(ps[bQ][:], lhsT=onesb[:], rhs=sqb[:], start=(c == 0), stop=(c == 3)),
                               reads=[r_sqb, r_const], writes=[r_ps[bQ]])
                        op(DVE, lambda: nc.vector.tensor_scalar(g2[:], ps[bS][:], 1.0 / 512, None, op0=ALU.mult),
                           reads=[r_ps[bS]], writes=[r_g[1]])
                        op(DVE, lambda: nc.vector.tensor_tensor(out=g3[:], in0=g2[:], in1=g2[:], op=ALU.mult),
                           reads=[r_g[1]], writes=[r_g[2]])
                        op(DVE, lambda: nc.vector.scalar_tensor_tensor(out=g3[:], in0=ps[bQ][:], scalar=1.0 / 512, in1=g3[:],
                                                                       op0=ALU.mult, op1=ALU.subtract),
                           reads=[r_ps[bQ], r_g[2]], writes=[r_g[2]])
                        op(DVE, lambda: nc.vector.tensor_scalar(g3[:], g3[:], 0.0, None, op0=ALU.max),
                           reads=[r_g[2]], writes=[r_g[2]])
                        rsqrt_act(g4[:], g3[:], 1.0, [r_g[2]], [r_g[3]], g1[:], r_g[0])
                        for c in range(4):
                            op(DVE, lambda c=c: nc.vector.tensor_tensor(out=cout[:, c, :], in0=cout[:, c, :], in1=g2[:], op=ALU.subtract),
                               reads=[r_g[1]], writes=[r_cout[c]])
                            op(POOL, lambda c=c: nc.gpsimd.tensor_tensor(out=cout[:, c, :], in0=cout[:, c, :], in1=g4[:], op=ALU.mult),
                               reads=[r_g[3]], writes=[r_cout[c]])
                            op(ACT, lambda c=c: nc.scalar.activation(out=hc2[:, c, :], in_=cout[:, c, :], func=AF.Silu,
                                                                     scale=pv[:, P_CLG + c:P_CLG + c + 1],
                                                                     bias=pv[:, P_CLB + c:P_CLB + c + 1]),
                               reads=[r_cout[c], r_pv], writes=[r_hc2[c]])
                        if DEBUG and l == 0 and s == 0 and t == 0:
                            dbg("oT", oT[:], [128, 4, T], BF16, r_oT)
                            dbg("hc2", hc2[:], [128, 4, T], BF16, r_hc2)
                        barrier()

                    with contextlib.ExitStack() as ls:
                        mixf = sb("mixf", [128, 8, T], F32, ls)
                        mixb = sb("mixb", [128, 8, T], BF16, ls)
                        gt4 = [sb(f"gt4_{i}", [128, 4, T], BF16, ls) for i in range(2)]
                        tmpc = sb("tmpc", [128, T], F32, ls)
                        r_mixf = regs("mixf", 8)
                        r_mixb = regs("mixb", 8)
                        r_gt4 = regs("gt4", 2)
                        r_tmpc = reg("tmpc")
                        srcs = [(oT, r_oT, "w_att"), (hc2, r_hc2, "w_conv"), (gmT, r_gm, "w_gmlp")]
                        gi = 0
                        for bi, (srcT, r_src, wn) in enumerate(srcs):
                            wo_, rwo = load_panel(wn, l, 0, 4, 0, D)
                            for half in range(2):
                                gp, rgp = load_panel("w_in", l, 0, 8, (9 + bi * 2 + half) * 512, 512)
                                gtile = gt4[gi % 2]
                                rg = r_gt4[gi % 2]
                                gi += 1
                                for cc in range(4):
                                    b = next_bank()
                                    for kc in range(8):
                                        op(PE, lambda kc=kc, b=b, cc=cc, gp=gp: nc.tensor.matmul(
                                            ps[b][:], lhsT=gp[:, kc, cc * 128:(cc + 1) * 128], rhs=hT[:, kc, :],
                                            start=(kc == 0), stop=(kc == 7)), reads=[rgp, r_hT[kc]], writes=[r_ps[b]])
                                    col = P_BIN + (9 + bi * 2 + half) * 4 + cc
                                    op(ACT, lambda b=b, cc=cc, col=col, gtile=gtile: nc.scalar.activation(
                                        out=gtile[:, cc, :], in_=ps[b][:], func=AF.Sigmoid, bias=pv[:, col:col + 1]),
                                       reads=[r_ps[b], r_pv], writes=[rg])
                                for cc in range(4):
                                    dc = half * 4 + cc
                                    b = next_bank()
                                    for kc in range(4):
                                        op(PE, lambda kc=kc, b=b, dc=dc, wo_=wo_, srcT=srcT: nc.tensor.matmul(
                                            ps[b][:], lhsT=wo_[:, kc, dc * 128:(dc + 1) * 128], rhs=srcT[:, kc, :],
                                            start=(kc == 0), stop=(kc == 3)), reads=[rwo, r_src[kc]], writes=[r_ps[b]])
                                    if bi == 0:
                                        op(DVE, lambda b=b, dc=dc, cc=cc, gtile=gtile: nc.vector.tensor_tensor(
                                            out=mixf[:, dc, :], in0=ps[b][:], in1=gtile[:, cc, :], op=ALU.mult),
                                           reads=[r_ps[b], rg], writes=[r_mixf[dc]])
                                    else:
                                        op(DVE, lambda b=b, cc=cc, gtile=gtile: nc.vector.tensor_tensor(
                                            out=tmpc[:], in0=ps[b][:], in1=gtile[:, cc, :], op=ALU.mult),
                                           reads=[r_ps[b], rg], writes=[r_tmpc])
                                        if bi == 1:
                                            op(POOL, lambda dc=dc: nc.gpsimd.tensor_tensor(
                                                out=mixf[:, dc, :], in0=mixf[:, dc, :], in1=tmpc[:], op=ALU.add),
                                               reads=[r_tmpc], writes=[r_mixf[dc]])
                                        else:
                                            op(POOL, lambda dc=dc: nc.gpsimd.tensor_tensor(
                                                out=mixb[:, dc, :], in0=mixf[:, dc, :], in1=tmpc[:], op=ALU.add),
                                               reads=[r_tmpc, r_mixf[dc]], writes=[r_mixb[dc]])
                        if DEBUG and l == 0 and s == 0 and t == 0:
                            dbg("mixb", mixb[:], [128, 8, T], BF16, r_mixb)
                        for nh in range(2):
                            wo_, rwo = load_panel("w_o", l, 0, 8, nh * 512, 512)
                            for ts in range(4):
                                b = next_bank()
                                for kc in range(8):
                                    op(PE, lambda kc=kc, b=b, ts=ts, wo_=wo_: nc.tensor.matmul(
                                        ps[b][:], lhsT=mixb[:, kc, ts * 128:(ts + 1) * 128], rhs=wo_[:, kc, :],
                                        start=(kc == 0), stop=(kc == 7)), reads=[rwo, r_mixb[kc]], writes=[r_ps[b]])
                                op(DVE, lambda b=b, nh=nh: nc.vector.tensor_tensor(
                                    out=tmpc[:], in0=ps[b][:], in1=gbc[:, 0, nh * 512:(nh + 1) * 512], op=ALU.mult),
                                   reads=[r_ps[b], r_gbc[nh]], writes=[r_tmpc])
                                op(POOL, lambda ts=ts, nh=nh: nc.gpsimd.tensor_tensor(
                                    out=xt[:, ts, nh * 512:(nh + 1) * 512], in0=xt[:, ts, nh * 512:(nh + 1) * 512], in1=tmpc[:], op=ALU.add),
                                   reads=[r_tmpc], writes=[r_xt[ts]])
                        if DEBUG and l == 0 and s == 0 and t == 0:
                            dbg("xmid", xt[:], [128, 4, D], F32, r_xt)
                        barrier()

                    norm_to_hT(16, 24, "d")
                    with contextlib.ExitStack() as ls:
                        actT = sb("actT", [128, 22, T], BF16, ls)
                        stg = [sb(f"stg{i}", [128, 514], F32, ls) for i in range(4)]
                        ft = [sb(f"ft{i}", [128, T], F32, ls) for i in range(4)]
                        r_act = regs("actT", 22)
                        r_stg = regs("stg", 4)
                        r_ft = regs("ft", 4)
                        pending = []
                        pair_i = [0]
                        for pi in range(11):
                            pan, rp = load_panel("w_up", l, 0, 8, pi * 512, 512)
                            for pr in range(2):
                                info = []
                                for q_ in range(2):
                                    cc = pr * 2 + q_
                                    j = pi * 4 + cc
                                    b = next_bank()
                                    for kc in range(8):
                                        op(PE, lambda kc=kc, b=b, cc=cc, pan=pan: nc.tensor.matmul(
                                            ps[b][:], lhsT=pan[:, kc, cc * 128:(cc + 1) * 128], rhs=hT[:, kc, :],
                                            start=(kc == 0), stop=(kc == 7)), reads=[rp, r_hT[kc]], writes=[r_ps[b]])
                                    bi_ = (pair_i[0] % 2) * 2 + q_
                                    sg_, rs_, f_, rf_ = stg[bi_], r_stg[bi_], ft[bi_], r_ft[bi_]
                                    op(POOL, lambda sg_=sg_, j=j: nc.gpsimd.tensor_copy(sg_[:, 0:2], hal[:, j, :]),
                                       reads=[r_hal], writes=[rs_])
                                    op(ACT, lambda sg_=sg_, b=b: nc.scalar.activation(out=sg_[:, 2:514], in_=ps[b][:], func=AF.Copy),
                                       reads=[r_ps[b]], writes=[rs_])
                                    op(POOL, lambda sg_=sg_, j=j: nc.gpsimd.tensor_copy(hal[:, j, :], sg_[:, 512:514]),
                                       reads=[rs_], writes=[r_hal])
                                    info.append((j, sg_, rs_, f_, rf_))
                                for (j, sg_, rs_, f_, rf_) in info:
                                    op(DVE, lambda sg_=sg_, f_=f_, j=j: nc.vector.tensor_scalar(
                                        f_[:], sg_[:, 2:514], pv[:, P_FW + 2 * 44 + j:P_FW + 2 * 44 + j + 1], pv[:, P_FB + j:P_FB + j + 1],
                                        op0=ALU.mult, op1=ALU.add), reads=[rs_, r_pv], writes=[rf_])
                                for (j, sg_, rs_, f_, rf_) in info:
                                    op(DVE, lambda sg_=sg_, f_=f_, j=j: nc.vector.scalar_tensor_tensor(
                                        out=f_[:], in0=sg_[:, 1:513], scalar=pv[:, P_FW + 44 + j:P_FW + 44 + j + 1], in1=f_[:],
                                        op0=ALU.mult, op1=ALU.add), reads=[rs_, r_pv], writes=[rf_])
                                for (j, sg_, rs_, f_, rf_) in info:
                                    op(DVE, lambda sg_=sg_, f_=f_, j=j: nc.vector.scalar_tensor_tensor(
                                        out=f_[:], in0=sg_[:, 0:512], scalar=pv[:, P_FW + j:P_FW + j + 1], in1=f_[:],
                                        op0=ALU.mult, op1=ALU.add), reads=[rs_, r_pv], writes=[rf_])
                                pair_i[0] += 1

                                def finals(items):
                                    for (j, sg_, rs_, f_, rf_) in items:
                                        if j < 22:
                                            op(ACT, lambda f_=f_, j=j: nc.scalar.activation(out=actT[:, j, :], in_=f_[:], func=AF.Silu),
                                               reads=[rf_], writes=[r_act[j]])
                                        else:
                                            op(POOL, lambda f_=f_, j=j: nc.gpsimd.tensor_tensor(
                                                out=actT[:, j - 22, :], in0=actT[:, j - 22, :], in1=f_[:], op=ALU.mult),
                                               reads=[rf_], writes=[r_act[j - 22]])

                                if pending:
                                    finals(pending.pop())
                                pending.append(info)
                        if pending:
                            finals(pending.pop())
                        if DEBUG and l == 0 and s == 0 and t == 0:
                            dbg("actT", actT[:], [128, 22, T], BF16, r_act)
                        for pi in range(11):
                            pan, rp = load_panel("w_down", l, 2 * pi, 2, 0, D)
                            for kk in range(2):
                                kc = 2 * pi + kk
                                for ts in range(4):
                                    for nh in range(2):
                                        b = ts * 2 + nh
                                        op(PE, lambda kc=kc, kk=kk, b=b, ts=ts, nh=nh, pan=pan: nc.tensor.matmul(
                                            ps[b][:], lhsT=actT[:, kc, ts * 128:(ts + 1) * 128], rhs=pan[:, kk, nh * 512:(nh + 1) * 512],
                                            start=(kc == 0), stop=(kc == 21), skip_group_check=True),
                                           reads=[rp, r_act[kc]], writes=[r_ps[b]])
                        for ts in range(4):
                            for nh in range(2):
                                b = ts * 2 + nh
                                f_ = ft[b % 2]
                                rf_ = r_ft[b % 2]
                                op(DVE, lambda b=b, nh=nh, f_=f_: nc.vector.tensor_tensor(
                                    out=f_[:], in0=ps[b][:], in1=gbc[:, 1, nh * 512:(nh + 1) * 512], op=ALU.mult),
                                   reads=[r_ps[b], r_gbc[2 + nh]], writes=[rf_])
                                op(POOL, lambda ts=ts, nh=nh, f_=f_: nc.gpsimd.tensor_tensor(
                                    out=xt[:, ts, nh * 512:(nh + 1) * 512], in0=xt[:, ts, nh * 512:(nh + 1) * 512], in1=f_[:], op=ALU.add),
                                   reads=[rf_], writes=[r_xt[ts]])
                        barrier()

                    if l == NL - 1:
                        with contextlib.ExitStack() as ls:
                            junk = sb("fjunk", [128, D], BF16, ls)
                            ss = sb("fss", [128, 4], F32, ls)
                            rs = sb("frs", [128, 4], F32, ls)
                            lt = sb("flt", [128, 4], F32, ls)
                            r_j, r_s1, r_s2, r_s3 = reg("fjunk"), reg("fss"), reg("frs"), reg("flt")
                            for ts in range(4):
                                op(ACT, lambda ts=ts: nc.scalar.activation(out=junk[:], in_=xt[:, ts, :], func=AF.Square,
                                                                            accum_out=ss[:, ts:ts + 1]),
                                   reads=[r_xt[ts]], writes=[r_j, r_s1])
                            rsqrt_act(rs[:], ss[:], 1.0 / D, [r_s1], [r_s2], lt[:], r_s3)
                            for ts in range(4):
                                op(DVE, lambda ts=ts: nc.vector.scalar_tensor_tensor(
                                    out=xt[:, ts, :], in0=xt[:, ts, :], scalar=rs[:, ts:ts + 1], in1=fgt[:],
                                    op0=ALU.mult, op1=ALU.mult), reads=[r_s2, r_fgt], writes=[r_xt[ts]])
                            barrier()
                    xdst = out[row0:row0 + T, :].rearrange("(ts p) d -> p ts d", p=128)
                    dma(POOL, ch_xs, lambda: nc.gpsimd.dma_start(out=xdst, in_=xt[:]), reads=r_xt, writes=[r_xdram[key]])

        nc.gpsimd.wait_ge(ch_xs.sem, ch_xs.count)
        nc.sync.wait_ge(ch_xs.sem, ch_xs.count)
    return nc, dbg_outs


def _prep_shared(inp):
    f = lambda a: np.ascontiguousarray(np.asarray(a, dtype=np.float32))
    w_in = f(inp["w_in"])
    b_in = f(inp["b_in"])
    perm = np.arange(512).reshape(4, 2, 64)
    perm = np.concatenate([perm[:, :, 32:], perm[:, :, :32]], axis=2).reshape(512)
    segs = [np.arange(0, 512), perm, 512 + np.arange(512), 512 + perm, 1024 + np.arange(512),
            1536 + np.arange(1024), 2560 + np.arange(1024), 3584 + np.arange(3072)]
    cols = np.concatenate(segs)
    assert cols.shape[0] == NEXT
    w_in_e = np.ascontiguousarray(w_in[:, :, cols])
    b_in_e = b_in[:, cols]

    def fm(v):
        Ln, n = v.shape
        return v.reshape(Ln, n // 128, 128).transpose(0, 2, 1)

    pvec = np.zeros((L, 128, NV), np.float32)
    pvec[:, :, P_BIN:P_BIN + 60] = fm(b_in_e)
    pvec[:, :, P_LN1G:P_LN1G + 8] = fm(f(inp["ln1_g"]))
    pvec[:, :, P_LN2G:P_LN2G + 8] = fm(f(inp["ln2_g"]))
    cw = f(inp["conv_dw_w"])
    pvec[:, :, P_CW:P_CW + 124] = cw.reshape(L, 31, 4, 128).transpose(0, 3, 1, 2).reshape(L, 128, 124)
    pvec[:, :, P_CB:P_CB + 4] = fm(f(inp["conv_dw_b"]))
    pvec[:, :, P_CLG:P_CLG + 4] = fm(f(inp["conv_ln_g"]))
    pvec[:, :, P_CLB:P_CLB + 4] = fm(f(inp["conv_ln_b"]))
    pvec[:, :, P_SUBG:P_SUBG + 1] = fm(f(inp["attn_subln_g"]))
    fw = f(inp["ffn_dw_w"])
    pvec[:, :, P_FW:P_FW + 132] = fw.reshape(L, 3, 44, 128).transpose(0, 3, 1, 2).reshape(L, 128, 132)
    pvec[:, :, P_FB:P_FB + 44] = fm(f(inp["ffn_dw_b"]))
    b_ada = f(inp["b_ada"])
    pvec[:, :, P_BADA:P_BADA + 48] = fm(b_ada)

    bvec = np.zeros((L, NB), np.float32)
    bvec[:, B_BV:B_BV + 512] = b_in[:, 1024:1536]
    bvec[:, B_BVV:B_BVV + 512] = b_in[:, 3072:3584]
    bvec[:, B_GG:B_GG + 512] = f(inp["gmlp_ln_g"])
    bvec[:, B_GB:B_GB + 512] = f(inp["gmlp_ln_b"])
    bvec[:, B_BSP:B_BSP + 512] = f(inp["b_spatial"]).reshape(L, 512)
    bvec[:, B_BG1:B_BG1 + 1024] = b_ada[:, 2048:3072]
    bvec[:, B_BG2:B_BG2 + 1024] = b_ada[:, 5120:6144]
    bvec[:, B_LAM:B_LAM + 64] = f(inp["lambda_q1"])
    bvec[:, B_LAM + 64:B_LAM + 128] = f(inp["lambda_k1"])
    bvec[:, B_LAM + 128:B_LAM + 192] = f(inp["lambda_q2"])
    bvec[:, B_LAM + 192:B_LAM + 256] = f(inp["lambda_k2"])

    inv_freq = (1.0 / (10000.0 ** (np.arange(0, 64, 2, dtype=np.float32) / 64.0))).astype(np.float32)
    cst = np.zeros((128, 4), np.float32)
    d = np.arange(128) % 64
    cst[:, 0] = inv_freq[d % 32]
    cst[:, 1] = np.where(d < 32, -1.0, 1.0)

    shared = {
        "cst": cst, "pvec": pvec, "bvec": bvec, "fing": f(inp["final_g"]).reshape(1, D),
        "wspT": np.ascontiguousarray(f(inp["w_spatial"]).transpose(0, 3, 1, 2)),
        "w_in": w_in_e, "w_ada": f(inp["w_ada"]), "w_att": f(inp["w_attn_out"]), "w_conv": f(inp["w_conv_out"]),
        "w_gmlp": f(inp["w_gmlp_out"]), "w_o": f(inp["w_o"]), "w_up": f(inp["w_up"]), "w_down": f(inp["w_down"]),
    }
    return shared


def _in_maps(inp, shared):
    x = np.asarray(inp["x"], dtype=np.float32)
    c = np.asarray(inp["c"], dtype=np.float32)
    pos = np.asarray(inp["positions"], dtype=np.int32)
    maps = []
    for core in range(8):
        b0 = 2 * core
        m = dict(shared)
        m["x"] = np.ascontiguousarray(x[b0:b0 + 2].reshape(2 * S, D))
        m["cT"] = np.ascontiguousarray(c[b0:b0 + 2].reshape(2, 8, 128).transpose(2, 1, 0))
        m["pos"] = np.ascontiguousarray(pos[b0:b0 + 2])
        maps.append(m)
    return maps


def kernel(**inputs):
    shared = _prep_shared(inputs)
    maps = _in_maps(inputs, shared)
    nc, _ = build_nc()
    res = run_bass_kernel_spmd(nc, maps, core_ids=list(range(8)))
    outs = [np.asarray(r["out"], dtype=np.float32).reshape(2, S, D) for r in res.results]
    return np.concatenate(outs, axis=0)
```

```python
import math
import contextlib
import numpy as np
import concourse.bass as bass
import concourse.mybir as mybir
from concourse.bass_utils import run_bass_kernel_spmd

F32 = mybir.dt.float32
BF16 = mybir.dt.bfloat16
I32 = mybir.dt.int32
AF = mybir.ActivationFunctionType
ALU = mybir.AluOpType

D = 1024
S = 4096
L = 4
T = 512
FF = 2816
NEXT = 7680
EPS = 1e-6
TWO_PI = 2.0 * math.pi

P_BIN = 0
P_LN1G = 60
P_LN2G = 68
P_CW = 76
P_CB = 200
P_CLG = 204
P_CLB = 208
P_SUBG = 212
P_FW = 213
P_FB = 345
P_BADA = 389
NV = 437
B_BV = 0
B_BVV = 512
B_GG = 1024
B_GB = 1536
B_BSP = 2048
B_BG1 = 2560
B_BG2 = 3584
B_LAM = 4608
NB = 4864


class _ES:
    def __init__(self, name, eng, sem, inc):
        self.name, self.eng, self.sem, self.inc = name, eng, sem, inc
        self.count = 0
        self.seen = {}


class _Reg:
    __slots__ = ("w", "r")

    def __init__(self):
        self.w = None
        self.r = {}


def build_nc(NL=L, NSEQ=2, NT=8, DEBUG=False, LW=L):
    nc = bass.Bass("TRN2", target_bir_lowering=False)

    def din(name, shape, dt=F32):
        return nc.dram_tensor(name, list(shape), dt, kind="ExternalInput").ap()

    def dscr(name, shape, dt=BF16):
        return nc.dram_tensor(name, list(shape), dt, kind="Internal").ap()

    x_in = din("x", [2 * S, D])
    cT_in = din("cT", [128, 8, 2])
    pos_in = din("pos", [2, S], I32)
    cst_in = din("cst", [128, 4])
    pvec_in = din("pvec", [LW, 128, NV])
    bvec_in = din("bvec", [LW, NB])
    fing_in = din("fing", [1, D])
    wsp_in = din("wspT", [LW, 128, 4, 128])
    wnames = [("w_in", D, NEXT), ("w_ada", D, 6144), ("w_att", 512, D), ("w_conv", 512, D),
              ("w_gmlp", 512, D), ("w_o", D, D), ("w_up", D, 2 * FF), ("w_down", FF, D)]
    wf = {}
    wb = {}
    for nm, r, c in wnames:
        wf[nm] = din(nm, [LW, r, c])
        wb[nm] = dscr(nm + "_b", [LW, r, c])
    out = nc.dram_tensor("out", [2 * S, D], F32, kind="ExternalOutput").ap()
    dbg_outs = {}

    es = contextlib.ExitStack()
    with es:
        def sem(n):
            return es.enter_context(nc.semaphore(n))

        PE = _ES("pe", nc.tensor, sem("s_pe"), 1)
        ACT = _ES("act", nc.scalar, sem("s_act"), 1)
        DVE = _ES("dve", nc.vector, sem("s_dve"), 1)
        POOL = _ES("pool", nc.gpsimd, sem("s_pool"), 1)
        SP = _ES("sp", nc.sync, sem("s_sp"), 1)
        COMPUTE = [PE, ACT, DVE, POOL]

        def chan(n):
            return _ES(n, None, sem("c_" + n), 16)

        def _deps(reads, writes):
            d = {}
            for r in reads:
                if r.w is not None:
                    st, c = r.w
                    if d.get(st, 0) < c:
                        d[st] = c
            for w in writes:
                if w.w is not None:
                    st, c = w.w
                    if d.get(st, 0) < c:
                        d[st] = c
                for st, c in w.r.items():
                    if d.get(st, 0) < c:
                        d[st] = c
            return d

        def _wait(E, d):
            for st, c in d.items():
                if st is E and (E is PE or c < E.count):
                    continue
                if E.seen.get(st, 0) < c:
                    E.eng.wait_ge(st.sem, c)
                    E.seen[st] = c

        def op(E, fn, reads=(), writes=()):
            _wait(E, _deps(reads, writes))
            ins = fn()
            E.count += 1
            ins.then_inc(E.sem, 1)
            for w in writes:
                w.w = (E, E.count)
                w.r = {}
            for r in reads:
                r.r[E] = E.count

        bar_counts = {}

        def dma(Q, ch, fn, reads=(), writes=(), local=False):
            if local:
                _wait(Q, dict(bar_counts))
            _wait(Q, _deps(reads, writes))
            ins = fn()
            ch.count += 16
            ins.then_inc(ch.sem, 16)
            for w in writes:
                w.w = (ch, ch.count)
                w.r = {}
            for r in reads:
                r.r[ch] = ch.count
            if ch.name == "misc":
                Q.eng.wait_ge(ch.sem, ch.count)

        def barrier():
            for E in COMPUTE:
                for Fx in COMPUTE:
                    if Fx is E:
                        continue
                    if E.seen.get(Fx, 0) < Fx.count:
                        E.eng.wait_ge(Fx.sem, Fx.count)
                        E.seen[Fx] = Fx.count
            for E in COMPUTE:
                bar_counts[E] = E.count

        uniq = [0]

        def sb(name, shape, dt=F32, stack=es):
            uniq[0] += 1
            return stack.enter_context(nc.sbuf_tensor(f"{name}_{uniq[0]}", list(shape), dt))

        cv = [sem(f"cv{l}") for l in range(L)]
        cvtot = [0] * L
        for l in range(NL):
            for nm, r, c in wnames:
                for r0 in range(0, r, 128):
                    nc.gpsimd.dma_start(out=wb[nm][l, r0:r0 + 128, :], in_=wf[nm][l, r0:r0 + 128, :]).then_inc(cv[l], 16)
                    cvtot[l] += 16
        cv_waited = [False] * L

        xt = sb("xt", [128, 4, D])
        KT = sb("KT", [128, 4, S], BF16)
        VC = sb("VC", [128, 32, 512], BF16)
        NSLOT = 3
        wpan = [sb(f"wpan{i}", [128, 4096], BF16) for i in range(NSLOT)]
        cin = sb("cin", [128, 4, 542])
        hal = sb("hal", [128, 44, 2])
        hT = sb("hT", [128, 8, T], BF16)
        qT = sb("qT", [128, 4, T], BF16)
        oT = sb("oT", [128, 4, T], BF16)
        hc2 = sb("hc2", [128, 4, T], BF16)
        gmT = sb("gmT", [128, 4, T], BF16)
        gbc = sb("gbc", [128, 2, D])
        pv = sb("pv", [128, NV])
        bvt = sb("bvt", [128, 2560])
        fgt = sb("fgt", [128, D])
        cstt = sb("cstt", [128, 4])
        ident = sb("ident", [128, 128], BF16)
        onesb = sb("onesb", [128, 128], BF16)
        wsT = sb("wsT", [128, 4, 128], BF16)
        cact = sb("cact", [128, 8, 2], BF16)
        cactf = sb("cactf", [128, 8, 2])
        crep = sb("crep", [128, 8, 128], BF16)
        modsb = sb("modsb", [128, 48])
        drv = sb("drv", [128, 40])
        nlam_t = sb("nlam_t", [128, 16])
        gsub_t = sb("gsub_t", [128, 16])
        eps_t = sb("eps_t", [128, 16])
        psall = es.enter_context(nc.psum_tensor("psall", [128, 8, 512], F32))
        ps = [psall[:, i, :] for i in range(8)]

        R = {}

        def reg(name):
            if name not in R:
                R[name] = _Reg()
            return R[name]

        def regs(name, n):
            return [reg(f"{name}{i}") for i in range(n)]

        r_xt = regs("xt", 4)
        r_KT = regs("KT", 4)
        r_VC = reg("VC")
        r_wpan = regs("wpan", NSLOT)
        r_cin = regs("cin", 4)
        r_hal = reg("hal")
        r_hT = regs("hT", 8)
        r_qT = regs("qT", 4)
        r_oT = regs("oT", 4)
        r_hc2 = regs("hc2", 4)
        r_gm = regs("gm", 4)
        r_gbc = regs("gbc", 4)
        r_pv = reg("pv")
        r_bvt = reg("bvt")
        r_fgt = reg("fgt")
        r_cst = reg("cst")
        r_const = reg("const")
        r_wsT = reg("wsT")
        r_cact = reg("cact")
        r_crep = reg("crep")
        r_mod = reg("modsb")
        r_drv = reg("drv")
        r_ps = regs("ps", 8)
        r_xdram = {}

        ch_w = [chan(f"w{i}") for i in range(NSLOT)]
        ch_x = chan("xld")
        ch_xs = chan("xst")
        ch_m = chan("misc")
        ch_p = chan("pos")
        ch_d = chan("dbg")

        def dbg(name, ap, shape, dt, rlist):
            if not DEBUG:
                return
            t = nc.dram_tensor("dbg_" + name, list(shape), dt, kind="ExternalOutput").ap()
            dbg_outs[name] = (list(shape), dt)
            dma(SP, ch_d, lambda: nc.sync.dma_start(out=t, in_=ap), reads=rlist)
            SP.eng.wait_ge(ch_d.sem, ch_d.count)

        dma(SP, ch_m, lambda: nc.sync.dma_start(out=cstt[:], in_=cst_in[:, :]), writes=[r_cst])
        dma(SP, ch_m, lambda: nc.sync.dma_start(out=cactf[:], in_=cT_in[:, :, :]), writes=[r_cact])
        dma(SP, ch_m, lambda: nc.sync.dma_start(
            out=fgt[:], in_=bass.AP(tensor=fing_in.tensor, offset=0, ap=[[0, 128], [1, D]])), writes=[r_fgt])
        with contextlib.ExitStack() as cs:
            onesf = sb("onesf", [128, 128], F32, cs)
            idf = sb("idf", [128, 128], F32, cs)
            r_t = reg("ctmp")
            op(POOL, lambda: nc.gpsimd.memset(onesf[:], 1.0), writes=[r_t])
            op(POOL, lambda: nc.gpsimd.affine_select(out=idf[:], in_=onesf[:], pattern=[[1, 128]],
                                                     compare_op=ALU.is_equal, fill=0.0, base=0,
                                                     channel_multiplier=-1), reads=[r_t], writes=[r_const])
            op(DVE, lambda: nc.vector.tensor_copy(ident[:], idf[:]), reads=[r_const], writes=[r_const])
            op(DVE, lambda: nc.vector.tensor_copy(onesb[:], onesf[:]), reads=[r_t], writes=[r_const])
            op(ACT, lambda: nc.scalar.activation(out=cact[:], in_=cactf[:], func=AF.Silu), reads=[r_cact], writes=[r_cact])
            op(DVE, lambda: nc.vector.memset(drv[:], 0.0), writes=[r_drv])
            op(POOL, lambda: nc.gpsimd.memset(eps_t[:], EPS), reads=[r_drv], writes=[r_drv])
            op(POOL, lambda: nc.gpsimd.memset(nlam_t[:], 0.0), reads=[r_drv], writes=[r_drv])
            op(POOL, lambda: nc.gpsimd.memset(gsub_t[:], 0.0), reads=[r_drv], writes=[r_drv])
            barrier()

        bank_rr = [0]

        def next_bank(lo=0, hi=8):
            b = lo + (bank_rr[0] % (hi - lo))
            bank_rr[0] += 1
            return b

        slot_rr = [0]

        def load_panel(nm, l, kc0, kcn, c0, ncols):
            if not cv_waited[l]:
                SP.eng.wait_ge(cv[l], cvtot[l])
                cv_waited[l] = True
            s = slot_rr[0] % NSLOT
            slot_rr[0] += 1
            view = wpan[s][:, 0:kcn * ncols].rearrange("p (k n) -> p k n", n=ncols)
            src = wb[nm][l].rearrange("(kc p) n -> p kc n", p=128)[:, kc0:kc0 + kcn, c0:c0 + ncols]
            dma(SP, ch_w[s], lambda: nc.sync.dma_start(out=view, in_=src), writes=[r_wpan[s]])
            return view, r_wpan[s]

        def rsqrt_act(out_ap, in_ap, scale, rin, rout, tmp_ap, rtmp):
            op(ACT, lambda: nc.scalar.activation(out=tmp_ap, in_=in_ap, func=AF.Ln, scale=scale, bias=eps_t[:, 0:1]),
               reads=rin + [r_drv], writes=[rtmp])
            op(ACT, lambda: nc.scalar.activation(out=out_ap, in_=tmp_ap, func=AF.Exp, scale=-0.5),
               reads=[rtmp], writes=rout)

        def norm_to_hT(gs_col, sh_col, stack_name):
            with contextlib.ExitStack() as ls:
                xn = sb("xn" + stack_name, [128, 4, D], BF16, ls)
                junk = sb("junk" + stack_name, [128, D], BF16, ls)
                ss = sb("ss" + stack_name, [128, 4], F32, ls)
                rs = sb("rs" + stack_name, [128, 4], F32, ls)
                lt = sb("lt" + stack_name, [128, 4], F32, ls)
                r_xn = regs("xn_" + stack_name, 4)
                r_junk, r_ss, r_rs, r_lt = reg("junk"), reg("ss"), reg("rs"), reg("lt")
                for ts in range(4):
                    op(ACT, lambda ts=ts: nc.scalar.activation(out=junk[:], in_=xt[:, ts, :], func=AF.Square,
                                                                accum_out=ss[:, ts:ts + 1]),
                       reads=[r_xt[ts]], writes=[r_junk, r_ss])
                rsqrt_act(rs[:], ss[:], 1.0 / D, [r_ss], [r_rs], lt[:], r_lt)
                for ts in range(4):
                    op(DVE, lambda ts=ts: nc.vector.tensor_scalar(xn[:, ts, :], xt[:, ts, :], rs[:, ts:ts + 1], None,
                                                                  op0=ALU.mult),
                       reads=[r_xt[ts], r_rs], writes=[r_xn[ts]])
                for kc in range(8):
                    b = next_bank()
                    pb = ps[b][:].bitcast(BF16)
                    for ts in range(4):
                        op(PE, lambda ts=ts, kc=kc, pb=pb: nc.tensor.transpose(
                            pb[:, ts * 128:(ts + 1) * 128], xn[:, ts, kc * 128:(kc + 1) * 128], ident[:]),
                           reads=[r_xn[ts], r_const], writes=[r_ps[b]])
                    op(DVE, lambda kc=kc, pb=pb: nc.vector.tensor_scalar(
                        hT[:, kc, :], pb[:, 0:T], drv[:, gs_col + kc:gs_col + kc + 1],
                        drv[:, sh_col + kc:sh_col + kc + 1], op0=ALU.mult, op1=ALU.add),
                       reads=[r_ps[b], r_drv], writes=[r_hT[kc]])
                barrier()

        for l in range(NL):
            lam_init = 0.8 - 0.6 * math.exp(-0.3 * l)
            dma(SP, ch_m, lambda: nc.sync.dma_start(out=pv[:], in_=pvec_in[l, :, :]), writes=[r_pv])
            dma(SP, ch_m, lambda: nc.sync.dma_start(
                out=bvt[:], in_=bass.AP(tensor=bvec_in.tensor, offset=l * NB, ap=[[0, 128], [1, 2560]])),
                writes=[r_bvt])
            with contextlib.ExitStack() as ls:
                wsf = sb("wsf", [128, 4, 128], F32, ls)
                wsm = sb("wsm", [128, 4, 128], F32, ls)
                lmt = sb("lmt", [128, 256], F32, ls)
                lpr = sb("lpr", [128, 128], F32, ls)
                lsm = sb("lsm", [128, 2], F32, ls)
                lex = sb("lex", [128, 2], F32, ls)
                r_wsf, r_lmt, r_l2, r_l3, r_l4 = reg("wsf"), reg("lmt"), reg("lpr"), reg("lsm"), reg("lex")
                dma(SP, ch_m, lambda: nc.sync.dma_start(out=wsf[:], in_=wsp_in[l, :, :, :]), writes=[r_wsf], local=True)
                dma(SP, ch_m, lambda: nc.sync.dma_start(
                    out=lmt[:], in_=bass.AP(tensor=bvec_in.tensor, offset=l * NB + B_LAM, ap=[[0, 128], [1, 256]])),
                    writes=[r_lmt], local=True)
                op(POOL, lambda: nc.gpsimd.affine_select(out=wsm[:], in_=wsf[:], pattern=[[0, 4], [1, 128]],
                                                         compare_op=ALU.is_ge, fill=0.0, base=0,
                                                         channel_multiplier=-1), reads=[r_wsf], writes=[r_l2])
                op(DVE, lambda: nc.vector.tensor_copy(wsT[:], wsm[:]), reads=[r_l2], writes=[r_wsT])
                lp3 = sb("lp3", [128, 2, 64], F32, ls)
                lyy = sb("lyy", [128, 2], F32, ls)
                r_l6 = reg("lyy")
                op(DVE, lambda: nc.vector.tensor_tensor(out=lp3[:, 0, :], in0=lmt[:, 0:64], in1=lmt[:, 64:128], op=ALU.mult),
                   reads=[r_lmt], writes=[r_l3])
                op(DVE, lambda: nc.vector.tensor_tensor(out=lp3[:, 1, :], in0=lmt[:, 128:192], in1=lmt[:, 192:256], op=ALU.mult),
                   reads=[r_lmt], writes=[r_l3])
                for w_ in (32, 16, 8, 4, 2, 1):
                    op(DVE, lambda w_=w_: nc.vector.tensor_tensor(out=lp3[:, :, 0:w_], in0=lp3[:, :, 0:w_], in1=lp3[:, :, w_:2 * w_], op=ALU.add),
                       reads=[], writes=[r_l3])
                lxa = sb("lxa", [128, 16], F32, ls)
                lxb = sb("lxb", [128, 16], F32, ls)
                r_lx = [reg("lxa"), reg("lxb")]
                lx = [lxa, lxb]
                step = [0]

                def chain(fn_dve, fn_pool, extra_reads):
                    i_ = step[0] % 2
                    src, dst = lx[i_], lx[1 - i_]
                    if step[0] % 2 == 0:
                        op(DVE, lambda: fn_dve(dst, src), reads=[r_lx[i_]] + extra_reads, writes=[r_lx[1 - i_]])
                    else:
                        op(POOL, lambda: fn_pool(dst, src), reads=[r_lx[i_]] + extra_reads, writes=[r_lx[1 - i_]])
                    step[0] += 1

                op(POOL, lambda: nc.gpsimd.tensor_scalar(lyy[:], lp3[:, :, 0], 1.0 / 64, None, op0=ALU.mult), reads=[r_l3], writes=[r_l6])
                op(DVE, lambda: nc.vector.tensor_scalar(lxa[:, 0:2], lyy[:], 0.2, 1.0, op0=ALU.mult, op1=ALU.add), reads=[r_l6], writes=[r_lx[0]])
                for cf in (0.25, 1.0 / 3.0, 0.5, 1.0):
                    chain(lambda d_, s_: nc.vector.tensor_tensor(out=d_[:, 0:2], in0=s_[:, 0:2], in1=lyy[:], op=ALU.mult),
                          lambda d_, s_: nc.gpsimd.tensor_tensor(out=d_[:, 0:2], in0=s_[:, 0:2], in1=lyy[:], op=ALU.mult), [r_l6])
                    chain(lambda d_, s_, cf=cf: nc.vector.tensor_scalar(d_[:, 0:2], s_[:, 0:2], cf, 1.0, op0=ALU.mult, op1=ALU.add),
                          lambda d_, s_, cf=cf: nc.gpsimd.tensor_scalar(d_[:, 0:2], s_[:, 0:2], cf, 1.0, op0=ALU.mult, op1=ALU.add), [])
                for _ in range(6):
                    chain(lambda d_, s_: nc.vector.tensor_tensor(out=d_[:, 0:2], in0=s_[:, 0:2], in1=s_[:, 0:2], op=ALU.mult),
                          lambda d_, s_: nc.gpsimd.tensor_tensor(out=d_[:, 0:2], in0=s_[:, 0:2], in1=s_[:, 0:2], op=ALU.mult), [])
                lexf = lx[step[0] % 2]
                r_lexf = r_lx[step[0] % 2]
                op(DVE, lambda: nc.vector.scalar_tensor_tensor(out=nlam_t[:, 0:1], in0=lexf[:, 1:2], scalar=-lam_init,
                                                               in1=lexf[:, 0:1], op0=ALU.add, op1=ALU.subtract),
                   reads=[r_lexf, r_drv], writes=[r_drv])
                op(POOL, lambda: nc.gpsimd.tensor_scalar(gsub_t[:, 0:1], pv[:, P_SUBG:P_SUBG + 1], 1.0 - lam_init, None,
                                                        op0=ALU.mult), reads=[r_pv, r_drv], writes=[r_drv])
                barrier()

            for s in range(NSEQ):
                with contextlib.ExitStack() as ls:
                    bgt = sb("bgt", [128, 2048], F32, ls)
                    onesl = sb("onesl", [128, 128], BF16, ls)
                    r_bgt = reg("bgt")
                    dma(SP, ch_m, lambda: nc.sync.dma_start(
                        out=bgt[:], in_=bass.AP(tensor=bvec_in.tensor, offset=l * NB + B_BG1, ap=[[0, 128], [1, 2048]])),
                        writes=[r_bgt], local=True)
                    for kc in range(8):
                        op(DVE, lambda kc=kc: nc.vector.tensor_scalar(crep[:, kc, :], onesb[:], cact[:, kc, s:s + 1], None,
                                                                      op0=ALU.mult),
                           reads=[r_const, r_cact], writes=[r_crep])
                    mb = next_bank()
                    for pi in range(12):
                        pan, rp = load_panel("w_ada", l, 0, 8, pi * 512, 512)
                        which = pi // 2
                        if which in (2, 5):
                            gb = next_bank()
                            while gb == mb:
                                gb = next_bank()
                            for kc in range(8):
                                op(PE, lambda kc=kc, gb=gb, pan=pan: nc.tensor.matmul(
                                    ps[gb][:], lhsT=crep[:, kc, :], rhs=pan[:, kc, :], start=(kc == 0), stop=(kc == 7)),
                                   reads=[r_crep, rp], writes=[r_ps[gb]])
                            gi = 0 if which == 2 else 1
                            half = pi % 2
                            op(DVE, lambda gb=gb, gi=gi, half=half: nc.vector.tensor_tensor(
                                out=gbc[:, gi, half * 512:(half + 1) * 512], in0=ps[gb][:],
                                in1=bgt[:, gi * 1024 + half * 512: gi * 1024 + (half + 1) * 512], op=ALU.add),
                               reads=[r_ps[gb], r_bgt], writes=[r_gbc[gi * 2 + half]])
                        else:
                            for cc in range(4):
                                j = pi * 4 + cc
                                for kc in range(8):
                                    op(PE, lambda kc=kc, j=j, cc=cc, pan=pan: nc.tensor.matmul(
                                        ps[mb][:, j:j + 1], lhsT=pan[:, kc, cc * 128:(cc + 1) * 128],
                                        rhs=cact[:, kc, s:s + 1], start=(kc == 0), stop=(kc == 7)),
                                       reads=[r_cact, rp], writes=[r_ps[mb]])
                    op(DVE, lambda: nc.vector.tensor_tensor(out=modsb[:, 0:16], in0=ps[mb][:, 0:16],
                                                            in1=pv[:, P_BADA:P_BADA + 16], op=ALU.add),
                       reads=[r_ps[mb], r_pv], writes=[r_mod])
                    op(DVE, lambda: nc.vector.tensor_tensor(out=modsb[:, 24:40], in0=ps[mb][:, 24:40],
                                                            in1=pv[:, P_BADA + 24:P_BADA + 40], op=ALU.add),
                       reads=[r_ps[mb], r_pv, r_mod], writes=[r_mod])
                    op(DVE, lambda: nc.vector.scalar_tensor_tensor(out=drv[:, 0:8], in0=modsb[:, 8:16], scalar=1.0,
                                                                   in1=pv[:, P_LN1G:P_LN1G + 8], op0=ALU.add, op1=ALU.mult),
                       reads=[r_mod, r_pv, r_drv], writes=[r_drv])
                    op(POOL, lambda: nc.gpsimd.tensor_copy(drv[:, 8:16], modsb[:, 0:8]), reads=[r_mod, r_drv], writes=[r_drv])
                    op(DVE, lambda: nc.vector.scalar_tensor_tensor(out=drv[:, 16:24], in0=modsb[:, 32:40], scalar=1.0,
                                                                   in1=pv[:, P_LN2G:P_LN2G + 8], op0=ALU.add, op1=ALU.mult),
                       reads=[r_mod, r_pv, r_drv], writes=[r_drv])
                    op(POOL, lambda: nc.gpsimd.tensor_copy(drv[:, 24:32], modsb[:, 24:32]), reads=[r_mod, r_drv], writes=[r_drv])
                    barrier()

                op(POOL, lambda: nc.gpsimd.memset(hal[:], 0.0), writes=[r_hal])
                op(POOL, lambda: nc.gpsimd.memset(cin[:, :, 0:30], 0.0), writes=r_cin)

                for t in range(NT):
                    row0 = s * S + t * T
                    key = (s, t)
                    if key not in r_xdram:
                        r_xdram[key] = _Reg()
                    xsrc = (x_in if l == 0 else out)[row0:row0 + T, :].rearrange("(ts p) d -> p ts d", p=128)
                    dma(SP, ch_x, lambda: nc.sync.dma_start(out=xt[:], in_=xsrc), reads=[r_xdram[key]], writes=r_xt)

                    norm_to_hT(0, 8, "a")
                    if DEBUG and l == 0 and s == 0 and t == 0:
                        dbg("hT", hT[:], [128, 8, T], BF16, r_hT)
                    with contextlib.ExitStack() as ls:
                        cosT = sb("cosT", [128, T], F32, ls)
                        sinT = sb("sinT", [128, T], F32, ls)
                        pti = sb("pti", [128, T], I32, ls)
                        ang = sb("ang", [128, T], F32, ls)
                        tk = sb("tk", [128, T], I32, ls)
                        tr = sb("tr", [128, T], F32, ls)
                        tm = sb("tm", [128, T], F32, ls)
                        tA = sb("tA", [128, T], F32, ls)
                        tB = sb("tB", [128, T], F32, ls)
                        uT = sb("uT", [128, 4, T], BF16, ls)
                        g1 = sb("g1", [128, T], F32, ls)
                        g2 = sb("g2", [128, T], F32, ls)
                        g3 = sb("g3", [128, T], F32, ls)
                        g4 = sb("g4", [128, T], F32, ls)
                        vvn = sb("vvn", [128, T], BF16, ls)
                        st6 = sb("st6", [128, 6], F32, ls)
                        mv = sb("mv", [128, 2], F32, ls)
                        mv2 = sb("mv2", [128, 2], F32, ls)
                        r_cos, r_sin, r_pti, r_ang, r_tk, r_tr, r_tm = (reg("cosT"), reg("sinT"), reg("pti"), reg("ang"),
                                                                         reg("tk"), reg("tr"), reg("tm"))
                        r_tA, r_tB = reg("tA"), reg("tB")
                        r_uT = regs("uT", 4)
                        r_g = regs("gtmp", 5)
                        r_vvn, r_st6, r_mv, r_mv2 = reg("vvn"), reg("st6"), reg("mv"), reg("mv2")

                        psrc = bass.AP(tensor=pos_in.tensor, offset=s * S + t * T, ap=[[0, 128], [1, T]])
                        dma(SP, ch_p, lambda: nc.sync.dma_start(out=pti[:], in_=psrc), writes=[r_pti], local=True)
                        op(DVE, lambda: nc.vector.tensor_scalar(ang[:], pti[:], cstt[:, 0:1], None, op0=ALU.mult),
                           reads=[r_pti, r_cst], writes=[r_ang])

                        def sin_table(dst, r_dst, shift, scale_ap):
                            src = ang
                            if shift != 0.0:
                                op(DVE, lambda: nc.vector.tensor_scalar(tm[:], ang[:], shift, None, op0=ALU.add),
                                   reads=[r_ang], writes=[r_tm])
                                src = tm
                            op(DVE, lambda: nc.vector.tensor_scalar(tk[:], src[:], 1.0 / TWO_PI, None, op0=ALU.mult),
                               reads=[r_ang, r_tm], writes=[r_tk])
                            op(DVE, lambda: nc.vector.scalar_tensor_tensor(out=tr[:], in0=tk[:], scalar=-TWO_PI, in1=src[:],
                                                                           op0=ALU.mult, op1=ALU.add),
                               reads=[r_tk, r_ang, r_tm], writes=[r_tr])
                            op(DVE, lambda: nc.vector.tensor_scalar(tm[:], tr[:], math.pi, TWO_PI, op0=ALU.is_gt, op1=ALU.mult),
                               reads=[r_tr], writes=[r_tm])
                            op(DVE, lambda: nc.vector.tensor_tensor(out=tr[:], in0=tr[:], in1=tm[:], op=ALU.subtract),
                               reads=[r_tm, r_tr], writes=[r_tr])
                            if scale_ap is None:
                                op(ACT, lambda: nc.scalar.activation(out=dst[:], in_=tr[:], func=AF.Sin),
                                   reads=[r_tr], writes=[r_dst])
                            else:
                                op(ACT, lambda: nc.scalar.activation(out=dst[:], in_=tr[:], func=AF.Sin, scale=scale_ap),
                                   reads=[r_tr, r_cst], writes=[r_dst])

                        sin_table(sinT, r_sin, 0.0, cstt[:, 1:2])
                        sin_table(cosT, r_cos, math.pi / 2.0, None)

                        def fm_chunk(pan, rp, cc):
                            b = next_bank()
                            for kc in range(8):
                                op(PE, lambda kc=kc, b=b: nc.tensor.matmul(
                                    ps[b][:], lhsT=pan[:, kc, cc * 128:(cc + 1) * 128], rhs=hT[:, kc, :],
                                    start=(kc == 0), stop=(kc == 7)),
                                   reads=[rp, r_hT[kc]], writes=[r_ps[b]])
                            return b

                        def tm_sub(pan, rp, ts):
                            b = next_bank()
                            for kc in range(8):
                                op(PE, lambda kc=kc, b=b: nc.tensor.matmul(
                                    ps[b][:], lhsT=hT[:, kc, ts * 128:(ts + 1) * 128], rhs=pan[:, kc, :],
                                    start=(kc == 0), stop=(kc == 7)),
                                   reads=[rp, r_hT[kc]], writes=[r_ps[b]])
                            return b

                        for which in range(2):
                            panA, rpA = load_panel("w_in", l, 0, 8, (2 * which) * 512, 512)
                            panB, rpB = load_panel("w_in", l, 0, 8, (2 * which + 1) * 512, 512)
                            for hd in range(4):
                                bA = fm_chunk(panA, rpA, hd)
                                bB = fm_chunk(panB, rpB, hd)
                                colA = P_BIN + (2 * which) * 4 + hd
                                colB = P_BIN + (2 * which + 1) * 4 + hd
                                op(DVE, lambda bA=bA, colA=colA: nc.vector.scalar_tensor_tensor(
                                    out=tA[:], in0=ps[bA][:], scalar=pv[:, colA:colA + 1], in1=cosT[:],
                                    op0=ALU.add, op1=ALU.mult), reads=[r_ps[bA], r_pv, r_cos], writes=[r_tA])
                                op(DVE, lambda bB=bB, colB=colB: nc.vector.scalar_tensor_tensor(
                                    out=tB[:], in0=ps[bB][:], scalar=pv[:, colB:colB + 1], in1=sinT[:],
                                    op0=ALU.add, op1=ALU.mult), reads=[r_ps[bB], r_pv, r_sin], writes=[r_tB])
                                if which == 0:
                                    op(POOL, lambda hd=hd: nc.gpsimd.tensor_tensor(out=qT[:, hd, :], in0=tA[:], in1=tB[:], op=ALU.add),
                                       reads=[r_tA, r_tB], writes=[r_qT[hd]])
                                else:
                                    op(POOL, lambda hd=hd: nc.gpsimd.tensor_tensor(out=KT[:, hd, t * T:(t + 1) * T], in0=tA[:],
                                                                                   in1=tB[:], op=ALU.add),
                                       reads=[r_tA, r_tB], writes=[r_KT[hd]])
                        pan, rp = load_panel("w_in", l, 0, 8, 4 * 512, 512)
                        for ts in range(4):
                            b = tm_sub(pan, rp, ts)
                            op(DVE, lambda b=b, ts=ts: nc.vector.tensor_tensor(
                                out=VC[:, t * 4 + ts, :], in0=ps[b][:], in1=bvt[:, B_BV:B_BV + 512], op=ALU.add),
                               reads=[r_ps[b], r_bvt], writes=[r_VC])
                        if DEBUG and l == 0 and s == 0 and t == 0:
                            dbg("qT", qT[:], [128, 4, T], BF16, r_qT)
                            dbg("KT", KT[:, :, 0:T], [128, 4, T], BF16, r_KT)
                            dbg("VC", VC[:, 0:4, :], [128, 4, 512], BF16, [r_VC])

                        panA, rpA = load_panel("w_in", l, 0, 8, 5 * 512, 512)
                        panG, rpG = load_panel("w_in", l, 0, 8, 6 * 512, 512)
                        for c in range(4):
                            bG = fm_chunk(panG, rpG, c)
                            op(ACT, lambda bG=bG, c=c: nc.scalar.activation(out=g1[:], in_=ps[bG][:], func=AF.Sigmoid,
                                                                            bias=pv[:, P_BIN + 24 + c:P_BIN + 25 + c]),
                               reads=[r_ps[bG], r_pv], writes=[r_g[0]])
                            bA = fm_chunk(panA, rpA, c)
                            op(DVE, lambda bA=bA, c=c: nc.vector.scalar_tensor_tensor(
                                out=cin[:, c, 30:542], in0=ps[bA][:], scalar=pv[:, P_BIN + 20 + c:P_BIN + 21 + c], in1=g1[:],
                                op0=ALU.add, op1=ALU.mult), reads=[r_ps[bA], r_pv, r_g[0]], writes=[r_cin[c]])
                        def gelu_tanh(dst_ap, r_dst, xin, r_x):
                            op(POOL, lambda: nc.gpsimd.tensor_tensor(out=g2[:], in0=xin[:], in1=xin[:], op=ALU.mult),
                               reads=[r_x], writes=[r_g[1]])
                            op(DVE, lambda: nc.vector.tensor_scalar(g2[:], g2[:], 0.044715, 1.0, op0=ALU.mult, op1=ALU.add),
                               reads=[r_g[1]], writes=[r_g[1]])
                            op(POOL, lambda: nc.gpsimd.tensor_tensor(out=g2[:], in0=g2[:], in1=xin[:], op=ALU.mult),
                               reads=[r_x, r_g[1]], writes=[r_g[1]])
                            op(ACT, lambda: nc.scalar.activation(out=g3[:], in_=g2[:], func=AF.Sigmoid, scale=1.5957691216057308),
                               reads=[r_g[1]], writes=[r_g[2]])
                            op(DVE, lambda: nc.vector.tensor_tensor(out=dst_ap, in0=xin[:], in1=g3[:], op=ALU.mult),
                               reads=[r_x, r_g[2]], writes=r_dst)

                        pan, rp = load_panel("w_in", l, 0, 8, 7 * 512, 512)
                        for c in range(4):
                            b = fm_chunk(pan, rp, c)
                            op(ACT, lambda b=b, c=c: nc.scalar.activation(out=g1[:], in_=ps[b][:], func=AF.Identity,
                                                                          bias=pv[:, P_BIN + 28 + c:P_BIN + 29 + c]),
                               reads=[r_ps[b], r_pv], writes=[r_g[0]])
                            gelu_tanh(uT[:, c, :], [r_uT[c]], g1, r_g[0])
                        pan, rp = load_panel("w_in", l, 0, 8, 8 * 512, 512)
                        bsg = [next_bank() for _ in range(4)]
                        for ts in range(4):
                            b = next_bank()
                            while b in bsg:
                                b = next_bank()
                            for kc in range(8):
                                op(PE, lambda kc=kc, b=b, ts=ts: nc.tensor.matmul(
                                    ps[b][:], lhsT=hT[:, kc, ts * 128:(ts + 1) * 128], rhs=pan[:, kc, :],
                                    start=(kc == 0), stop=(kc == 7)), reads=[rp, r_hT[kc]], writes=[r_ps[b]])
                            op(DVE, lambda b=b: nc.vector.tensor_tensor(out=g1[:], in0=ps[b][:], in1=bvt[:, B_BVV:B_BVV + 512], op=ALU.add),
                               reads=[r_ps[b], r_bvt], writes=[r_g[0]])
                            gelu_tanh(g4[:], [r_g[3]], g1, r_g[0])
                            op(DVE, lambda: nc.vector.bn_stats(st6[:], g4[:]), reads=[r_g[3]], writes=[r_st6])
                            op(DVE, lambda: nc.vector.bn_aggr(mv[:], st6[:]), reads=[r_st6], writes=[r_mv])
                            rsqrt_act(mv2[:, 0:1], mv[:, 1:2], 1.0, [r_mv], [r_mv2], mv2[:, 1:2], r_mv2)
                            op(DVE, lambda: nc.vector.tensor_scalar(g4[:], g4[:], mv[:, 0:1], mv2[:, 0:1], op0=ALU.subtract, op1=ALU.mult),
                               reads=[r_mv, r_mv2], writes=[r_g[3]])
                            op(POOL, lambda: nc.gpsimd.tensor_tensor(out=g4[:], in0=g4[:], in1=bvt[:, B_GG:B_GG + 512], op=ALU.mult),
                               reads=[r_bvt], writes=[r_g[3]])
                            op(DVE, lambda: nc.vector.tensor_tensor(out=vvn[:], in0=g4[:], in1=bvt[:, B_GB:B_GB + 512], op=ALU.add),
                               reads=[r_g[3], r_bvt], writes=[r_vvn])
                            for g in range(4):
                                op(PE, lambda g=g, ts=ts: nc.tensor.matmul(
                                    ps[bsg[g]][:, ts * 128:(ts + 1) * 128], lhsT=vvn[:, g * 128:(g + 1) * 128], rhs=wsT[:, g, :],
                                    start=True, stop=True), reads=[r_vvn, r_wsT], writes=[r_ps[bsg[g]]])
                        for g in range(4):
                            bview = bvt[:, B_BSP + g * 128:B_BSP + (g + 1) * 128]
                            b3 = bass.AP(tensor=bview.tensor, offset=bview.offset, ap=[list(bview.ap[0]), [0, 4], list(bview.ap[1])])
                            op(DVE, lambda g=g, b3=b3: nc.vector.tensor_tensor(
                                out=g1[:].rearrange("p (a b) -> p a b", b=128), in0=ps[bsg[g]][:].rearrange("p (a b) -> p a b", b=128),
                                in1=b3, op=ALU.add), reads=[r_ps[bsg[g]], r_bvt], writes=[r_g[0]])
                            op(POOL, lambda g=g: nc.gpsimd.tensor_tensor(out=gmT[:, g, :], in0=g1[:], in1=uT[:, g, :], op=ALU.mult),
                               reads=[r_g[0], r_uT[g]], writes=[r_gm[g]])
                        if DEBUG and l == 0 and s == 0 and t == 0:
                            dbg("gmT", gmT[:], [128, 4, T], BF16, r_gm)
                        barrier()

                    with contextlib.ExitStack() as ls:
                        ptg = [sb(f"ptg{i}", [128, 4, T], BF16, ls) for i in range(2)]
                        r_ptg = regs("ptg", 2)
                        a1 = sb("a1", [128, T], F32, ls)
                        a2 = sb("a2", [128, T], F32, ls)
                        a3 = sb("a3", [128, T], F32, ls)
                        a4 = sb("a4", [128, T], F32, ls)
                        a5 = sb("a5", [128, T], BF16, ls)
                        a3s = sb("a3s", [128, 4, T], F32, ls)
                        r_a = regs("atmp", 5)
                        r_a3s = regs("a3s", 4)
                        cout = sb("cout", [128, 4, T], F32, ls)
                        g2 = sb("g2b", [128, T], F32, ls)
                        g3 = sb("g3b", [128, T], F32, ls)
                        g4 = sb("g4b", [128, T], F32, ls)
                        g1 = sb("g1b", [128, T], F32, ls)
                        hcb = sb("hcb", [128, T], BF16, ls)
                        sqb = sb("sqb", [128, T], BF16, ls)
                        r_cout = regs("cout", 4)
                        r_g = regs("gtmpb", 5)
                        r_hcb, r_sqb = reg("hcb"), reg("sqb")
                        nkt = 4 * (t + 1)
                        ngrp = nkt // 2
                        for hd in range(4):
                            bo1, bo2, bs1, bs2 = 4, 5, 6, 7

                            def scores(kt, hd=hd):
                                for m in range(2):
                                    bk = (kt % 2) * 2 + m
                                    op(PE, lambda m=m, bk=bk, kt=kt: nc.tensor.matmul(
                                        ps[bk][:, 0:T], lhsT=KT[m * 64:(m + 1) * 64, hd, kt * 128:(kt + 1) * 128],
                                        rhs=qT[m * 64:(m + 1) * 64, hd, 0:T], start=True, stop=True),
                                       reads=[r_KT[hd], r_qT[hd]], writes=[r_ps[bk]])

                            def exps(kt):
                                b0 = (kt % 2) * 2
                                op(ACT, lambda kt=kt, b0=b0: nc.scalar.activation(out=ptg[kt % 2][:, 0:2, :], in_=psall[:, b0:b0 + 2, :],
                                                                               func=AF.Exp, scale=0.125),
                                   reads=[r_ps[b0], r_ps[b0 + 1]], writes=[r_ptg[kt % 2]])

                            def pvs(kt, hd=hd):
                                j = kt - 4 * t
                                c0 = 128 * j if j > 0 else 0
                                first = (kt == 0)
                                last = (kt == nkt - 1)
                                rp_ = r_ptg[kt % 2]
                                if j >= 0:
                                    op(POOL, lambda kt=kt, c0=c0: nc.gpsimd.memset(ptg[kt % 2][64:128, 0:2, c0:c0 + 64], 0.0),
                                       reads=[], writes=[rp_])
                                for m in range(2):
                                    p_ = ptg[kt % 2][:, m, :]
                                    bo = bo1 if m == 0 else bo2
                                    bs_ = bs1 if m == 0 else bs2
                                    for (dst_b, lw_full) in ((bo, VC[:, kt, hd * 128:(hd + 1) * 128]), (bs_, onesb[:])):
                                        op(PE, lambda dst_b=dst_b, lw_full=lw_full, p_=p_, c0=c0, first=first, last=last: nc.tensor.matmul(
                                            ps[dst_b][:, c0:T], lhsT=lw_full, rhs=p_[:, c0:T],
                                            start=first, stop=last, skip_group_check=True),
                                           reads=[r_VC, r_const, rp_], writes=[r_ps[dst_b]])

                            scores(0)
                            exps(0)
                            for kt in range(nkt):
                                if kt + 1 < nkt:
                                    scores(kt + 1)
                                    exps(kt + 1)
                                pvs(kt)
                            c = hd
                            op(DVE, lambda c=c: nc.vector.tensor_scalar(
                                cout[:, c, :], cin[:, c, 0:512], pv[:, P_CW + c:P_CW + c + 1], pv[:, P_CB + c:P_CB + c + 1],
                                op0=ALU.mult, op1=ALU.add), reads=[r_cin[c], r_pv], writes=[r_cout[c]])
                            op(DVE, lambda c=c: nc.vector.tensor_scalar(
                                g2[:], cin[:, c, 1:513], pv[:, P_CW + 4 + c:P_CW + 4 + c + 1], None, op0=ALU.mult),
                               reads=[r_cin[c], r_pv], writes=[r_g[1]])
                            for k in range(2, 31):
                                if k % 2 == 0:
                                    op(DVE, lambda c=c, k=k: nc.vector.scalar_tensor_tensor(
                                        out=cout[:, c, :], in0=cin[:, c, k:k + 512], scalar=pv[:, P_CW + k * 4 + c:P_CW + k * 4 + c + 1],
                                        in1=cout[:, c, :], op0=ALU.mult, op1=ALU.add),
                                       reads=[r_cin[c], r_pv], writes=[r_cout[c]])
                                else:
                                    op(DVE, lambda c=c, k=k: nc.vector.scalar_tensor_tensor(
                                        out=g2[:], in0=cin[:, c, k:k + 512], scalar=pv[:, P_CW + k * 4 + c:P_CW + k * 4 + c + 1],
                                        in1=g2[:], op0=ALU.mult, op1=ALU.add),
                                       reads=[r_cin[c], r_pv], writes=[r_g[1]])
                            op(DVE, lambda c=c: nc.vector.tensor_tensor(out=cout[:, c, :], in0=cout[:, c, :], in1=g2[:], op=ALU.add),
                               reads=[r_g[1]], writes=[r_cout[c]])
                            op(POOL, lambda c=c: nc.gpsimd.tensor_copy(cin[:, c, 0:30], cin[:, c, 512:542]),
                               reads=[], writes=[r_cin[c]])
                            op(ACT, lambda: nc.scalar.activation(out=a1[:], in_=ps[bs1][:], func=AF.Ln), reads=[r_ps[bs1]], writes=[r_a[0]])
                            op(ACT, lambda: nc.scalar.activation(out=a2[:], in_=ps[bs2][:], func=AF.Ln), reads=[r_ps[bs2]], writes=[r_a[1]])
                            op(ACT, lambda: nc.scalar.activation(out=a1[:], in_=a1[:], func=AF.Exp, scale=-1.0), reads=[], writes=[r_a[0]])
                            op(ACT, lambda: nc.scalar.activation(out=a2[:], in_=a2[:], func=AF.Exp, scale=-1.0), reads=[], writes=[r_a[1]])
                            op(DVE, lambda: nc.vector.tensor_tensor(out=a1[:], in0=ps[bo1][:], in1=a1[:], op=ALU.mult),
                               reads=[r_ps[bo1]], writes=[r_a[0]])
                            op(DVE, lambda: nc.vector.tensor_tensor(out=a2[:], in0=ps[bo2][:], in1=a2[:], op=ALU.mult),
                               reads=[r_ps[bo2]], writes=[r_a[1]])
                            op(DVE, lambda hd=hd: nc.vector.scalar_tensor_tensor(out=a3s[:, hd, :], in0=a2[:], scalar=nlam_t[:, 0:1], in1=a1[:],
                                                                                 op0=ALU.mult, op1=ALU.add),
                               reads=[r_a[0], r_a[1], r_drv], writes=[r_a3s[hd]])
                            if DEBUG and l == 0 and s == 0 and t == 0 and hd == 0:
                                dbg("a1", a1[:], [128, T], F32, [r_a[0]])
                                dbg("a2", a2[:], [128, T], F32, [r_a[1]])
                        for hd in range(4):
                            op(POOL, lambda hd=hd: nc.gpsimd.tensor_tensor(out=a5[:], in0=a3s[:, hd, :], in1=a3s[:, hd, :], op=ALU.mult),
                               reads=[r_a3s[hd]], writes=[r_a[4]])
                            bq = hd % 4
                            op(PE, lambda bq=bq: nc.tensor.matmul(ps[bq][:], lhsT=onesb[:], rhs=a5[:], start=True, stop=True),
                               reads=[r_a[4], r_const], writes=[r_ps[bq]])
                            rsqrt_act(a4[:], ps[bq][:], 1.0 / 128, [r_ps[bq]], [r_a[3]], a1[:], r_a[0])
                            op(DVE, lambda hd=hd: nc.vector.scalar_tensor_tensor(out=oT[:, hd, :], in0=a3s[:, hd, :], scalar=gsub_t[:, 0:1],
                                                                                 in1=a4[:], op0=ALU.mult, op1=ALU.mult),
                               reads=[r_a3s[hd], r_a[3], r_drv], writes=[r_oT[hd]])
                        bS, bQ = 4, 5
                        for c in range(4):
                            op(ACT, lambda c=c: nc.scalar.activation(out=hcb[:], in_=cout[:, c, :], func=AF.Copy),
                               reads=[r_cout[c]], writes=[r_hcb])
                            op(ACT, lambda c=c: nc.scalar.activation(out=sqb[:], in_=cout[:, c, :], func=AF.Square),
                               reads=[r_cout[c]], writes=[r_sqb])
                            op(PE, lambda c=c: nc.tensor.matmul(ps[bS][:], lhsT=onesb[:], rhs=hcb[:], start=(c == 0), stop=(c == 3)),
                               reads=[r_hcb, r_const], writes=[r_ps[bS]])
                            op(PE, lambda c=c: nc.tensor.matmul(ps[bQ][:], lhsT=onesb[:], rhs=sqb[:], start=(c == 0), stop=(c == 3)),
                               reads=[r_sqb, r_const], writes=[r_ps[bQ]])
                        op(DVE, lambda: nc.vector.tensor_scalar(g2[:], ps[bS][:], 1.0 / 512, None, op0=ALU.mult),
                           reads=[r_ps[bS]], writes=[r_g[1]])
                        op(DVE, lambda: nc.vector.tensor_tensor(out=g3[:], in0=g2[:], in1=g2[:], op=ALU.mult),
                           reads=[r_g[1]], writes=[r_g[2]])
                        op(DVE, lambda: nc.vector.scalar_tensor_tensor(out=g3[:], in0=ps[bQ][:], scalar=1.0 / 512, in1=g3[:],
                                                                       op0=ALU.mult, op1=ALU.subtract),
                           reads=[r_ps[bQ], r_g[2]], writes=[r_g[2]])
                        op(DVE, lambda: nc.vector.tensor_scalar(g3[:], g3[:], 0.0, None, op0=ALU.max),
                           reads=[r_g[2]], writes=[r_g[2]])
                        rsqrt_act(g4[:], g3[:], 1.0, [r_g[2]], [r_g[3]], g1[:], r_g[0])
                        for c in range(4):
                            op(DVE, lambda c=c: nc.vector.tensor_tensor(out=cout[:, c, :], in0=cout[:, c, :], in1=g2[:], op=ALU.subtract),
                               reads=[r_g[1]], writes=[r_cout[c]])
                            op(POOL, lambda c=c: nc.gpsimd.tensor_tensor(out=cout[:, c, :], in0=cout[:, c, :], in1=g4[:], op=ALU.mult),
                               reads=[r_g[3]], writes=[r_cout[c]])
                            op(ACT, lambda c=c: nc.scalar.activation(out=hc2[:, c, :], in_=cout[:, c, :], func=AF.Silu,
                                                                     scale=pv[:, P_CLG + c:P_CLG + c + 1],
                                                                     bias=pv[:, P_CLB + c:P_CLB + c + 1]),
                               reads=[r_cout[c], r_pv], writes=[r_hc2[c]])
                        if DEBUG and l == 0 and s == 0 and t == 0:
                            dbg("oT", oT[:], [128, 4, T], BF16, r_oT)
                            dbg("hc2", hc2[:], [128, 4, T], BF16, r_hc2)
                        barrier()

                    with contextlib.ExitStack() as ls:
                        mixf = sb("mixf", [128, 8, T], F32, ls)
                        mixb = sb("mixb", [128, 8, T], BF16, ls)
                        gt4 = [sb(f"gt4_{i}", [128, 4, T], BF16, ls) for i in range(2)]
                        tmpc = sb("tmpc", [128, T], F32, ls)
                        r_mixf = regs("mixf", 8)
                        r_mixb = regs("mixb", 8)
                        r_gt4 = regs("gt4", 2)
                        r_tmpc = reg("tmpc")
                        srcs = [(oT, r_oT, "w_att"), (hc2, r_hc2, "w_conv"), (gmT, r_gm, "w_gmlp")]
                        gi = 0
                        for bi, (srcT, r_src, wn) in enumerate(srcs):
                            wo_, rwo = load_panel(wn, l, 0, 4, 0, D)
                            for half in range(2):
                                gp, rgp = load_panel("w_in", l, 0, 8, (9 + bi * 2 + half) * 512, 512)
                                gtile = gt4[gi % 2]
                                rg = r_gt4[gi % 2]
                                gi += 1
                                for cc in range(4):
                                    b = next_bank()
                                    for kc in range(8):
                                        op(PE, lambda kc=kc, b=b, cc=cc, gp=gp: nc.tensor.matmul(
                                            ps[b][:], lhsT=gp[:, kc, cc * 128:(cc + 1) * 128], rhs=hT[:, kc, :],
                                            start=(kc == 0), stop=(kc == 7)), reads=[rgp, r_hT[kc]], writes=[r_ps[b]])
                                    col = P_BIN + (9 + bi * 2 + half) * 4 + cc
                                    op(ACT, lambda b=b, cc=cc, col=col, gtile=gtile: nc.scalar.activation(
                                        out=gtile[:, cc, :], in_=ps[b][:], func=AF.Sigmoid, bias=pv[:, col:col + 1]),
                                       reads=[r_ps[b], r_pv], writes=[rg])
                                for cc in range(4):
                                    dc = half * 4 + cc
                                    b = next_bank()
                                    for kc in range(4):
                                        op(PE, lambda kc=kc, b=b, dc=dc, wo_=wo_, srcT=srcT: nc.tensor.matmul(
                                            ps[b][:], lhsT=wo_[:, kc, dc * 128:(dc + 1) * 128], rhs=srcT[:, kc, :],
                                            start=(kc == 0), stop=(kc == 3)), reads=[rwo, r_src[kc]], writes=[r_ps[b]])
                                    if bi == 0:
                                        op(DVE, lambda b=b, dc=dc, cc=cc, gtile=gtile: nc.vector.tensor_tensor(
                                            out=mixf[:, dc, :], in0=ps[b][:], in1=gtile[:, cc, :], op=ALU.mult),
                                           reads=[r_ps[b], rg], writes=[r_mixf[dc]])
                                    else:
                                        op(DVE, lambda b=b, cc=cc, gtile=gtile: nc.vector.tensor_tensor(
                                            out=tmpc[:], in0=ps[b][:], in1=gtile[:, cc, :], op=ALU.mult),
                                           reads=[r_ps[b], rg], writes=[r_tmpc])
                                        if bi == 1:
                                            op(POOL, lambda dc=dc: nc.gpsimd.tensor_tensor(
                                                out=mixf[:, dc, :], in0=mixf[:, dc, :], in1=tmpc[:], op=ALU.add),
                                               reads=[r_tmpc], writes=[r_mixf[dc]])
                                        else:
                                            op(POOL, lambda dc=dc: nc.gpsimd.tensor_tensor(
                                                out=mixb[:, dc, :], in0=mixf[:, dc, :], in1=tmpc[:], op=ALU.add),
                                               reads=[r_tmpc, r_mixf[dc]], writes=[r_mixb[dc]])
                        if DEBUG and l == 0 and s == 0 and t == 0:
                            dbg("mixb", mixb[:], [128, 8, T], BF16, r_mixb)
                        for nh in range(2):
                            wo_, rwo = load_panel("w_o", l, 0, 8, nh * 512, 512)
                            for ts in range(4):
                                b = next_bank()
                                for kc in range(8):
                                    op(PE, lambda kc=kc, b=b, ts=ts, wo_=wo_: nc.tensor.matmul(
                                        ps[b][:], lhsT=mixb[:, kc, ts * 128:(ts + 1) * 128], rhs=wo_[:, kc, :],
                                        start=(kc == 0), stop=(kc == 7)), reads=[rwo, r_mixb[kc]], writes=[r_ps[b]])
                                op(DVE, lambda b=b, nh=nh: nc.vector.tensor_tensor(
                                    out=tmpc[:], in0=ps[b][:], in1=gbc[:, 0, nh * 512:(nh + 1) * 512], op=ALU.mult),
                                   reads=[r_ps[b], r_gbc[nh]], writes=[r_tmpc])
                                op(POOL, lambda ts=ts, nh=nh: nc.gpsimd.tensor_tensor(
                                    out=xt[:, ts, nh * 512:(nh + 1) * 512], in0=xt[:, ts, nh * 512:(nh + 1) * 512], in1=tmpc[:], op=ALU.add),
                                   reads=[r_tmpc], writes=[r_xt[ts]])
                        if DEBUG and l == 0 and s == 0 and t == 0:
                            dbg("xmid", xt[:], [128, 4, D], F32, r_xt)
                        barrier()

                    norm_to_hT(16, 24, "d")
                    with contextlib.ExitStack() as ls:
                        actT = sb("actT", [128, 22, T], BF16, ls)
                        stg4 = sb("stg4", [128, 4, 514], F32, ls)
                        stg = [stg4[:, i, :] for i in range(4)]
                        ft = [sb(f"ft{i}", [128, T], F32, ls) for i in range(4)]
                        r_act = regs("actT", 22)
                        r_stg = regs("stg", 4)
                        r_ft = regs("ft", 4)
                        pending = []
                        pair_i = [0]
                        for pi in range(11):
                            pan, rp = load_panel("w_up", l, 0, 8, pi * 512, 512)
                            for pr in range(2):
                                info = []
                                for q_ in range(2):
                                    cc = pr * 2 + q_
                                    j = pi * 4 + cc
                                    b = next_bank()
                                    for kc in range(8):
                                        op(PE, lambda kc=kc, b=b, cc=cc, pan=pan: nc.tensor.matmul(
                                            ps[b][:], lhsT=pan[:, kc, cc * 128:(cc + 1) * 128], rhs=hT[:, kc, :],
                                            start=(kc == 0), stop=(kc == 7)), reads=[rp, r_hT[kc]], writes=[r_ps[b]])
                                    bi_ = (pair_i[0] % 2) * 2 + q_
                                    sg_, rs_, f_, rf_ = stg[bi_], r_stg[bi_], ft[bi_], r_ft[bi_]
                                    if q_ == 0:
                                        op(POOL, lambda bi_=bi_, j=j: nc.gpsimd.tensor_copy(stg4[:, bi_:bi_ + 2, 0:2], hal[:, j:j + 2, :]),
                                           reads=[r_hal], writes=[r_stg[bi_], r_stg[bi_ + 1]])
                                    op(ACT, lambda sg_=sg_, b=b: nc.scalar.activation(out=sg_[:, 2:514], in_=ps[b][:], func=AF.Copy),
                                       reads=[r_ps[b]], writes=[rs_])
                                    if q_ == 1:
                                        op(POOL, lambda bi_=bi_, j=j: nc.gpsimd.tensor_copy(hal[:, j - 1:j + 1, :], stg4[:, bi_ - 1:bi_ + 1, 512:514]),
                                           reads=[r_stg[bi_ - 1], r_stg[bi_]], writes=[r_hal])
                                    info.append((j, sg_, rs_, f_, rf_))
                                for (j, sg_, rs_, f_, rf_) in info:
                                    op(DVE, lambda sg_=sg_, f_=f_, j=j: nc.vector.tensor_scalar(
                                        f_[:], sg_[:, 2:514], pv[:, P_FW + 2 * 44 + j:P_FW + 2 * 44 + j + 1], pv[:, P_FB + j:P_FB + j + 1],
                                        op0=ALU.mult, op1=ALU.add), reads=[rs_, r_pv], writes=[rf_])
                                for (j, sg_, rs_, f_, rf_) in info:
                                    op(DVE, lambda sg_=sg_, f_=f_, j=j: nc.vector.scalar_tensor_tensor(
                                        out=f_[:], in0=sg_[:, 1:513], scalar=pv[:, P_FW + 44 + j:P_FW + 44 + j + 1], in1=f_[:],
                                        op0=ALU.mult, op1=ALU.add), reads=[rs_, r_pv], writes=[rf_])
                                for (j, sg_, rs_, f_, rf_) in info:
                                    op(DVE, lambda sg_=sg_, f_=f_, j=j: nc.vector.scalar_tensor_tensor(
                                        out=f_[:], in0=sg_[:, 0:512], scalar=pv[:, P_FW + j:P_FW + j + 1], in1=f_[:],
                                        op0=ALU.mult, op1=ALU.add), reads=[rs_, r_pv], writes=[rf_])
                                pair_i[0] += 1

                                def finals(items):
                                    for (j, sg_, rs_, f_, rf_) in items:
                                        if j < 22:
                                            op(ACT, lambda f_=f_, j=j: nc.scalar.activation(out=actT[:, j, :], in_=f_[:], func=AF.Silu),
                                               reads=[rf_], writes=[r_act[j]])
                                        else:
                                            op(POOL, lambda f_=f_, j=j: nc.gpsimd.tensor_tensor(
                                                out=actT[:, j - 22, :], in0=actT[:, j - 22, :], in1=f_[:], op=ALU.mult),
                                               reads=[rf_], writes=[r_act[j - 22]])

                                if pending:
                                    finals(pending.pop())
                                pending.append(info)
                        if pending:
                            finals(pending.pop())
                        if DEBUG and l == 0 and s == 0 and t == 0:
                            dbg("actT", actT[:], [128, 22, T], BF16, r_act)
                        for pi in range(11):
                            pan, rp = load_panel("w_down", l, 2 * pi, 2, 0, D)
                            for kk in range(2):
                                kc = 2 * pi + kk
                                for ts in range(4):
                                    for nh in range(2):
                                        b = ts * 2 + nh
                                        op(PE, lambda kc=kc, kk=kk, b=b, ts=ts, nh=nh, pan=pan: nc.tensor.matmul(
                                            ps[b][:], lhsT=actT[:, kc, ts * 128:(ts + 1) * 128], rhs=pan[:, kk, nh * 512:(nh + 1) * 512],
                                            start=(kc == 0), stop=(kc == 21), skip_group_check=True),
                                           reads=[rp, r_act[kc]], writes=[r_ps[b]])
                        for ts in range(4):
                            for nh in range(2):
                                b = ts * 2 + nh
                                f_ = ft[b % 2]
                                rf_ = r_ft[b % 2]
                                op(DVE, lambda b=b, nh=nh, f_=f_: nc.vector.tensor_tensor(
                                    out=f_[:], in0=ps[b][:], in1=gbc[:, 1, nh * 512:(nh + 1) * 512], op=ALU.mult),
                                   reads=[r_ps[b], r_gbc[2 + nh]], writes=[rf_])
                                op(POOL, lambda ts=ts, nh=nh, f_=f_: nc.gpsimd.tensor_tensor(
                                    out=xt[:, ts, nh * 512:(nh + 1) * 512], in0=xt[:, ts, nh * 512:(nh + 1) * 512], in1=f_[:], op=ALU.add),
                                   reads=[rf_], writes=[r_xt[ts]])
                        barrier()

                    if l == NL - 1:
                        with contextlib.ExitStack() as ls:
                            junk = sb("fjunk", [128, D], BF16, ls)
                            ss = sb("fss", [128, 4], F32, ls)
                            rs = sb("frs", [128, 4], F32, ls)
                            lt = sb("flt", [128, 4], F32, ls)
                            r_j, r_s1, r_s2, r_s3 = reg("fjunk"), reg("fss"), reg("frs"), reg("flt")
                            for ts in range(4):
                                op(ACT, lambda ts=ts: nc.scalar.activation(out=junk[:], in_=xt[:, ts, :], func=AF.Square,
                                                                            accum_out=ss[:, ts:ts + 1]),
                                   reads=[r_xt[ts]], writes=[r_j, r_s1])
                            rsqrt_act(rs[:], ss[:], 1.0 / D, [r_s1], [r_s2], lt[:], r_s3)
                            for ts in range(4):
                                op(DVE, lambda ts=ts: nc.vector.scalar_tensor_tensor(
                                    out=xt[:, ts, :], in0=xt[:, ts, :], scalar=rs[:, ts:ts + 1], in1=fgt[:],
                                    op0=ALU.mult, op1=ALU.mult), reads=[r_s2, r_fgt], writes=[r_xt[ts]])
                            barrier()
                    xdst = out[row0:row0 + T, :].rearrange("(ts p) d -> p ts d", p=128)
                    dma(POOL, ch_xs, lambda: nc.gpsimd.dma_start(out=xdst, in_=xt[:]), reads=r_xt, writes=[r_xdram[key]])

        nc.gpsimd.wait_ge(ch_xs.sem, ch_xs.count)
        nc.sync.wait_ge(ch_xs.sem, ch_xs.count)
    return nc, dbg_outs


def _prep_shared(inp):
    f = lambda a: np.ascontiguousarray(np.asarray(a, dtype=np.float32))
    w_in = f(inp["w_in"])
    b_in = f(inp["b_in"])
    perm = np.arange(512).reshape(4, 2, 64)
    perm = np.concatenate([perm[:, :, 32:], perm[:, :, :32]], axis=2).reshape(512)
    segs = [np.arange(0, 512), perm, 512 + np.arange(512), 512 + perm, 1024 + np.arange(512),
            1536 + np.arange(1024), 2560 + np.arange(1024), 3584 + np.arange(3072)]
    cols = np.concatenate(segs)
    assert cols.shape[0] == NEXT
    w_in_e = np.ascontiguousarray(w_in[:, :, cols])
    b_in_e = b_in[:, cols]

    def fm(v):
        Ln, n = v.shape
        return v.reshape(Ln, n // 128, 128).transpose(0, 2, 1)

    pvec = np.zeros((L, 128, NV), np.float32)
    pvec[:, :, P_BIN:P_BIN + 60] = fm(b_in_e)
    pvec[:, :, P_LN1G:P_LN1G + 8] = fm(f(inp["ln1_g"]))
    pvec[:, :, P_LN2G:P_LN2G + 8] = fm(f(inp["ln2_g"]))
    cw = f(inp["conv_dw_w"])
    pvec[:, :, P_CW:P_CW + 124] = cw.reshape(L, 31, 4, 128).transpose(0, 3, 1, 2).reshape(L, 128, 124)
    pvec[:, :, P_CB:P_CB + 4] = fm(f(inp["conv_dw_b"]))
    pvec[:, :, P_CLG:P_CLG + 4] = fm(f(inp["conv_ln_g"]))
    pvec[:, :, P_CLB:P_CLB + 4] = fm(f(inp["conv_ln_b"]))
    pvec[:, :, P_SUBG:P_SUBG + 1] = fm(f(inp["attn_subln_g"]))
    fw = f(inp["ffn_dw_w"])
    pvec[:, :, P_FW:P_FW + 132] = fw.reshape(L, 3, 44, 128).transpose(0, 3, 1, 2).reshape(L, 128, 132)
    pvec[:, :, P_FB:P_FB + 44] = fm(f(inp["ffn_dw_b"]))
    b_ada = f(inp["b_ada"])
    pvec[:, :, P_BADA:P_BADA + 48] = fm(b_ada)

    bvec = np.zeros((L, NB), np.float32)
    bvec[:, B_BV:B_BV + 512] = b_in[:, 1024:1536]
    bvec[:, B_BVV:B_BVV + 512] = b_in[:, 3072:3584]
    bvec[:, B_GG:B_GG + 512] = f(inp["gmlp_ln_g"])
    bvec[:, B_GB:B_GB + 512] = f(inp["gmlp_ln_b"])
    bvec[:, B_BSP:B_BSP + 512] = f(inp["b_spatial"]).reshape(L, 512)
    bvec[:, B_BG1:B_BG1 + 1024] = b_ada[:, 2048:3072]
    bvec[:, B_BG2:B_BG2 + 1024] = b_ada[:, 5120:6144]
    bvec[:, B_LAM:B_LAM + 64] = f(inp["lambda_q1"])
    bvec[:, B_LAM + 64:B_LAM + 128] = f(inp["lambda_k1"])
    bvec[:, B_LAM + 128:B_LAM + 192] = f(inp["lambda_q2"])
    bvec[:, B_LAM + 192:B_LAM + 256] = f(inp["lambda_k2"])

    inv_freq = (1.0 / (10000.0 ** (np.arange(0, 64, 2, dtype=np.float32) / 64.0))).astype(np.float32)
    cst = np.zeros((128, 4), np.float32)
    d = np.arange(128) % 64
    cst[:, 0] = inv_freq[d % 32]
    cst[:, 1] = np.where(d < 32, -1.0, 1.0)

    shared = {
        "cst": cst, "pvec": pvec, "bvec": bvec, "fing": f(inp["final_g"]).reshape(1, D),
        "wspT": np.ascontiguousarray(f(inp["w_spatial"]).transpose(0, 3, 1, 2)),
        "w_in": w_in_e, "w_ada": f(inp["w_ada"]), "w_att": f(inp["w_attn_out"]), "w_conv": f(inp["w_conv_out"]),
        "w_gmlp": f(inp["w_gmlp_out"]), "w_o": f(inp["w_o"]), "w_up": f(inp["w_up"]), "w_down": f(inp["w_down"]),
    }
    return shared


def _in_maps(inp, shared):
    x = np.asarray(inp["x"], dtype=np.float32)
    c = np.asarray(inp["c"], dtype=np.float32)
    pos = np.asarray(inp["positions"], dtype=np.int32)
    maps = []
    for core in range(8):
        b0 = 2 * core
        m = dict(shared)
        m["x"] = np.ascontiguousarray(x[b0:b0 + 2].reshape(2 * S, D))
        m["cT"] = np.ascontiguousarray(c[b0:b0 + 2].reshape(2, 8, 128).transpose(2, 1, 0))
        m["pos"] = np.ascontiguousarray(pos[b0:b0 + 2])
        maps.append(m)
    return maps


def kernel(**inputs):
    shared = _prep_shared(inputs)
    maps = _in_maps(inputs, shared)
    nc, _ = build_nc()
    res = run_bass_kernel_spmd(nc, maps, core_ids=list(range(8)))
    outs = [np.asarray(r["out"], dtype=np.float32).reshape(2, S, D) for r in res.results]
    return np.concatenate(outs, axis=0)
```

```python
import math
import contextlib
import numpy as np
import concourse.bass as bass
import concourse.mybir as mybir
from concourse.bass_utils import run_bass_kernel_spmd

F32 = mybir.dt.float32
BF16 = mybir.dt.bfloat16
I32 = mybir.dt.int32
AF = mybir.ActivationFunctionType
ALU = mybir.AluOpType

D = 1024
S = 4096
L = 4
T = 512
FF = 2816
NEXT = 7680
EPS = 1e-6
TWO_PI = 2.0 * math.pi

P_BIN = 0
P_LN1G = 60
P_LN2G = 68
P_CW = 76
P_CB = 200
P_CLG = 204
P_CLB = 208
P_SUBG = 212
P_FW = 213
P_FB = 345
P_BADA = 389
NV = 437
B_BV = 0
B_BVV = 512
B_GG = 1024
B_GB = 1536
B_BSP = 2048
B_BG1 = 2560
B_BG2 = 3584
B_LAM = 4608
NB = 4864


class _ES:
    def __init__(self, name, eng, sem, inc):
        self.name, self.eng, self.sem, self.inc = name, eng, sem, inc
        self.count = 0
        self.seen = {}


class _Reg:
    __slots__ = ("w", "r")

    def __init__(self):
        self.w = None
        self.r = {}


def build_nc(NL=L, NSEQ=2, NT=8, DEBUG=False, LW=L):
    nc = bass.Bass("TRN2", target_bir_lowering=False)

    def din(name, shape, dt=F32):
        return nc.dram_tensor(name, list(shape), dt, kind="ExternalInput").ap()

    def dscr(name, shape, dt=BF16):
        return nc.dram_tensor(name, list(shape), dt, kind="Internal").ap()

    x_in = din("x", [2 * S, D])
    cT_in = din("cT", [128, 8, 2])
    pos_in = din("pos", [2, S], I32)
    cst_in = din("cst", [128, 4])
    pvec_in = din("pvec", [LW, 128, NV])
    bvec_in = din("bvec", [LW, NB])
    fing_in = din("fing", [1, D])
    wsp_in = din("wspT", [LW, 128, 4, 128])
    wnames = [("w_in", D, NEXT), ("w_ada", D, 6144), ("w_att", 512, D), ("w_conv", 512, D),
              ("w_gmlp", 512, D), ("w_o", D, D), ("w_up", D, 2 * FF), ("w_down", FF, D)]
    wf = {}
    wb = {}
    for nm, r, c in wnames:
        wf[nm] = din(nm, [LW, r, c])
        wb[nm] = dscr(nm + "_b", [LW, r, c])
    out = nc.dram_tensor("out", [2 * S, D], F32, kind="ExternalOutput").ap()
    dbg_outs = {}

    es = contextlib.ExitStack()
    with es:
        def sem(n):
            return es.enter_context(nc.semaphore(n))

        PE = _ES("pe", nc.tensor, sem("s_pe"), 1)
        ACT = _ES("act", nc.scalar, sem("s_act"), 1)
        DVE = _ES("dve", nc.vector, sem("s_dve"), 1)
        POOL = _ES("pool", nc.gpsimd, sem("s_pool"), 1)
        SP = _ES("sp", nc.sync, sem("s_sp"), 1)
        COMPUTE = [PE, ACT, DVE, POOL]

        def chan(n):
            return _ES(n, None, sem("c_" + n), 16)

        def _deps(reads, writes):
            d = {}
            for r in reads:
                if r.w is not None:
                    st, c = r.w
                    if d.get(st, 0) < c:
                        d[st] = c
            for w in writes:
                if w.w is not None:
                    st, c = w.w
                    if d.get(st, 0) < c:
                        d[st] = c
                for st, c in w.r.items():
                    if d.get(st, 0) < c:
                        d[st] = c
            return d

        def _wait(E, d):
            for st, c in d.items():
                if st is E and (E is PE or c < E.count):
                    continue
                if E.seen.get(st, 0) < c:
                    E.eng.wait_ge(st.sem, c)
                    E.seen[st] = c

        def op(E, fn, reads=(), writes=()):
            _wait(E, _deps(reads, writes))
            ins = fn()
            E.count += 1
            ins.then_inc(E.sem, 1)
            for w in writes:
                w.w = (E, E.count)
                w.r = {}
            for r in reads:
                r.r[E] = E.count

        bar_counts = {}

        def dma(Q, ch, fn, reads=(), writes=(), local=False):
            if local:
                _wait(Q, dict(bar_counts))
            _wait(Q, _deps(reads, writes))
            ins = fn()
            ch.count += 16
            ins.then_inc(ch.sem, 16)
            for w in writes:
                w.w = (ch, ch.count)
                w.r = {}
            for r in reads:
                r.r[ch] = ch.count
            if ch.name == "misc":
                Q.eng.wait_ge(ch.sem, ch.count)

        def barrier():
            for E in COMPUTE:
                for Fx in COMPUTE:
                    if Fx is E:
                        continue
                    if E.seen.get(Fx, 0) < Fx.count:
                        E.eng.wait_ge(Fx.sem, Fx.count)
                        E.seen[Fx] = Fx.count
            for E in COMPUTE:
                bar_counts[E] = E.count

        uniq = [0]

        def sb(name, shape, dt=F32, stack=es):
            uniq[0] += 1
            return stack.enter_context(nc.sbuf_tensor(f"{name}_{uniq[0]}", list(shape), dt))

        cv = [sem(f"cv{l}") for l in range(L)]
        cvtot = [0] * L
        for l in range(NL):
            for nm, r, c in wnames:
                for r0 in range(0, r, 128):
                    nc.gpsimd.dma_start(out=wb[nm][l, r0:r0 + 128, :], in_=wf[nm][l, r0:r0 + 128, :]).then_inc(cv[l], 16)
                    cvtot[l] += 16
        cv_waited = [False] * L

        xt = sb("xt", [128, 4, D])
        KT = sb("KT", [128, 4, S], BF16)
        VC = sb("VC", [128, 32, 512], BF16)
        NSLOT = 3
        wpan = [sb(f"wpan{i}", [128, 4096], BF16) for i in range(NSLOT)]
        cin = sb("cin", [128, 4, 542])
        hal = sb("hal", [128, 44, 2])
        hT = sb("hT", [128, 8, T], BF16)
        qT = sb("qT", [128, 4, T], BF16)
        oT = sb("oT", [128, 4, T], BF16)
        hc2 = sb("hc2", [128, 4, T], BF16)
        gmT = sb("gmT", [128, 4, T], BF16)
        gbc = sb("gbc", [128, 2, D])
        pv = sb("pv", [128, NV])
        bvt = sb("bvt", [128, 2560])
        fgt = sb("fgt", [128, D])
        cstt = sb("cstt", [128, 4])
        ident = sb("ident", [128, 128], BF16)
        onesb = sb("onesb", [128, 128], BF16)
        wsT = sb("wsT", [128, 4, 128], BF16)
        cact = sb("cact", [128, 8, 2], BF16)
        cactf = sb("cactf", [128, 8, 2])
        crep = sb("crep", [128, 8, 128], BF16)
        modsb = sb("modsb", [128, 48])
        drv = sb("drv", [128, 40])
        nlam_t = sb("nlam_t", [128, 16])
        gsub_t = sb("gsub_t", [128, 16])
        eps_t = sb("eps_t", [128, 16])
        psall = es.enter_context(nc.psum_tensor("psall", [128, 8, 512], F32))
        ps = [psall[:, i, :] for i in range(8)]

        R = {}

        def reg(name):
            if name not in R:
                R[name] = _Reg()
            return R[name]

        def regs(name, n):
            return [reg(f"{name}{i}") for i in range(n)]

        r_xt = regs("xt", 4)
        r_KT = regs("KT", 4)
        r_VC = reg("VC")
        r_wpan = regs("wpan", NSLOT)
        r_cin = regs("cin", 4)
        r_hal = reg("hal")
        r_hT = regs("hT", 8)
        r_qT = regs("qT", 4)
        r_oT = regs("oT", 4)
        r_hc2 = regs("hc2", 4)
        r_gm = regs("gm", 4)
        r_gbc = regs("gbc", 4)
        r_pv = reg("pv")
        r_bvt = reg("bvt")
        r_fgt = reg("fgt")
        r_cst = reg("cst")
        r_const = reg("const")
        r_wsT = reg("wsT")
        r_cact = reg("cact")
        r_crep = reg("crep")
        r_mod = reg("modsb")
        r_drv = reg("drv")
        r_ps = regs("ps", 8)
        r_xdram = {}

        ch_w = [chan(f"w{i}") for i in range(NSLOT)]
        ch_x = chan("xld")
        ch_xs = chan("xst")
        ch_m = chan("misc")
        ch_p = chan("pos")
        ch_d = chan("dbg")

        def dbg(name, ap, shape, dt, rlist):
            if not DEBUG:
                return
            t = nc.dram_tensor("dbg_" + name, list(shape), dt, kind="ExternalOutput").ap()
            dbg_outs[name] = (list(shape), dt)
            dma(SP, ch_d, lambda: nc.sync.dma_start(out=t, in_=ap), reads=rlist)
            SP.eng.wait_ge(ch_d.sem, ch_d.count)

        dma(SP, ch_m, lambda: nc.sync.dma_start(out=cstt[:], in_=cst_in[:, :]), writes=[r_cst])
        dma(SP, ch_m, lambda: nc.sync.dma_start(out=cactf[:], in_=cT_in[:, :, :]), writes=[r_cact])
        dma(SP, ch_m, lambda: nc.sync.dma_start(
            out=fgt[:], in_=bass.AP(tensor=fing_in.tensor, offset=0, ap=[[0, 128], [1, D]])), writes=[r_fgt])
        with contextlib.ExitStack() as cs:
            onesf = sb("onesf", [128, 128], F32, cs)
            idf = sb("idf", [128, 128], F32, cs)
            r_t = reg("ctmp")
            op(POOL, lambda: nc.gpsimd.memset(onesf[:], 1.0), writes=[r_t])
            op(POOL, lambda: nc.gpsimd.affine_select(out=idf[:], in_=onesf[:], pattern=[[1, 128]],
                                                     compare_op=ALU.is_equal, fill=0.0, base=0,
                                                     channel_multiplier=-1), reads=[r_t], writes=[r_const])
            op(DVE, lambda: nc.vector.tensor_copy(ident[:], idf[:]), reads=[r_const], writes=[r_const])
            op(DVE, lambda: nc.vector.tensor_copy(onesb[:], onesf[:]), reads=[r_t], writes=[r_const])
            op(ACT, lambda: nc.scalar.activation(out=cact[:], in_=cactf[:], func=AF.Silu), reads=[r_cact], writes=[r_cact])
            op(DVE, lambda: nc.vector.memset(drv[:], 0.0), writes=[r_drv])
            op(POOL, lambda: nc.gpsimd.memset(eps_t[:], EPS), reads=[r_drv], writes=[r_drv])
            op(POOL, lambda: nc.gpsimd.memset(nlam_t[:], 0.0), reads=[r_drv], writes=[r_drv])
            op(POOL, lambda: nc.gpsimd.memset(gsub_t[:], 0.0), reads=[r_drv], writes=[r_drv])
            barrier()

        bank_rr = [0]

        def next_bank(lo=0, hi=8):
            b = lo + (bank_rr[0] % (hi - lo))
            bank_rr[0] += 1
            return b

        slot_rr = [0]

        def load_panel(nm, l, kc0, kcn, c0, ncols):
            if not cv_waited[l]:
                SP.eng.wait_ge(cv[l], cvtot[l])
                cv_waited[l] = True
            s = slot_rr[0] % NSLOT
            slot_rr[0] += 1
            view = wpan[s][:, 0:kcn * ncols].rearrange("p (k n) -> p k n", n=ncols)
            src = wb[nm][l].rearrange("(kc p) n -> p kc n", p=128)[:, kc0:kc0 + kcn, c0:c0 + ncols]
            dma(SP, ch_w[s], lambda: nc.sync.dma_start(out=view, in_=src), writes=[r_wpan[s]])
            return view, r_wpan[s]

        def rsqrt_act(out_ap, in_ap, scale, rin, rout, tmp_ap, rtmp):
            op(ACT, lambda: nc.scalar.activation(out=tmp_ap, in_=in_ap, func=AF.Ln, scale=scale, bias=eps_t[:, 0:1]),
               reads=rin + [r_drv], writes=[rtmp])
            op(ACT, lambda: nc.scalar.activation(out=out_ap, in_=tmp_ap, func=AF.Exp, scale=-0.5),
               reads=[rtmp], writes=rout)

        def norm_to_hT(gs_col, sh_col, stack_name):
            with contextlib.ExitStack() as ls:
                xn = sb("xn" + stack_name, [128, 4, D], BF16, ls)
                junk = sb("junk" + stack_name, [128, D], BF16, ls)
                ss = sb("ss" + stack_name, [128, 4], F32, ls)
                rs = sb("rs" + stack_name, [128, 4], F32, ls)
                lt = sb("lt" + stack_name, [128, 4], F32, ls)
                r_xn = regs("xn_" + stack_name, 4)
                r_junk, r_ss, r_rs, r_lt = reg("junk"), reg("ss"), reg("rs"), reg("lt")
                for ts in range(4):
                    op(ACT, lambda ts=ts: nc.scalar.activation(out=junk[:], in_=xt[:, ts, :], func=AF.Square,
                                                                accum_out=ss[:, ts:ts + 1]),
                       reads=[r_xt[ts]], writes=[r_junk, r_ss])
                rsqrt_act(rs[:], ss[:], 1.0 / D, [r_ss], [r_rs], lt[:], r_lt)
                for ts in range(4):
                    op(DVE, lambda ts=ts: nc.vector.tensor_scalar(xn[:, ts, :], xt[:, ts, :], rs[:, ts:ts + 1], None,
                                                                  op0=ALU.mult),
                       reads=[r_xt[ts], r_rs], writes=[r_xn[ts]])
                for kc in range(8):
                    b = next_bank()
                    pb = ps[b][:].bitcast(BF16)
                    for ts in range(4):
                        op(PE, lambda ts=ts, kc=kc, pb=pb: nc.tensor.transpose(
                            pb[:, ts * 128:(ts + 1) * 128], xn[:, ts, kc * 128:(kc + 1) * 128], ident[:]),
                           reads=[r_xn[ts], r_const], writes=[r_ps[b]])
                    op(DVE, lambda kc=kc, pb=pb: nc.vector.tensor_scalar(
                        hT[:, kc, :], pb[:, 0:T], drv[:, gs_col + kc:gs_col + kc + 1],
                        drv[:, sh_col + kc:sh_col + kc + 1], op0=ALU.mult, op1=ALU.add),
                       reads=[r_ps[b], r_drv], writes=[r_hT[kc]])
                barrier()

        for l in range(NL):
            lam_init = 0.8 - 0.6 * math.exp(-0.3 * l)
            dma(SP, ch_m, lambda: nc.sync.dma_start(out=pv[:], in_=pvec_in[l, :, :]), writes=[r_pv])
            dma(SP, ch_m, lambda: nc.sync.dma_start(
                out=bvt[:], in_=bass.AP(tensor=bvec_in.tensor, offset=l * NB, ap=[[0, 128], [1, 2560]])),
                writes=[r_bvt])
            with contextlib.ExitStack() as ls:
                wsf = sb("wsf", [128, 4, 128], F32, ls)
                wsm = sb("wsm", [128, 4, 128], F32, ls)
                lmt = sb("lmt", [128, 256], F32, ls)
                lpr = sb("lpr", [128, 128], F32, ls)
                lsm = sb("lsm", [128, 2], F32, ls)
                lex = sb("lex", [128, 2], F32, ls)
                r_wsf, r_lmt, r_l2, r_l3, r_l4 = reg("wsf"), reg("lmt"), reg("lpr"), reg("lsm"), reg("lex")
                dma(SP, ch_m, lambda: nc.sync.dma_start(out=wsf[:], in_=wsp_in[l, :, :, :]), writes=[r_wsf], local=True)
                dma(SP, ch_m, lambda: nc.sync.dma_start(
                    out=lmt[:], in_=bass.AP(tensor=bvec_in.tensor, offset=l * NB + B_LAM, ap=[[0, 128], [1, 256]])),
                    writes=[r_lmt], local=True)
                op(POOL, lambda: nc.gpsimd.affine_select(out=wsm[:], in_=wsf[:], pattern=[[0, 4], [1, 128]],
                                                         compare_op=ALU.is_ge, fill=0.0, base=0,
                                                         channel_multiplier=-1), reads=[r_wsf], writes=[r_l2])
                op(DVE, lambda: nc.vector.tensor_copy(wsT[:], wsm[:]), reads=[r_l2], writes=[r_wsT])
                lp3 = sb("lp3", [128, 2, 64], F32, ls)
                lyy = sb("lyy", [128, 2], F32, ls)
                r_l6 = reg("lyy")
                op(DVE, lambda: nc.vector.tensor_tensor(out=lp3[:, 0, :], in0=lmt[:, 0:64], in1=lmt[:, 64:128], op=ALU.mult),
                   reads=[r_lmt], writes=[r_l3])
                op(DVE, lambda: nc.vector.tensor_tensor(out=lp3[:, 1, :], in0=lmt[:, 128:192], in1=lmt[:, 192:256], op=ALU.mult),
                   reads=[r_lmt], writes=[r_l3])
                for w_ in (32, 16, 8, 4, 2, 1):
                    op(DVE, lambda w_=w_: nc.vector.tensor_tensor(out=lp3[:, :, 0:w_], in0=lp3[:, :, 0:w_], in1=lp3[:, :, w_:2 * w_], op=ALU.add),
                       reads=[], writes=[r_l3])
                lxa = sb("lxa", [128, 16], F32, ls)
                lxb = sb("lxb", [128, 16], F32, ls)
                r_lx = [reg("lxa"), reg("lxb")]
                lx = [lxa, lxb]
                step = [0]

                def chain(fn_dve, fn_pool, extra_reads):
                    i_ = step[0] % 2
                    src, dst = lx[i_], lx[1 - i_]
                    if step[0] % 2 == 0:
                        op(DVE, lambda: fn_dve(dst, src), reads=[r_lx[i_]] + extra_reads, writes=[r_lx[1 - i_]])
                    else:
                        op(POOL, lambda: fn_pool(dst, src), reads=[r_lx[i_]] + extra_reads, writes=[r_lx[1 - i_]])
                    step[0] += 1

                op(POOL, lambda: nc.gpsimd.tensor_scalar(lyy[:], lp3[:, :, 0], 1.0 / 64, None, op0=ALU.mult), reads=[r_l3], writes=[r_l6])
                op(DVE, lambda: nc.vector.tensor_scalar(lxa[:, 0:2], lyy[:], 0.2, 1.0, op0=ALU.mult, op1=ALU.add), reads=[r_l6], writes=[r_lx[0]])
                for cf in (0.25, 1.0 / 3.0, 0.5, 1.0):
                    chain(lambda d_, s_: nc.vector.tensor_tensor(out=d_[:, 0:2], in0=s_[:, 0:2], in1=lyy[:], op=ALU.mult),
                          lambda d_, s_: nc.gpsimd.tensor_tensor(out=d_[:, 0:2], in0=s_[:, 0:2], in1=lyy[:], op=ALU.mult), [r_l6])
                    chain(lambda d_, s_, cf=cf: nc.vector.tensor_scalar(d_[:, 0:2], s_[:, 0:2], cf, 1.0, op0=ALU.mult, op1=ALU.add),
                          lambda d_, s_, cf=cf: nc.gpsimd.tensor_scalar(d_[:, 0:2], s_[:, 0:2], cf, 1.0, op0=ALU.mult, op1=ALU.add), [])
                for _ in range(6):
                    chain(lambda d_, s_: nc.vector.tensor_tensor(out=d_[:, 0:2], in0=s_[:, 0:2], in1=s_[:, 0:2], op=ALU.mult),
                          lambda d_, s_: nc.gpsimd.tensor_tensor(out=d_[:, 0:2], in0=s_[:, 0:2], in1=s_[:, 0:2], op=ALU.mult), [])
                lexf = lx[step[0] % 2]
                r_lexf = r_lx[step[0] % 2]
                op(DVE, lambda: nc.vector.scalar_tensor_tensor(out=nlam_t[:, 0:1], in0=lexf[:, 1:2], scalar=-lam_init,
                                                               in1=lexf[:, 0:1], op0=ALU.add, op1=ALU.subtract),
                   reads=[r_lexf, r_drv], writes=[r_drv])
                op(POOL, lambda: nc.gpsimd.tensor_scalar(gsub_t[:, 0:1], pv[:, P_SUBG:P_SUBG + 1], 1.0 - lam_init, None,
                                                        op0=ALU.mult), reads=[r_pv, r_drv], writes=[r_drv])
                barrier()

            for s in range(NSEQ):
                with contextlib.ExitStack() as ls:
                    bgt = sb("bgt", [128, 2048], F32, ls)
                    onesl = sb("onesl", [128, 128], BF16, ls)
                    r_bgt = reg("bgt")
                    dma(SP, ch_m, lambda: nc.sync.dma_start(
                        out=bgt[:], in_=bass.AP(tensor=bvec_in.tensor, offset=l * NB + B_BG1, ap=[[0, 128], [1, 2048]])),
                        writes=[r_bgt], local=True)
                    for kc in range(8):
                        op(DVE, lambda kc=kc: nc.vector.tensor_scalar(crep[:, kc, :], onesb[:], cact[:, kc, s:s + 1], None,
                                                                      op0=ALU.mult),
                           reads=[r_const, r_cact], writes=[r_crep])
                    mb = next_bank()
                    for pi in range(12):
                        pan, rp = load_panel("w_ada", l, 0, 8, pi * 512, 512)
                        which = pi // 2
                        if which in (2, 5):
                            gb = next_bank()
                            while gb == mb:
                                gb = next_bank()
                            for kc in range(8):
                                op(PE, lambda kc=kc, gb=gb, pan=pan: nc.tensor.matmul(
                                    ps[gb][:], lhsT=crep[:, kc, :], rhs=pan[:, kc, :], start=(kc == 0), stop=(kc == 7)),
                                   reads=[r_crep, rp], writes=[r_ps[gb]])
                            gi = 0 if which == 2 else 1
                            half = pi % 2
                            op(DVE, lambda gb=gb, gi=gi, half=half: nc.vector.tensor_tensor(
                                out=gbc[:, gi, half * 512:(half + 1) * 512], in0=ps[gb][:],
                                in1=bgt[:, gi * 1024 + half * 512: gi * 1024 + (half + 1) * 512], op=ALU.add),
                               reads=[r_ps[gb], r_bgt], writes=[r_gbc[gi * 2 + half]])
                        else:
                            for cc in range(4):
                                j = pi * 4 + cc
                                for kc in range(8):
                                    op(PE, lambda kc=kc, j=j, cc=cc, pan=pan: nc.tensor.matmul(
                                        ps[mb][:, j:j + 1], lhsT=pan[:, kc, cc * 128:(cc + 1) * 128],
                                        rhs=cact[:, kc, s:s + 1], start=(kc == 0), stop=(kc == 7)),
                                       reads=[r_cact, rp], writes=[r_ps[mb]])
                    op(DVE, lambda: nc.vector.tensor_tensor(out=modsb[:, 0:16], in0=ps[mb][:, 0:16],
                                                            in1=pv[:, P_BADA:P_BADA + 16], op=ALU.add),
                       reads=[r_ps[mb], r_pv], writes=[r_mod])
                    op(DVE, lambda: nc.vector.tensor_tensor(out=modsb[:, 24:40], in0=ps[mb][:, 24:40],
                                                            in1=pv[:, P_BADA + 24:P_BADA + 40], op=ALU.add),
                       reads=[r_ps[mb], r_pv, r_mod], writes=[r_mod])
                    op(DVE, lambda: nc.vector.scalar_tensor_tensor(out=drv[:, 0:8], in0=modsb[:, 8:16], scalar=1.0,
                                                                   in1=pv[:, P_LN1G:P_LN1G + 8], op0=ALU.add, op1=ALU.mult),
                       reads=[r_mod, r_pv, r_drv], writes=[r_drv])
                    op(POOL, lambda: nc.gpsimd.tensor_copy(drv[:, 8:16], modsb[:, 0:8]), reads=[r_mod, r_drv], writes=[r_drv])
                    op(DVE, lambda: nc.vector.scalar_tensor_tensor(out=drv[:, 16:24], in0=modsb[:, 32:40], scalar=1.0,
                                                                   in1=pv[:, P_LN2G:P_LN2G + 8], op0=ALU.add, op1=ALU.mult),
                       reads=[r_mod, r_pv, r_drv], writes=[r_drv])
                    op(POOL, lambda: nc.gpsimd.tensor_copy(drv[:, 24:32], modsb[:, 24:32]), reads=[r_mod, r_drv], writes=[r_drv])
                    barrier()

                op(POOL, lambda: nc.gpsimd.memset(hal[:], 0.0), writes=[r_hal])
                op(POOL, lambda: nc.gpsimd.memset(cin[:, :, 0:30], 0.0), writes=r_cin)

                for t in range(NT):
                    row0 = s * S + t * T
                    key = (s, t)
                    if key not in r_xdram:
                        r_xdram[key] = _Reg()
                    xsrc = (x_in if l == 0 else out)[row0:row0 + T, :].rearrange("(ts p) d -> p ts d", p=128)
                    dma(SP, ch_x, lambda: nc.sync.dma_start(out=xt[:], in_=xsrc), reads=[r_xdram[key]], writes=r_xt)

                    norm_to_hT(0, 8, "a")
                    if DEBUG and l == 0 and s == 0 and t == 0:
                        dbg("hT", hT[:], [128, 8, T], BF16, r_hT)
                    with contextlib.ExitStack() as ls:
                        cosT = sb("cosT", [128, T], F32, ls)
                        sinT = sb("sinT", [128, T], F32, ls)
                        pti = sb("pti", [128, T], I32, ls)
                        ang = sb("ang", [128, T], F32, ls)
                        tk = sb("tk", [128, T], I32, ls)
                        tr = sb("tr", [128, T], F32, ls)
                        tm = sb("tm", [128, T], F32, ls)
                        tA = sb("tA", [128, T], F32, ls)
                        tB = sb("tB", [128, T], F32, ls)
                        uT = sb("uT", [128, 4, T], BF16, ls)
                        g1 = sb("g1", [128, T], F32, ls)
                        g2 = sb("g2", [128, T], F32, ls)
                        g3 = sb("g3", [128, T], F32, ls)
                        g4 = sb("g4", [128, T], F32, ls)
                        vvn = sb("vvn", [128, T], BF16, ls)
                        st6 = sb("st6", [128, 6], F32, ls)
                        mv = sb("mv", [128, 2], F32, ls)
                        mv2 = sb("mv2", [128, 2], F32, ls)
                        r_cos, r_sin, r_pti, r_ang, r_tk, r_tr, r_tm = (reg("cosT"), reg("sinT"), reg("pti"), reg("ang"),
                                                                         reg("tk"), reg("tr"), reg("tm"))
                        r_tA, r_tB = reg("tA"), reg("tB")
                        r_uT = regs("uT", 4)
                        r_g = regs("gtmp", 5)
                        r_vvn, r_st6, r_mv, r_mv2 = reg("vvn"), reg("st6"), reg("mv"), reg("mv2")

                        psrc = bass.AP(tensor=pos_in.tensor, offset=s * S + t * T, ap=[[0, 128], [1, T]])
                        dma(SP, ch_p, lambda: nc.sync.dma_start(out=pti[:], in_=psrc), writes=[r_pti], local=True)
                        op(DVE, lambda: nc.vector.tensor_scalar(ang[:], pti[:], cstt[:, 0:1], None, op0=ALU.mult),
                           reads=[r_pti, r_cst], writes=[r_ang])

                        def sin_table(dst, r_dst, shift, scale_ap):
                            src = ang
                            if shift != 0.0:
                                op(DVE, lambda: nc.vector.tensor_scalar(tm[:], ang[:], shift, None, op0=ALU.add),
                                   reads=[r_ang], writes=[r_tm])
                                src = tm
                            op(DVE, lambda: nc.vector.tensor_scalar(tk[:], src[:], 1.0 / TWO_PI, None, op0=ALU.mult),
                               reads=[r_ang, r_tm], writes=[r_tk])
                            op(DVE, lambda: nc.vector.scalar_tensor_tensor(out=tr[:], in0=tk[:], scalar=-TWO_PI, in1=src[:],
                                                                           op0=ALU.mult, op1=ALU.add),
                               reads=[r_tk, r_ang, r_tm], writes=[r_tr])
                            op(DVE, lambda: nc.vector.tensor_scalar(tm[:], tr[:], math.pi, TWO_PI, op0=ALU.is_gt, op1=ALU.mult),
                               reads=[r_tr], writes=[r_tm])
                            op(DVE, lambda: nc.vector.tensor_tensor(out=tr[:], in0=tr[:], in1=tm[:], op=ALU.subtract),
                               reads=[r_tm, r_tr], writes=[r_tr])
                            op(DVE, lambda: nc.vector.tensor_scalar(tm[:], tr[:], -math.pi, TWO_PI, op0=ALU.is_lt, op1=ALU.mult),
                               reads=[r_tr], writes=[r_tm])
                            op(DVE, lambda: nc.vector.tensor_tensor(out=tr[:], in0=tr[:], in1=tm[:], op=ALU.add),
                               reads=[r_tm, r_tr], writes=[r_tr])
                            op(DVE, lambda: nc.vector.tensor_scalar(tr[:], tr[:], 3.1415925, -3.1415925, op0=ALU.min, op1=ALU.max),
                               reads=[r_tr], writes=[r_tr])
                            if scale_ap is None:
                                op(ACT, lambda: nc.scalar.activation(out=dst[:], in_=tr[:], func=AF.Sin),
                                   reads=[r_tr], writes=[r_dst])
                            else:
                                op(ACT, lambda: nc.scalar.activation(out=dst[:], in_=tr[:], func=AF.Sin, scale=scale_ap),
                                   reads=[r_tr, r_cst], writes=[r_dst])

                        sin_table(sinT, r_sin, 0.0, cstt[:, 1:2])
                        sin_table(cosT, r_cos, math.pi / 2.0, None)

                        def fm_chunk(pan, rp, cc):
                            b = next_bank()
                            for kc in range(8):
                                op(PE, lambda kc=kc, b=b: nc.tensor.matmul(
                                    ps[b][:], lhsT=pan[:, kc, cc * 128:(cc + 1) * 128], rhs=hT[:, kc, :],
                                    start=(kc == 0), stop=(kc == 7)),
                                   reads=[rp, r_hT[kc]], writes=[r_ps[b]])
                            return b

                        def tm_sub(pan, rp, ts):
                            b = next_bank()
                            for kc in range(8):
                                op(PE, lambda kc=kc, b=b: nc.tensor.matmul(
                                    ps[b][:], lhsT=hT[:, kc, ts * 128:(ts + 1) * 128], rhs=pan[:, kc, :],
                                    start=(kc == 0), stop=(kc == 7)),
                                   reads=[rp, r_hT[kc]], writes=[r_ps[b]])
                            return b

                        for which in range(2):
                            panA, rpA = load_panel("w_in", l, 0, 8, (2 * which) * 512, 512)
                            panB, rpB = load_panel("w_in", l, 0, 8, (2 * which + 1) * 512, 512)
                            for hd in range(4):
                                bA = fm_chunk(panA, rpA, hd)
                                bB = fm_chunk(panB, rpB, hd)
                                colA = P_BIN + (2 * which) * 4 + hd
                                colB = P_BIN + (2 * which + 1) * 4 + hd
                                op(DVE, lambda bA=bA, colA=colA: nc.vector.scalar_tensor_tensor(
                                    out=tA[:], in0=ps[bA][:], scalar=pv[:, colA:colA + 1], in1=cosT[:],
                                    op0=ALU.add, op1=ALU.mult), reads=[r_ps[bA], r_pv, r_cos], writes=[r_tA])
                                op(DVE, lambda bB=bB, colB=colB: nc.vector.scalar_tensor_tensor(
                                    out=tB[:], in0=ps[bB][:], scalar=pv[:, colB:colB + 1], in1=sinT[:],
                                    op0=ALU.add, op1=ALU.mult), reads=[r_ps[bB], r_pv, r_sin], writes=[r_tB])
                                if which == 0:
                                    op(POOL, lambda hd=hd: nc.gpsimd.tensor_tensor(out=qT[:, hd, :], in0=tA[:], in1=tB[:], op=ALU.add),
                                       reads=[r_tA, r_tB], writes=[r_qT[hd]])
                                else:
                                    op(POOL, lambda hd=hd: nc.gpsimd.tensor_tensor(out=KT[:, hd, t * T:(t + 1) * T], in0=tA[:],
                                                                                   in1=tB[:], op=ALU.add),
                                       reads=[r_tA, r_tB], writes=[r_KT[hd]])
                        pan, rp = load_panel("w_in", l, 0, 8, 4 * 512, 512)
                        for ts in range(4):
                            b = tm_sub(pan, rp, ts)
                            op(DVE, lambda b=b, ts=ts: nc.vector.tensor_tensor(
                                out=VC[:, t * 4 + ts, :], in0=ps[b][:], in1=bvt[:, B_BV:B_BV + 512], op=ALU.add),
                               reads=[r_ps[b], r_bvt], writes=[r_VC])
                        if DEBUG and l == 0 and s == 0 and t == 0:
                            dbg("qT", qT[:], [128, 4, T], BF16, r_qT)
                            dbg("KT", KT[:, :, 0:T], [128, 4, T], BF16, r_KT)
                            dbg("VC", VC[:, 0:4, :], [128, 4, 512], BF16, [r_VC])

                        panA, rpA = load_panel("w_in", l, 0, 8, 5 * 512, 512)
                        panG, rpG = load_panel("w_in", l, 0, 8, 6 * 512, 512)
                        for c in range(4):
                            bG = fm_chunk(panG, rpG, c)
                            op(ACT, lambda bG=bG, c=c: nc.scalar.activation(out=g1[:], in_=ps[bG][:], func=AF.Sigmoid,
                                                                            bias=pv[:, P_BIN + 24 + c:P_BIN + 25 + c]),
                               reads=[r_ps[bG], r_pv], writes=[r_g[0]])
                            bA = fm_chunk(panA, rpA, c)
                            op(DVE, lambda bA=bA, c=c: nc.vector.scalar_tensor_tensor(
                                out=cin[:, c, 30:542], in0=ps[bA][:], scalar=pv[:, P_BIN + 20 + c:P_BIN + 21 + c], in1=g1[:],
                                op0=ALU.add, op1=ALU.mult), reads=[r_ps[bA], r_pv, r_g[0]], writes=[r_cin[c]])
                        def gelu_tanh(dst_ap, r_dst, xin, r_x):
                            op(POOL, lambda: nc.gpsimd.tensor_tensor(out=g2[:], in0=xin[:], in1=xin[:], op=ALU.mult),
                               reads=[r_x], writes=[r_g[1]])
                            op(DVE, lambda: nc.vector.tensor_scalar(g2[:], g2[:], 0.044715, 1.0, op0=ALU.mult, op1=ALU.add),
                               reads=[r_g[1]], writes=[r_g[1]])
                            op(POOL, lambda: nc.gpsimd.tensor_tensor(out=g2[:], in0=g2[:], in1=xin[:], op=ALU.mult),
                               reads=[r_x, r_g[1]], writes=[r_g[1]])
                            op(ACT, lambda: nc.scalar.activation(out=g3[:], in_=g2[:], func=AF.Sigmoid, scale=1.5957691216057308),
                               reads=[r_g[1]], writes=[r_g[2]])
                            op(DVE, lambda: nc.vector.tensor_tensor(out=dst_ap, in0=xin[:], in1=g3[:], op=ALU.mult),
                               reads=[r_x, r_g[2]], writes=r_dst)

                        pan, rp = load_panel("w_in", l, 0, 8, 7 * 512, 512)
                        for c in range(4):
                            b = fm_chunk(pan, rp, c)
                            op(ACT, lambda b=b, c=c: nc.scalar.activation(out=g1[:], in_=ps[b][:], func=AF.Identity,
                                                                          bias=pv[:, P_BIN + 28 + c:P_BIN + 29 + c]),
                               reads=[r_ps[b], r_pv], writes=[r_g[0]])
                            gelu_tanh(uT[:, c, :], [r_uT[c]], g1, r_g[0])
                        pan, rp = load_panel("w_in", l, 0, 8, 8 * 512, 512)
                        bsg = [next_bank() for _ in range(4)]
                        for ts in range(4):
                            b = next_bank()
                            while b in bsg:
                                b = next_bank()
                            for kc in range(8):
                                op(PE, lambda kc=kc, b=b, ts=ts: nc.tensor.matmul(
                                    ps[b][:], lhsT=hT[:, kc, ts * 128:(ts + 1) * 128], rhs=pan[:, kc, :],
                                    start=(kc == 0), stop=(kc == 7)), reads=[rp, r_hT[kc]], writes=[r_ps[b]])
                            op(DVE, lambda b=b: nc.vector.tensor_tensor(out=g1[:], in0=ps[b][:], in1=bvt[:, B_BVV:B_BVV + 512], op=ALU.add),
                               reads=[r_ps[b], r_bvt], writes=[r_g[0]])
                            gelu_tanh(g4[:], [r_g[3]], g1, r_g[0])
                            op(DVE, lambda: nc.vector.bn_stats(st6[:], g4[:]), reads=[r_g[3]], writes=[r_st6])
                            op(DVE, lambda: nc.vector.bn_aggr(mv[:], st6[:]), reads=[r_st6], writes=[r_mv])
                            rsqrt_act(mv2[:, 0:1], mv[:, 1:2], 1.0, [r_mv], [r_mv2], mv2[:, 1:2], r_mv2)
                            op(DVE, lambda: nc.vector.tensor_scalar(g4[:], g4[:], mv[:, 0:1], mv2[:, 0:1], op0=ALU.subtract, op1=ALU.mult),
                               reads=[r_mv, r_mv2], writes=[r_g[3]])
                            op(POOL, lambda: nc.gpsimd.tensor_tensor(out=g4[:], in0=g4[:], in1=bvt[:, B_GG:B_GG + 512], op=ALU.mult),
                               reads=[r_bvt], writes=[r_g[3]])
                            op(DVE, lambda: nc.vector.tensor_tensor(out=vvn[:], in0=g4[:], in1=bvt[:, B_GB:B_GB + 512], op=ALU.add),
                               reads=[r_g[3], r_bvt], writes=[r_vvn])
                            for g in range(4):
                                op(PE, lambda g=g, ts=ts: nc.tensor.matmul(
                                    ps[bsg[g]][:, ts * 128:(ts + 1) * 128], lhsT=vvn[:, g * 128:(g + 1) * 128], rhs=wsT[:, g, :],
                                    start=True, stop=True), reads=[r_vvn, r_wsT], writes=[r_ps[bsg[g]]])
                        for g in range(4):
                            bview = bvt[:, B_BSP + g * 128:B_BSP + (g + 1) * 128]
                            b3 = bass.AP(tensor=bview.tensor, offset=bview.offset, ap=[list(bview.ap[0]), [0, 4], list(bview.ap[1])])
                            op(DVE, lambda g=g, b3=b3: nc.vector.tensor_tensor(
                                out=g1[:].rearrange("p (a b) -> p a b", b=128), in0=ps[bsg[g]][:].rearrange("p (a b) -> p a b", b=128),
                                in1=b3, op=ALU.add), reads=[r_ps[bsg[g]], r_bvt], writes=[r_g[0]])
                            op(POOL, lambda g=g: nc.gpsimd.tensor_tensor(out=gmT[:, g, :], in0=g1[:], in1=uT[:, g, :], op=ALU.mult),
                               reads=[r_g[0], r_uT[g]], writes=[r_gm[g]])
                        if DEBUG and l == 0 and s == 0 and t == 0:
                            dbg("gmT", gmT[:], [128, 4, T], BF16, r_gm)
                        barrier()

                    with contextlib.ExitStack() as ls:
                        ptg = [sb(f"ptg{i}", [128, 4, T], BF16, ls) for i in range(2)]
                        r_ptg = regs("ptg", 2)
                        a1 = sb("a1", [128, T], F32, ls)
                        a2 = sb("a2", [128, T], F32, ls)
                        a3 = sb("a3", [128, T], F32, ls)
                        a4 = sb("a4", [128, T], F32, ls)
                        a5 = sb("a5", [128, T], BF16, ls)
                        a3s = sb("a3s", [128, 4, T], F32, ls)
                        r_a = regs("atmp", 5)
                        r_a3s = regs("a3s", 4)
                        cout = sb("cout", [128, 4, T], F32, ls)
                        g2 = sb("g2b", [128, T], F32, ls)
                        g3 = sb("g3b", [128, T], F32, ls)
                        g4 = sb("g4b", [128, T], F32, ls)
                        g1 = sb("g1b", [128, T], F32, ls)
                        hcb = sb("hcb", [128, T], BF16, ls)
                        sqb = sb("sqb", [128, T], BF16, ls)
                        r_cout = regs("cout", 4)
                        r_g = regs("gtmpb", 5)
                        r_hcb, r_sqb = reg("hcb"), reg("sqb")
                        nkt = 4 * (t + 1)
                        ngrp = nkt // 2
                        for hd in range(4):
                            bo1, bo2, bs1, bs2 = 4, 5, 6, 7

                            def scores(g, hd=hd):
                                for i_ in range(2):
                                    kt = 2 * g + i_
                                    for m in range(2):
                                        bk = i_ * 2 + m
                                        op(PE, lambda m=m, bk=bk, kt=kt: nc.tensor.matmul(
                                            ps[bk][:, 0:T], lhsT=KT[m * 64:(m + 1) * 64, hd, kt * 128:(kt + 1) * 128],
                                            rhs=qT[m * 64:(m + 1) * 64, hd, 0:T], start=True, stop=True),
                                           reads=[r_KT[hd], r_qT[hd]], writes=[r_ps[bk]])

                            def exps(g):
                                op(ACT, lambda g=g: nc.scalar.activation(out=ptg[g % 2][:, :, :], in_=psall[:, 0:4, :],
                                                                         func=AF.Exp, scale=0.125),
                                   reads=[r_ps[0], r_ps[1], r_ps[2], r_ps[3]], writes=[r_ptg[g % 2]])

                            def pvs(g, hd=hd):
                                for i_ in range(2):
                                    kt = 2 * g + i_
                                    j = kt - 4 * t
                                    c0 = 128 * j if j > 0 else 0
                                    c1 = c0 + 64 if j >= 0 else c0
                                    first = (kt == 0)
                                    last = (kt == nkt - 1)
                                    for m in range(2):
                                        p_ = ptg[g % 2][:, i_ * 2 + m, :]
                                        rp_ = r_ptg[g % 2]
                                        bo = bo1 if m == 0 else bo2
                                        bs_ = bs1 if m == 0 else bs2
                                        for (dst_b, lw_full, lw_half) in (
                                                (bo, VC[:, kt, hd * 128:(hd + 1) * 128], VC[0:64, kt, hd * 128:(hd + 1) * 128]),
                                                (bs_, onesb[:], onesb[0:64, :])):
                                            op(PE, lambda dst_b=dst_b, lw_full=lw_full, p_=p_, c1=c1, first=first, last=last, j=j: nc.tensor.matmul(
                                                ps[dst_b][:, c1:T], lhsT=lw_full, rhs=p_[:, c1:T],
                                                start=first, stop=(last and j < 0), skip_group_check=True),
                                               reads=[r_VC, r_const, rp_], writes=[r_ps[dst_b]])
                                            if j >= 0:
                                                op(PE, lambda dst_b=dst_b, lw_half=lw_half, p_=p_, c0=c0, last=last: nc.tensor.matmul(
                                                    ps[dst_b][:, c0:c0 + 64], lhsT=lw_half, rhs=p_[0:64, c0:c0 + 64],
                                                    start=False, stop=last, skip_group_check=True),
                                                   reads=[r_VC, r_const, rp_], writes=[r_ps[dst_b]])

                            scores(0)
                            exps(0)
                            for g in range(ngrp):
                                if g + 1 < ngrp:
                                    scores(g + 1)
                                    exps(g + 1)
                                pvs(g)
                            c = hd
                            op(DVE, lambda c=c: nc.vector.tensor_scalar(
                                cout[:, c, :], cin[:, c, 0:512], pv[:, P_CW + c:P_CW + c + 1], pv[:, P_CB + c:P_CB + c + 1],
                                op0=ALU.mult, op1=ALU.add), reads=[r_cin[c], r_pv], writes=[r_cout[c]])
                            op(DVE, lambda c=c: nc.vector.tensor_scalar(
                                g2[:], cin[:, c, 1:513], pv[:, P_CW + 4 + c:P_CW + 4 + c + 1], None, op0=ALU.mult),
                               reads=[r_cin[c], r_pv], writes=[r_g[1]])
                            for k in range(2, 31):
                                if k % 2 == 0:
                                    op(DVE, lambda c=c, k=k: nc.vector.scalar_tensor_tensor(
                                        out=cout[:, c, :], in0=cin[:, c, k:k + 512], scalar=pv[:, P_CW + k * 4 + c:P_CW + k * 4 + c + 1],
                                        in1=cout[:, c, :], op0=ALU.mult, op1=ALU.add),
                                       reads=[r_cin[c], r_pv], writes=[r_cout[c]])
                                else:
                                    op(DVE, lambda c=c, k=k: nc.vector.scalar_tensor_tensor(
                                        out=g2[:], in0=cin[:, c, k:k + 512], scalar=pv[:, P_CW + k * 4 + c:P_CW + k * 4 + c + 1],
                                        in1=g2[:], op0=ALU.mult, op1=ALU.add),
                                       reads=[r_cin[c], r_pv], writes=[r_g[1]])
                            op(DVE, lambda c=c: nc.vector.tensor_tensor(out=cout[:, c, :], in0=cout[:, c, :], in1=g2[:], op=ALU.add),
                               reads=[r_g[1]], writes=[r_cout[c]])
                            op(POOL, lambda c=c: nc.gpsimd.tensor_copy(cin[:, c, 0:30], cin[:, c, 512:542]),
                               reads=[], writes=[r_cin[c]])
                            op(ACT, lambda: nc.scalar.activation(out=a1[:], in_=ps[bs1][:], func=AF.Ln), reads=[r_ps[bs1]], writes=[r_a[0]])
                            op(ACT, lambda: nc.scalar.activation(out=a2[:], in_=ps[bs2][:], func=AF.Ln), reads=[r_ps[bs2]], writes=[r_a[1]])
                            op(ACT, lambda: nc.scalar.activation(out=a1[:], in_=a1[:], func=AF.Exp, scale=-1.0), reads=[], writes=[r_a[0]])
                            op(ACT, lambda: nc.scalar.activation(out=a2[:], in_=a2[:], func=AF.Exp, scale=-1.0), reads=[], writes=[r_a[1]])
                            op(DVE, lambda: nc.vector.tensor_tensor(out=a1[:], in0=ps[bo1][:], in1=a1[:], op=ALU.mult),
                               reads=[r_ps[bo1]], writes=[r_a[0]])
                            op(DVE, lambda: nc.vector.tensor_tensor(out=a2[:], in0=ps[bo2][:], in1=a2[:], op=ALU.mult),
                               reads=[r_ps[bo2]], writes=[r_a[1]])
                            op(DVE, lambda hd=hd: nc.vector.scalar_tensor_tensor(out=a3s[:, hd, :], in0=a2[:], scalar=nlam_t[:, 0:1], in1=a1[:],
                                                                                 op0=ALU.mult, op1=ALU.add),
                               reads=[r_a[0], r_a[1], r_drv], writes=[r_a3s[hd]])
                            if DEBUG and l == 0 and s == 0 and t == 0 and hd == 0:
                                dbg("a1", a1[:], [128, T], F32, [r_a[0]])
                                dbg("a2", a2[:], [128, T], F32, [r_a[1]])
                        for hd in range(4):
                            op(POOL, lambda hd=hd: nc.gpsimd.tensor_tensor(out=a5[:], in0=a3s[:, hd, :], in1=a3s[:, hd, :], op=ALU.mult),
                               reads=[r_a3s[hd]], writes=[r_a[4]])
                            bq = hd % 4
                            op(PE, lambda bq=bq: nc.tensor.matmul(ps[bq][:], lhsT=onesb[:], rhs=a5[:], start=True, stop=True),
                               reads=[r_a[4], r_const], writes=[r_ps[bq]])
                            rsqrt_act(a4[:], ps[bq][:], 1.0 / 128, [r_ps[bq]], [r_a[3]], a1[:], r_a[0])
                            op(DVE, lambda hd=hd: nc.vector.scalar_tensor_tensor(out=oT[:, hd, :], in0=a3s[:, hd, :], scalar=gsub_t[:, 0:1],
                                                                                 in1=a4[:], op0=ALU.mult, op1=ALU.mult),
                               reads=[r_a3s[hd], r_a[3], r_drv], writes=[r_oT[hd]])
                        bS, bQ = 4, 5
                        for c in range(4):
                            op(ACT, lambda c=c: nc.scalar.activation(out=hcb[:], in_=cout[:, c, :], func=AF.Copy),
                               reads=[r_cout[c]], writes=[r_hcb])
                            op(ACT, lambda c=c: nc.scalar.activation(out=sqb[:], in_=cout[:, c, :], func=AF.Square),
                               reads=[r_cout[c]], writes=[r_sqb])
                            op(PE, lambda c=c: nc.tensor.matmul(ps[bS][:], lhsT=onesb[:], rhs=hcb[:], start=(c == 0), stop=(c == 3)),
                               reads=[r_hcb, r_const], writes=[r_ps[bS]])
                            op(PE, lambda c=c: nc.tensor.matmul(ps[bQ][:], lhsT=onesb[:], rhs=sqb[:], start=(c == 0), stop=(c == 3)),
                               reads=[r_sqb, r_const], writes=[r_ps[bQ]])
                        op(DVE, lambda: nc.vector.tensor_scalar(g2[:], ps[bS][:], 1.0 / 512, None, op0=ALU.mult),
                           reads=[r_ps[bS]], writes=[r_g[1]])
                        op(DVE, lambda: nc.vector.tensor_tensor(out=g3[:], in0=g2[:], in1=g2[:], op=ALU.mult),
                           reads=[r_g[1]], writes=[r_g[2]])
                        op(DVE, lambda: nc.vector.scalar_tensor_tensor(out=g3[:], in0=ps[bQ][:], scalar=1.0 / 512, in1=g3[:],
                                                                       op0=ALU.mult, op1=ALU.subtract),
                           reads=[r_ps[bQ], r_g[2]], writes=[r_g[2]])
                        op(DVE, lambda: nc.vector.tensor_scalar(g3[:], g3[:], 0.0, None, op0=ALU.max),
                           reads=[r_g[2]], writes=[r_g[2]])
                        rsqrt_act(g4[:], g3[:], 1.0, [r_g[2]], [r_g[3]], g1[:], r_g[0])
                        for c in range(4):
                            op(DVE, lambda c=c: nc.vector.tensor_tensor(out=cout[:, c, :], in0=cout[:, c, :], in1=g2[:], op=ALU.subtract),
                               reads=[r_g[1]], writes=[r_cout[c]])
                            op(POOL, lambda c=c: nc.gpsimd.tensor_tensor(out=cout[:, c, :], in0=cout[:, c, :], in1=g4[:], op=ALU.mult),
                               reads=[r_g[3]], writes=[r_cout[c]])
                            op(ACT, lambda c=c: nc.scalar.activation(out=hc2[:, c, :], in_=cout[:, c, :], func=AF.Silu,
                                                                     scale=pv[:, P_CLG + c:P_CLG + c + 1],
                                                                     bias=pv[:, P_CLB + c:P_CLB + c + 1]),
                               reads=[r_cout[c], r_pv], writes=[r_hc2[c]])
                        if DEBUG and l == 0 and s == 0 and t == 0:
                            dbg("oT", oT[:], [128, 4, T], BF16, r_oT)
                            dbg("hc2", hc2[:], [128, 4, T], BF16, r_hc2)
                        barrier()

                    with contextlib.ExitStack() as ls:
                        mixf = sb("mixf", [128, 8, T], F32, ls)
                        mixb = sb("mixb", [128, 8, T], BF16, ls)
                        gt4 = [sb(f"gt4_{i}", [128, 4, T], BF16, ls) for i in range(2)]
                        tmpc = sb("tmpc", [128, T], F32, ls)
                        r_mixf = regs("mixf", 8)
                        r_mixb = regs("mixb", 8)
                        r_gt4 = regs("gt4", 2)
                        r_tmpc = reg("tmpc")
                        srcs = [(oT, r_oT, "w_att"), (hc2, r_hc2, "w_conv"), (gmT, r_gm, "w_gmlp")]
                        gi = 0
                        for bi, (srcT, r_src, wn) in enumerate(srcs):
                            wo_, rwo = load_panel(wn, l, 0, 4, 0, D)
                            for half in range(2):
                                gp, rgp = load_panel("w_in", l, 0, 8, (9 + bi * 2 + half) * 512, 512)
                                gtile = gt4[gi % 2]
                                rg = r_gt4[gi % 2]
                                gi += 1
                                for cc in range(4):
                                    b = next_bank()
                                    for kc in range(8):
                                        op(PE, lambda kc=kc, b=b, cc=cc, gp=gp: nc.tensor.matmul(
                                            ps[b][:], lhsT=gp[:, kc, cc * 128:(cc + 1) * 128], rhs=hT[:, kc, :],
                                            start=(kc == 0), stop=(kc == 7)), reads=[rgp, r_hT[kc]], writes=[r_ps[b]])
                                    col = P_BIN + (9 + bi * 2 + half) * 4 + cc
                                    op(ACT, lambda b=b, cc=cc, col=col, gtile=gtile: nc.scalar.activation(
                                        out=gtile[:, cc, :], in_=ps[b][:], func=AF.Sigmoid, bias=pv[:, col:col + 1]),
                                       reads=[r_ps[b], r_pv], writes=[rg])
                                for cc in range(4):
                                    dc = half * 4 + cc
                                    b = next_bank()
                                    for kc in range(4):
                                        op(PE, lambda kc=kc, b=b, dc=dc, wo_=wo_, srcT=srcT: nc.tensor.matmul(
                                            ps[b][:], lhsT=wo_[:, kc, dc * 128:(dc + 1) * 128], rhs=srcT[:, kc, :],
                                            start=(kc == 0), stop=(kc == 3)), reads=[rwo, r_src[kc]], writes=[r_ps[b]])
                                    if bi == 0:
                                        op(DVE, lambda b=b, dc=dc, cc=cc, gtile=gtile: nc.vector.tensor_tensor(
                                            out=mixf[:, dc, :], in0=ps[b][:], in1=gtile[:, cc, :], op=ALU.mult),
                                           reads=[r_ps[b], rg], writes=[r_mixf[dc]])
                                    else:
                                        op(DVE, lambda b=b, cc=cc, gtile=gtile: nc.vector.tensor_tensor(
                                            out=tmpc[:], in0=ps[b][:], in1=gtile[:, cc, :], op=ALU.mult),
                                           reads=[r_ps[b], rg], writes=[r_tmpc])
                                        if bi == 1:
                                            op(POOL, lambda dc=dc: nc.gpsimd.tensor_tensor(
                                                out=mixf[:, dc, :], in0=mixf[:, dc, :], in1=tmpc[:], op=ALU.add),
                                               reads=[r_tmpc], writes=[r_mixf[dc]])
                                        else:
                                            op(POOL, lambda dc=dc: nc.gpsimd.tensor_tensor(
                                                out=mixb[:, dc, :], in0=mixf[:, dc, :], in1=tmpc[:], op=ALU.add),
                                               reads=[r_tmpc, r_mixf[dc]], writes=[r_mixb[dc]])
                        if DEBUG and l == 0 and s == 0 and t == 0:
                            dbg("mixb", mixb[:], [128, 8, T], BF16, r_mixb)
                        for nh in range(2):
                            wo_, rwo = load_panel("w_o", l, 0, 8, nh * 512, 512)
                            for ts in range(4):
                                b = next_bank()
                                for kc in range(8):
                                    op(PE, lambda kc=kc, b=b, ts=ts, wo_=wo_: nc.tensor.matmul(
                                        ps[b][:], lhsT=mixb[:, kc, ts * 128:(ts + 1) * 128], rhs=wo_[:, kc, :],
                                        start=(kc == 0), stop=(kc == 7)), reads=[rwo, r_mixb[kc]], writes=[r_ps[b]])
                                op(DVE, lambda b=b, nh=nh: nc.vector.tensor_tensor(
                                    out=tmpc[:], in0=ps[b][:], in1=gbc[:, 0, nh * 512:(nh + 1) * 512], op=ALU.mult),
                                   reads=[r_ps[b], r_gbc[nh]], writes=[r_tmpc])
                                op(POOL, lambda ts=ts, nh=nh: nc.gpsimd.tensor_tensor(
                                    out=xt[:, ts, nh * 512:(nh + 1) * 512], in0=xt[:, ts, nh * 512:(nh + 1) * 512], in1=tmpc[:], op=ALU.add),
                                   reads=[r_tmpc], writes=[r_xt[ts]])
                        if DEBUG and l == 0 and s == 0 and t == 0:
                            dbg("xmid", xt[:], [128, 4, D], F32, r_xt)
                        barrier()

                    norm_to_hT(16, 24, "d")
                    with contextlib.ExitStack() as ls:
                        actT = sb("actT", [128, 22, T], BF16, ls)
                        stg4 = sb("stg4", [128, 4, 514], F32, ls)
                        stg = [stg4[:, i, :] for i in range(4)]
                        ft = [sb(f"ft{i}", [128, T], F32, ls) for i in range(4)]
                        r_act = regs("actT", 22)
                        r_stg = regs("stg", 4)
                        r_ft = regs("ft", 4)
                        pending = []
                        pair_i = [0]
                        for pi in range(11):
                            pan, rp = load_panel("w_up", l, 0, 8, pi * 512, 512)
                            for pr in range(2):
                                info = []
                                for q_ in range(2):
                                    cc = pr * 2 + q_
                                    j = pi * 4 + cc
                                    b = next_bank()
                                    for kc in range(8):
                                        op(PE, lambda kc=kc, b=b, cc=cc, pan=pan: nc.tensor.matmul(
                                            ps[b][:], lhsT=pan[:, kc, cc * 128:(cc + 1) * 128], rhs=hT[:, kc, :],
                                            start=(kc == 0), stop=(kc == 7)), reads=[rp, r_hT[kc]], writes=[r_ps[b]])
                                    bi_ = (pair_i[0] % 2) * 2 + q_
                                    sg_, rs_, f_, rf_ = stg[bi_], r_stg[bi_], ft[bi_], r_ft[bi_]
                                    if q_ == 0:
                                        op(POOL, lambda bi_=bi_, j=j: nc.gpsimd.tensor_copy(stg4[:, bi_:bi_ + 2, 0:2], hal[:, j:j + 2, :]),
                                           reads=[r_hal], writes=[r_stg[bi_], r_stg[bi_ + 1]])
                                    op(ACT, lambda sg_=sg_, b=b: nc.scalar.activation(out=sg_[:, 2:514], in_=ps[b][:], func=AF.Copy),
                                       reads=[r_ps[b]], writes=[rs_])
                                    if q_ == 1:
                                        op(POOL, lambda bi_=bi_, j=j: nc.gpsimd.tensor_copy(hal[:, j - 1:j + 1, :], stg4[:, bi_ - 1:bi_ + 1, 512:514]),
                                           reads=[r_stg[bi_ - 1], r_stg[bi_]], writes=[r_hal])
                                    info.append((j, sg_, rs_, f_, rf_))
                                for (j, sg_, rs_, f_, rf_) in info:
                                    op(DVE, lambda sg_=sg_, f_=f_, j=j: nc.vector.tensor_scalar(
                                        f_[:], sg_[:, 2:514], pv[:, P_FW + 2 * 44 + j:P_FW + 2 * 44 + j + 1], pv[:, P_FB + j:P_FB + j + 1],
                                        op0=ALU.mult, op1=ALU.add), reads=[rs_, r_pv], writes=[rf_])
                                for (j, sg_, rs_, f_, rf_) in info:
                                    op(DVE, lambda sg_=sg_, f_=f_, j=j: nc.vector.scalar_tensor_tensor(
                                        out=f_[:], in0=sg_[:, 1:513], scalar=pv[:, P_FW + 44 + j:P_FW + 44 + j + 1], in1=f_[:],
                                        op0=ALU.mult, op1=ALU.add), reads=[rs_, r_pv], writes=[rf_])
                                for (j, sg_, rs_, f_, rf_) in info:
                                    op(DVE, lambda sg_=sg_, f_=f_, j=j: nc.vector.scalar_tensor_tensor(
                                        out=f_[:], in0=sg_[:, 0:512], scalar=pv[:, P_FW + j:P_FW + j + 1], in1=f_[:],
                                        op0=ALU.mult, op1=ALU.add), reads=[rs_, r_pv], writes=[rf_])
                                pair_i[0] += 1

                                def finals(items):
                                    for (j, sg_, rs_, f_, rf_) in items:
                                        if j < 22:
                                            op(ACT, lambda f_=f_, j=j: nc.scalar.activation(out=actT[:, j, :], in_=f_[:], func=AF.Silu),
                                               reads=[rf_], writes=[r_act[j]])
                                        else:
                                            op(POOL, lambda f_=f_, j=j: nc.gpsimd.tensor_tensor(
                                                out=actT[:, j - 22, :], in0=actT[:, j - 22, :], in1=f_[:], op=ALU.mult),
                                               reads=[rf_], writes=[r_act[j - 22]])

                                if pending:
                                    finals(pending.pop())
                                pending.append(info)
                        if pending:
                            finals(pending.pop())
                        if DEBUG and l == 0 and s == 0 and t == 0:
                            dbg("actT", actT[:], [128, 22, T], BF16, r_act)
                        for pi in range(11):
                            pan, rp = load_panel("w_down", l, 2 * pi, 2, 0, D)
                            for kk in range(2):
                                kc = 2 * pi + kk
                                for ts in range(4):
                                    for nh in range(2):
                                        b = ts * 2 + nh
                                        op(PE, lambda kc=kc, kk=kk, b=b, ts=ts, nh=nh, pan=pan: nc.tensor.matmul(
                                            ps[b][:], lhsT=actT[:, kc, ts * 128:(ts + 1) * 128], rhs=pan[:, kk, nh * 512:(nh + 1) * 512],
                                            start=(kc == 0), stop=(kc == 21), skip_group_check=True),
                                           reads=[rp, r_act[kc]], writes=[r_ps[b]])
                        for ts in range(4):
                            for nh in range(2):
                                b = ts * 2 + nh
                                f_ = ft[b % 2]
                                rf_ = r_ft[b % 2]
                                op(DVE, lambda b=b, nh=nh, f_=f_: nc.vector.tensor_tensor(
                                    out=f_[:], in0=ps[b][:], in1=gbc[:, 1, nh * 512:(nh + 1) * 512], op=ALU.mult),
                                   reads=[r_ps[b], r_gbc[2 + nh]], writes=[rf_])
                                op(POOL, lambda ts=ts, nh=nh, f_=f_: nc.gpsimd.tensor_tensor(
                                    out=xt[:, ts, nh * 512:(nh + 1) * 512], in0=xt[:, ts, nh * 512:(nh + 1) * 512], in1=f_[:], op=ALU.add),
                                   reads=[rf_], writes=[r_xt[ts]])
                        barrier()

                    if l == NL - 1:
                        with contextlib.ExitStack() as ls:
                            junk = sb("fjunk", [128, D], BF16, ls)
                            ss = sb("fss", [128, 4], F32, ls)
                            rs = sb("frs", [128, 4], F32, ls)
                            lt = sb("flt", [128, 4], F32, ls)
                            r_j, r_s1, r_s2, r_s3 = reg("fjunk"), reg("fss"), reg("frs"), reg("flt")
                            for ts in range(4):
                                op(ACT, lambda ts=ts: nc.scalar.activation(out=junk[:], in_=xt[:, ts, :], func=AF.Square,
                                                                            accum_out=ss[:, ts:ts + 1]),
                                   reads=[r_xt[ts]], writes=[r_j, r_s1])
                            rsqrt_act(rs[:], ss[:], 1.0 / D, [r_s1], [r_s2], lt[:], r_s3)
                            for ts in range(4):
                                op(DVE, lambda ts=ts: nc.vector.scalar_tensor_tensor(
                                    out=xt[:, ts, :], in0=xt[:, ts, :], scalar=rs[:, ts:ts + 1], in1=fgt[:],
                                    op0=ALU.mult, op1=ALU.mult), reads=[r_s2, r_fgt], writes=[r_xt[ts]])
                            barrier()
                    xdst = out[row0:row0 + T, :].rearrange("(ts p) d -> p ts d", p=128)
                    dma(POOL, ch_xs, lambda: nc.gpsimd.dma_start(out=xdst, in_=xt[:]), reads=r_xt, writes=[r_xdram[key]])

        nc.gpsimd.wait_ge(ch_xs.sem, ch_xs.count)
        nc.sync.wait_ge(ch_xs.sem, ch_xs.count)
    return nc, dbg_outs


def _prep_shared(inp):
    f = lambda a: np.ascontiguousarray(np.asarray(a, dtype=np.float32))
    w_in = f(inp["w_in"])
    b_in = f(inp["b_in"])
    perm = np.arange(512).reshape(4, 2, 64)
    perm = np.concatenate([perm[:, :, 32:], perm[:, :, :32]], axis=2).reshape(512)
    segs = [np.arange(0, 512), perm, 512 + np.arange(512), 512 + perm, 1024 + np.arange(512),
            1536 + np.arange(1024), 2560 + np.arange(1024), 3584 + np.arange(3072)]
    cols = np.concatenate(segs)
    assert cols.shape[0] == NEXT
    w_in_e = np.ascontiguousarray(w_in[:, :, cols])
    b_in_e = b_in[:, cols]

    def fm(v):
        Ln, n = v.shape
        return v.reshape(Ln, n // 128, 128).transpose(0, 2, 1)

    pvec = np.zeros((L, 128, NV), np.float32)
    pvec[:, :, P_BIN:P_BIN + 60] = fm(b_in_e)
    pvec[:, :, P_LN1G:P_LN1G + 8] = fm(f(inp["ln1_g"]))
    pvec[:, :, P_LN2G:P_LN2G + 8] = fm(f(inp["ln2_g"]))
    cw = f(inp["conv_dw_w"])
    pvec[:, :, P_CW:P_CW + 124] = cw.reshape(L, 31, 4, 128).transpose(0, 3, 1, 2).reshape(L, 128, 124)
    pvec[:, :, P_CB:P_CB + 4] = fm(f(inp["conv_dw_b"]))
    pvec[:, :, P_CLG:P_CLG + 4] = fm(f(inp["conv_ln_g"]))
    pvec[:, :, P_CLB:P_CLB + 4] = fm(f(inp["conv_ln_b"]))
    pvec[:, :, P_SUBG:P_SUBG + 1] = fm(f(inp["attn_subln_g"]))
    fw = f(inp["ffn_dw_w"])
    pvec[:, :, P_FW:P_FW + 132] = fw.reshape(L, 3, 44, 128).transpose(0, 3, 1, 2).reshape(L, 128, 132)
    pvec[:, :, P_FB:P_FB + 44] = fm(f(inp["ffn_dw_b"]))
    b_ada = f(inp["b_ada"])
    pvec[:, :, P_BADA:P_BADA + 48] = fm(b_ada)

    bvec = np.zeros((L, NB), np.float32)
    bvec[:, B_BV:B_BV + 512] = b_in[:, 1024:1536]
    bvec[:, B_BVV:B_BVV + 512] = b_in[:, 3072:3584]
    bvec[:, B_GG:B_GG + 512] = f(inp["gmlp_ln_g"])
    bvec[:, B_GB:B_GB + 512] = f(inp["gmlp_ln_b"])
    bvec[:, B_BSP:B_BSP + 512] = f(inp["b_spatial"]).reshape(L, 512)
    bvec[:, B_BG1:B_BG1 + 1024] = b_ada[:, 2048:3072]
    bvec[:, B_BG2:B_BG2 + 1024] = b_ada[:, 5120:6144]
    bvec[:, B_LAM:B_LAM + 64] = f(inp["lambda_q1"])
    bvec[:, B_LAM + 64:B_LAM + 128] = f(inp["lambda_k1"])
    bvec[:, B_LAM + 128:B_LAM + 192] = f(inp["lambda_q2"])
    bvec[:, B_LAM + 192:B_LAM + 256] = f(inp["lambda_k2"])

    inv_freq = (1.0 / (10000.0 ** (np.arange(0, 64, 2, dtype=np.float32) / 64.0))).astype(np.float32)
    cst = np.zeros((128, 4), np.float32)
    d = np.arange(128) % 64
    cst[:, 0] = inv_freq[d % 32]
    cst[:, 1] = np.where(d < 32, -1.0, 1.0)

    shared = {
        "cst": cst, "pvec": pvec, "bvec": bvec, "fing": f(inp["final_g"]).reshape(1, D),
        "wspT": np.ascontiguousarray(f(inp["w_spatial"]).transpose(0, 3, 1, 2)),
        "w_in": w_in_e, "w_ada": f(inp["w_ada"]), "w_att": f(inp["w_attn_out"]), "w_conv": f(inp["w_conv_out"]),
        "w_gmlp": f(inp["w_gmlp_out"]), "w_o": f(inp["w_o"]), "w_up": f(inp["w_up"]), "w_down": f(inp["w_down"]),
    }
    return shared


def _in_maps(inp, shared):
    x = np.asarray(inp["x"], dtype=np.float32)
    c = np.asarray(inp["c"], dtype=np.float32)
    pos = np.asarray(inp["positions"], dtype=np.int32)
    maps = []
    for core in range(8):
        b0 = 2 * core
        m = dict(shared)
        m["x"] = np.ascontiguousarray(x[b0:b0 + 2].reshape(2 * S, D))
        m["cT"] = np.ascontiguousarray(c[b0:b0 + 2].reshape(2, 8, 128).transpose(2, 1, 0))
        m["pos"] = np.ascontiguousarray(pos[b0:b0 + 2])
        maps.append(m)
    return maps


def kernel(**inputs):
    shared = _prep_shared(inputs)
    maps = _in_maps(inputs, shared)
    nc, _ = build_nc()
    res = run_bass_kernel_spmd(nc, maps, core_ids=list(range(8)))
    outs = [np.asarray(r["out"], dtype=np.float32).reshape(2, S, D) for r in res.results]
    return np.concatenate(outs, axis=0)
```
